# Optimizing a Trainium2 kernel written in Bass

```python
import jax, jax.numpy as jnp
from jax import lax
import numpy as np

D_MODEL = 1024
BATCH = 4
SEQ = 8192
DEPTH = 1

N_MEM = 256
EPS = 1e-6
MIX_WIDTH = D_MODEL
ATTN_WIDTH = MIX_WIDTH // 2
ATTN_HEAD_DIM = 64
ATTN_Q_HEADS = ATTN_WIDTH // ATTN_HEAD_DIM
ATTN_KV_HEADS = ATTN_Q_HEADS // 4
ATTN_KV_WIDTH = ATTN_KV_HEADS * ATTN_HEAD_DIM
WINDOW = 128
BLOCK = 128
HGRN_WIDTH = MIX_WIDTH - ATTN_WIDTH
HGRN_VAL_DIM = 128
HGRN_HEADS = HGRN_WIDTH // HGRN_VAL_DIM
HGRN_KEY_DIM = 128
HGRN_FDIM = HGRN_HEADS * HGRN_KEY_DIM
CHUNK = 64
IN_SPLITS = (ATTN_WIDTH, ATTN_KV_WIDTH, ATTN_KV_WIDTH, HGRN_FDIM, HGRN_FDIM, HGRN_WIDTH, HGRN_WIDTH)
IN_PROJ_WIDTH = sum(IN_SPLITS)
CA_HEADS = 4
CA_HEAD_DIM = D_MODEL // CA_HEADS
CA_WIDTH = CA_HEADS * CA_HEAD_DIM
D_FF = 2816
CONV_WIDTH = 3

kernel_name = "hybrid_swa_sink_hgrn2_memxattn_convffn"


def rms_norm(x, w):
    xf = x.astype(jnp.float32)
    y = xf * lax.rsqrt(jnp.mean(xf * xf, axis=-1, keepdims=True) + EPS)
    return (y * w.astype(jnp.float32)).astype(x.dtype)


def sliding_window_sink_attention(q, k, v, sinks):
    B, T, Hq, D = q.shape
    Hkv = k.shape[2]
    G = Hq // Hkv
    nb = T // BLOCK
    qb = q.reshape(B, nb, BLOCK, Hkv, G, D)

    def with_prev(t):
        tb = t.reshape(B, nb, BLOCK, Hkv, D)
        prev = jnp.pad(tb, ((0, 0), (1, 0), (0, 0), (0, 0), (0, 0)))[:, :-1]
        return jnp.concatenate([prev, tb], axis=2)

    kw, vw = with_prev(k), with_prev(v)
    s = jnp.einsum('bnqhgd,bnkhd->bnhgqk', qb, kw).astype(jnp.float32) * (D ** -0.5)
    qi = jnp.arange(BLOCK)[:, None]
    kj = jnp.arange(2 * BLOCK)[None, :]
    diff = qi + BLOCK - kj
    key_pos = jnp.arange(nb)[:, None, None] * BLOCK - BLOCK + kj[None]
    allowed = (diff >= 0) & (diff < WINDOW) & (key_pos >= 0)
    s = jnp.where(allowed[None, :, None, None], s, -jnp.inf)
    sink = sinks.astype(jnp.float32).reshape(Hkv, G)[None, None, :, :, None, None]
    sink = jnp.broadcast_to(sink, s.shape[:-1] + (1,))
    p = jax.nn.softmax(jnp.concatenate([s, sink], axis=-1), axis=-1)[..., :-1]
    o = jnp.einsum('bnhgqk,bnkhd->bnqhgd', p.astype(v.dtype), vw)
    return o.reshape(B, T, Hq * D)


def hgrn2_chunkwise(q, k, v, log_f):
    B, T, H, K = q.shape
    V = v.shape[-1]
    n = T // CHUNK

    def to_chunks(t):
        return t.reshape(B, n, CHUNK, H, t.shape[-1]).transpose(1, 0, 3, 2, 4)

    causal = jnp.tril(jnp.ones((CHUNK, CHUNK), dtype=bool))

    def step(S, xs):
        qc, kc, vc, gc = xs
        bc = jnp.cumsum(gc, axis=2)
        rel = bc[:, :, :, None, :] - bc[:, :, None, :, :]
        decay = jnp.exp(jnp.where(causal[:, :, None], rel, -jnp.inf))
        A = jnp.einsum('bhtk,bhsk,bhtsk->bhts', qc, kc, decay)
        o = jnp.einsum('bhts,bhsv->bhtv', A, vc) + jnp.einsum('bhtk,bhkv->bhtv', qc * jnp.exp(bc), S)
        b_last = bc[:, :, -1:, :]
        S = S * jnp.exp(b_last[:, :, 0, :])[..., None] + jnp.einsum(
            'bhsk,bhsv->bhkv', kc * jnp.exp(b_last - bc), vc)
        return S, o

    S0 = jnp.zeros((B, H, K, V), jnp.float32)
    _, o = lax.scan(step, S0, (to_chunks(q), to_chunks(k), to_chunks(v), to_chunks(log_f)))
    return o.transpose(1, 0, 3, 2, 4).reshape(B, T, H, V)


def hgrn2_group(q_raw, f_raw, i_raw, g_raw, lb, out_norm_w):
    B, T, _ = q_raw.shape
    f32 = jnp.float32
    q = jax.nn.silu(q_raw.astype(f32)).reshape(B, T, HGRN_HEADS, HGRN_KEY_DIM) * (HGRN_KEY_DIM ** -0.5)
    fr = f_raw.astype(f32)
    lb = lb.astype(f32)
    f = lb + (1.0 - lb) * jax.nn.sigmoid(fr)
    k = (1.0 - lb) * jax.nn.sigmoid(-fr)
    log_f = jnp.log(f)
    k = k.reshape(B, T, HGRN_HEADS, HGRN_KEY_DIM)
    log_f = log_f.reshape(B, T, HGRN_HEADS, HGRN_KEY_DIM)
    v = i_raw.astype(f32).reshape(B, T, HGRN_HEADS, HGRN_VAL_DIM)
    o = hgrn2_chunkwise(q, k, v, log_f)
    o = rms_norm(o, out_norm_w).reshape(B, T, HGRN_WIDTH)
    return (o * jax.nn.silu(g_raw.astype(f32))).astype(q_raw.dtype)


def memory_cross_attention(h, mem_n, wq, wk, wv, wo):
    B, T, _ = h.shape
    M = mem_n.shape[1]
    q = (h @ wq).reshape(B, T, CA_HEADS, CA_HEAD_DIM)
    k = (mem_n @ wk).reshape(B, M, CA_HEADS, CA_HEAD_DIM)
    v = (mem_n @ wv).reshape(B, M, CA_HEADS, CA_HEAD_DIM)
    s = jnp.einsum('bthd,bmhd->bhtm', q, k).astype(jnp.float32) * (CA_HEAD_DIM ** -0.5)
    p = jax.nn.softmax(s, axis=-1).astype(v.dtype)
    o = jnp.einsum('bhtm,bmhd->bthd', p, v).reshape(B, T, CA_WIDTH)
    return o @ wo


def conv_ffn(h, w_up, conv_w, conv_b, w_down):
    u = h @ w_up
    C = u.shape[-1]
    u = lax.conv_general_dilated(
        u, conv_w.reshape(CONV_WIDTH, 1, C).astype(u.dtype), window_strides=(1,),
        padding=[(CONV_WIDTH - 1, 0)], dimension_numbers=('NWC', 'WIO', 'NWC'),
        feature_group_count=C) + conv_b
    gate, val = jnp.split(u, 2, axis=-1)
    return (jax.nn.gelu(gate, approximate=True) * val) @ w_down


def setup_inputs(seed: int = 0) -> dict:
    key = jax.random.key(seed)
    ks = jax.random.split(key, 24)
    f32 = jnp.float32

    def w(k, shape, fan_in):
        return jax.random.normal(k, shape, f32) * (fan_in ** -0.5)

    def gain(k, shape):
        return 1.0 + 0.01 * jax.random.normal(k, shape, f32)

    return {
        "x": jax.random.normal(ks[0], (BATCH, SEQ, D_MODEL), f32),
        "mem": jax.random.normal(ks[1], (BATCH, N_MEM, D_MODEL), f32),
        "mix_pre_norm": gain(ks[2], (DEPTH, D_MODEL)),
        "w_in": w(ks[3], (DEPTH, D_MODEL, IN_PROJ_WIDTH), D_MODEL),
        "attn_sinks": 0.5 * jax.random.normal(ks[4], (DEPTH, ATTN_Q_HEADS), f32),
        "hgrn_lb_logits": 0.1 * jax.random.normal(ks[5], (DEPTH + 1, HGRN_FDIM), f32),
        "hgrn_out_norm": gain(ks[6], (DEPTH, HGRN_VAL_DIM)),
        "w_out": w(ks[7], (DEPTH, MIX_WIDTH, D_MODEL), MIX_WIDTH),
        "mix_post_norm": gain(ks[8], (DEPTH, D_MODEL)),
        "ca_pre_norm": gain(ks[9], (DEPTH, D_MODEL)),
        "mem_norm": gain(ks[10], (DEPTH, D_MODEL)),
        "ca_wq": w(ks[11], (DEPTH, D_MODEL, CA_WIDTH), D_MODEL),
        "ca_wk": w(ks[12], (DEPTH, D_MODEL, CA_WIDTH), D_MODEL),
        "ca_wv": w(ks[13], (DEPTH, D_MODEL, CA_WIDTH), D_MODEL),
        "ca_wo": w(ks[14], (DEPTH, CA_WIDTH, D_MODEL), CA_WIDTH),
        "ca_post_norm": gain(ks[15], (DEPTH, D_MODEL)),
        "ffn_pre_norm": gain(ks[16], (DEPTH, D_MODEL)),
        "ffn_w_up": w(ks[17], (DEPTH, D_MODEL, 2 * D_FF), D_MODEL),
        "ffn_conv_w": w(ks[18], (DEPTH, CONV_WIDTH, 2 * D_FF), CONV_WIDTH),
        "ffn_conv_b": 0.01 * jax.random.normal(ks[19], (DEPTH, 2 * D_FF), f32),
        "ffn_w_down": w(ks[20], (DEPTH, D_FF, D_MODEL), D_FF),
        "ffn_post_norm": gain(ks[21], (DEPTH, D_MODEL)),
    }


def reference(x, mem, mix_pre_norm, w_in, attn_sinks, hgrn_lb_logits, hgrn_out_norm, w_out,
              mix_post_norm, ca_pre_norm, mem_norm, ca_wq, ca_wk, ca_wv, ca_wo, ca_post_norm,
              ffn_pre_norm, ffn_w_up, ffn_conv_w, ffn_conv_b, ffn_w_down, ffn_post_norm):
    B, T, _ = x.shape
    lower_bounds = jnp.cumsum(jax.nn.softmax(hgrn_lb_logits.astype(jnp.float32), axis=0), axis=0)
    split_points = list(np.cumsum(IN_SPLITS)[:-1])
    for l in range(DEPTH):
        h = rms_norm(x, mix_pre_norm[l])
        z = h @ w_in[l]
        q_a, k_a, v_a, q_h, f_h, i_h, g_h = jnp.split(z, split_points, axis=-1)
        attn = sliding_window_sink_attention(
            q_a.reshape(B, T, ATTN_Q_HEADS, ATTN_HEAD_DIM),
            k_a.reshape(B, T, ATTN_KV_HEADS, ATTN_HEAD_DIM),
            v_a.reshape(B, T, ATTN_KV_HEADS, ATTN_HEAD_DIM),
            attn_sinks[l])
        rec = hgrn2_group(q_h, f_h, i_h, g_h, lower_bounds[l], hgrn_out_norm[l])
        m = jnp.concatenate([attn.astype(x.dtype), rec.astype(x.dtype)], axis=-1) @ w_out[l]
        x = x + rms_norm(m, mix_post_norm[l])
        h = rms_norm(x, ca_pre_norm[l])
        mem_n = rms_norm(mem, mem_norm[l])
        c = memory_cross_attention(h, mem_n, ca_wq[l], ca_wk[l], ca_wv[l], ca_wo[l])
        x = x + rms_norm(c, ca_post_norm[l])
        h = rms_norm(x, ffn_pre_norm[l])
        y = conv_ffn(h, ffn_w_up[l], ffn_conv_w[l], ffn_conv_b[l], ffn_w_down[l])
        x = x + rms_norm(y, ffn_post_norm[l])
    return x
```

```python
import numpy as np
from contextlib import ExitStack
import concourse.bass as bass
import concourse.mybir as mybir
from concourse.bass_utils import run_bass_kernel_spmd

F32 = mybir.dt.float32
BF16 = mybir.dt.bfloat16
AF = mybir.ActivationFunctionType
ALU = mybir.AluOpType

D = 1024
NMEM = 256
DFF = 2816
NPAIR = 22
EPS = 1e-6
NEG = -30000.0
GB = 4
BLK = 128
STAGES = ('mixer', 'ca', 'ffn')
STOPAT = None


class StopBuild(Exception):
    pass


def ckpt(name):
    if STOPAT == name:
        raise StopBuild(name)
RING = 16


class Buf:
    __slots__ = ("t", "w", "r", "name")

    def __init__(self, t, name=None):
        self.t = t
        self.w = None
        self.r = {}
        self.name = name


class Prog:
    ENGS = ("pe", "act", "dve", "pool", "sp")

    def __init__(self, nc, es):
        self.nc, self.es = nc, es
        self.items = {e: [] for e in self.ENGS}
        self.clk = {e: {} for e in self.ENGS}
        self.cnt = {e: 0 for e in self.ENGS}
        self.dcnt = {}
        self.nbuf = 0
        self.finals = []

    def sb(self, shape, dt, name=None):
        self.nbuf += 1
        nm = f"{name or 'b'}_{self.nbuf}"
        t = self.es.enter_context(self.nc.sbuf_tensor(nm, list(shape), dt))
        return Buf(t, nm)

    def ps(self, shape, dt, name=None):
        self.nbuf += 1
        nm = f"{name or 'p'}_{self.nbuf}"
        t = self.es.enter_context(self.nc.psum_tensor(nm, list(shape), dt))
        return Buf(t, nm)

    def _resolve(self, eng, r, w):
        clk = self.clk[eng]
        need = {}

        def add(kind, ev):
            key, val, snap = ev
            if key == eng and eng == "pe":
                return
            if clk.get(key, 0) >= val:
                return
            if key not in need or need[key][0] < val:
                need[key] = (val, snap)

        for b in r:
            if b.w is not None:
                add("raw", b.w)
        for b in w:
            if b.w is not None:
                add("waw", b.w)
            for ev in b.r.values():
                add("war", ev)
        for key, (val, snap) in need.items():
            if clk.get(key, 0) >= val:
                continue
            self.items[eng].append(("wait", key, val))
            for k2, v2 in snap.items():
                if clk.get(k2, 0) < v2:
                    clk[k2] = v2

    def _commit(self, ev, r, w):
        key = ev[0]
        for b in r:
            old = b.r.get(key)
            if old is None or old[1] < ev[1]:
                b.r[key] = ev
        for b in w:
            b.w = ev
            b.r = {}

    def op(self, eng, fn, r=(), w=()):
        self._resolve(eng, r, w)
        self.cnt[eng] += 1
        val = self.cnt[eng]
        snap = dict(self.clk[eng])
        snap[eng] = val
        ev = (eng, val, snap)
        self.items[eng].append(("op", fn, eng, 1))
        self._commit(ev, r, w)
        return ev

    def dma(self, q, fn, semkey, r=(), w=()):
        self._resolve(q, r, w)
        self.dcnt[semkey] = self.dcnt.get(semkey, 0) + 16
        val = self.dcnt[semkey]
        snap = dict(self.clk[q])
        snap[semkey] = val
        ev = (semkey, val, snap)
        self.items[q].append(("op", fn, semkey, 16))
        self._commit(ev, r, w)
        return ev

    def emit(self):
        nc, es = self.nc, self.es
        for ev in self.finals:
            if self.clk["sp"].get(ev[0], 0) < ev[1]:
                self.items["sp"].append(("wait", ev[0], ev[1]))
                self.clk["sp"][ev[0]] = ev[1]
        keys = []
        for e in self.ENGS:
            for it in self.items[e]:
                k = it[1] if it[0] == "wait" else it[2]
                if k not in keys:
                    keys.append(k)
        sems = {k: es.enter_context(nc.semaphore(f"sem{i}")) for i, k in enumerate(keys)}
        self.nsem = len(sems)
        block = es.enter_context(nc.Block())
        items = self.items

        def runner(name):
            def f(e):
                for it in items[name]:
                    if it[0] == "wait":
                        e.wait_ge(sems[it[1]], it[2])
                    else:
                        it[1](e).then_inc(sems[it[2]], it[3])
            return f

        block.tensor(runner("pe"))
        block.scalar(runner("act"))
        block.vector(runner("dve"))
        block.gpsimd(runner("pool"))
        block.sync(runner("sp"))


def slot_table():
    names = []
    names += [f"qa{c}" for c in range(4)]
    names += [f"ke{g}" for g in range(2)]
    names += [f"ko{g}" for g in range(2)]
    names += [f"vd{g}" for g in range(2)]
    for nm in ("qh", "fh", "ih", "gh"):
        names += [f"{nm}{h}" for h in range(4)]
    names += [f"wo{c}" for c in range(8)]
    names += [f"cq{c}" for c in range(8)]
    names += [f"co{c}" for c in range(8)]
    names += [f"up{c}" for c in range(44)]
    names += [f"dn{c}_{s}" for c in range(8) for s in range(3)]
    names += [f"ck{c}" for c in range(8)]
    names += [f"cv{c}" for c in range(8)]
    return {n: i for i, n in enumerate(names)}


SLOTS = slot_table()
NSLOT = len(SLOTS)


def _tile_cols(W, cols):
    K = W.shape[0]
    kc = K // 128
    out = np.zeros((128, 1024), np.float32)
    sub = W[:, cols]
    out[:, : kc * 128] = sub.reshape(kc, 128, 128).transpose(1, 0, 2).reshape(128, kc * 128)
    return out


def build_wslots(inp):
    w_in = np.asarray(inp["w_in"][0], np.float32)
    ws = np.zeros((NSLOT, 128, 1024), np.float32)
    ar = np.arange(128)
    for c in range(4):
        ws[SLOTS[f"qa{c}"]] = _tile_cols(w_in, c * 128 + ar)
    for g in range(2):
        kd = _tile_cols(w_in, 512 + g * 64 + (ar % 64)).reshape(128, 8, 128)
        ke = kd.copy(); ke[:, :, 64:] = 0.0
        ko = kd.copy(); ko[:, :, :64] = 0.0
        ws[SLOTS[f"ke{g}"]] = ke.reshape(128, 1024)
        ws[SLOTS[f"ko{g}"]] = ko.reshape(128, 1024)
        ws[SLOTS[f"vd{g}"]] = _tile_cols(w_in, 640 + g * 64 + (ar % 64))
    for h in range(4):
        ws[SLOTS[f"qh{h}"]] = _tile_cols(w_in, 768 + h * 128 + ar)
        ws[SLOTS[f"fh{h}"]] = _tile_cols(w_in, 1280 + h * 128 + ar)
        ws[SLOTS[f"ih{h}"]] = _tile_cols(w_in, 1792 + h * 128 + ar)
        ws[SLOTS[f"gh{h}"]] = _tile_cols(w_in, 2304 + h * 128 + ar)
    w_out = np.asarray(inp["w_out"][0], np.float32)
    cq = np.asarray(inp["ca_wq"][0], np.float32)
    ck = np.asarray(inp["ca_wk"][0], np.float32)
    cv = np.asarray(inp["ca_wv"][0], np.float32)
    co = np.asarray(inp["ca_wo"][0], np.float32)
    for c in range(8):
        ws[SLOTS[f"wo{c}"]] = _tile_cols(w_out, c * 128 + ar)
        ws[SLOTS[f"cq{c}"]] = _tile_cols(cq, c * 128 + ar)
        ws[SLOTS[f"co{c}"]] = _tile_cols(co, c * 128 + ar)
        ws[SLOTS[f"ck{c}"]] = _tile_cols(ck, c * 128 + ar)
        ws[SLOTS[f"cv{c}"]] = _tile_cols(cv, c * 128 + ar)
    up = np.asarray(inp["ffn_w_up"][0], np.float32)
    for c in range(44):
        ws[SLOTS[f"up{c}"]] = _tile_cols(up, c * 128 + ar)
    dn = np.asarray(inp["ffn_w_down"][0], np.float32)
    for c in range(8):
        for s in range(3):
            k0 = s * 8 * 128
            k1 = min(DFF, k0 + 1024)
            ws[SLOTS[f"dn{c}_{s}"]] = _tile_cols(dn[k0:k1], c * 128 + ar)
    return ws


V_MIXPRE, V_MIXPOST, V_CAPRE, V_MEMN, V_CAPOST, V_FFNPRE, V_FFNPOST = 0, 8, 16, 24, 32, 40, 48
V_ONW = 56
V_LB = 57
V_CW = 65
V_CB = V_CW + 132
NVEC = V_CB + 44


def build_vecs(inp):
    v = np.zeros((128, NVEC), np.float32)

    def col8(a):
        return np.asarray(a, np.float32).reshape(8, 128).T

    v[:, V_MIXPRE:V_MIXPRE + 8] = col8(inp["mix_pre_norm"][0])
    v[:, V_MIXPOST:V_MIXPOST + 8] = col8(inp["mix_post_norm"][0])
    v[:, V_CAPRE:V_CAPRE + 8] = col8(inp["ca_pre_norm"][0])
    v[:, V_MEMN:V_MEMN + 8] = col8(inp["mem_norm"][0])
    v[:, V_CAPOST:V_CAPOST + 8] = col8(inp["ca_post_norm"][0])
    v[:, V_FFNPRE:V_FFNPRE + 8] = col8(inp["ffn_pre_norm"][0])
    v[:, V_FFNPOST:V_FFNPOST + 8] = col8(inp["ffn_post_norm"][0])
    v[:, V_ONW] = np.asarray(inp["hgrn_out_norm"][0], np.float32)
    lb = np.asarray(inp["hgrn_lb_logits"], np.float32)
    v[:, V_LB:V_LB + 4] = lb[0].reshape(4, 128).T
    v[:, V_LB + 4:V_LB + 8] = lb[1].reshape(4, 128).T
    cw = np.asarray(inp["ffn_conv_w"][0], np.float32)
    for t in range(3):
        v[:, V_CW + t * 44:V_CW + (t + 1) * 44] = cw[t].reshape(44, 128).T
    v[:, V_CB:V_CB + 44] = np.asarray(inp["ffn_conv_b"][0], np.float32).reshape(44, 128).T
    return v


def build_cst(half, sinks):
    c = np.zeros((128, 8, 512), np.float32)
    k = np.arange(128)[:, None]
    q = np.arange(128)[None, :]
    c[:, 0, 0:128] = np.eye(128, dtype=np.float32)
    mc = np.where(k <= q, 0.0, NEG).astype(np.float32)
    mp = np.where(k > q, 0.0, NEG).astype(np.float32)
    cm = (k <= q).astype(np.float32)
    c[:, 1, :] = np.tile(mc, (1, 4))
    c[:, 2, :] = np.tile(mp, (1, 4))
    c[:, 3, :] = np.tile(cm, (1, 4))
    rm = np.ones((128, 512), np.float32)
    rm[:, ::128] = 0.0
    c[:, 4, :] = rm
    if half == 0:
        c[:, 5, :] = NEG
        c[:, 0, 128] = 0.0
    else:
        c[:, 5, :] = c[:, 2, :]
        c[:, 0, 128] = 1.0
    sr = np.repeat(np.asarray(sinks, np.float32), 128)
    c[0, 6, :] = sr[0:512]
    c[0, 7, :] = sr[512:1024]
    return c


def build_program(TH, PRE):
    assert TH % 512 == 0 and PRE % 512 == 0
    NTOT = PRE + BLK + TH
    nc = bass.Bass("TRN2", target_bir_lowering=False)
    xT = nc.dram_tensor("xT", [8, 128, NTOT], F32, kind="ExternalInput").ap()
    memT = nc.dram_tensor("memT", [8, 128, NMEM], F32, kind="ExternalInput").ap()
    wsl = nc.dram_tensor("wslots", [NSLOT, 128, 1024], F32, kind="ExternalInput").ap()
    vecs_d = nc.dram_tensor("vecs", [128, NVEC], F32, kind="ExternalInput").ap()
    cst_d = nc.dram_tensor("cst", [128, 8, 512], F32, kind="ExternalInput").ap()
    outT = nc.dram_tensor("outT", [8, 128, TH], F32, kind="ExternalOutput").ap()

    es = ExitStack()
    P = Prog(nc, es)
    NMAX = GB * BLK

    stage = P.sb([128, 8, NMAX], F32, "stage")
    cstf = stage
    vecs = P.sb([128, NVEC], F32, "vecs")
    P.dma("sp", lambda e: e.dma_start(out=vecs.t[:], in_=vecs_d), "c0", w=[vecs])
    P.dma("sp", lambda e: e.dma_start(out=cstf.t[:], in_=cst_d), "c0", w=[cstf])

    ident = P.sb([128, 128], BF16, "ident")
    ones = P.sb([128, 128], BF16, "ones")
    maskc = P.sb([128, 512], BF16, "maskc")
    maskp = P.sb([128, 512], BF16, "maskp")
    maskp0 = P.sb([128, 512], BF16, "maskp0")
    cmask = P.sb([128, 512], F32, "cmask")
    rmask = P.sb([128, 512], F32, "rmask")
    esink = P.sb([1, 1024], BF16, "esink")
    hflag = P.sb([128, 1], F32, "hflag")
    lbA = P.sb([128, 4], F32, "lbA")
    lbB = P.sb([128, 4], F32, "lbB")
    lbnB = P.sb([128, 4], F32, "lbnB")
    lbt = P.sb([128, 4], F32, "lbt")

    P.op("dve", lambda e: e.tensor_copy(ident.t[:], cstf.t[:, 0, 0:128]), r=[cstf], w=[ident])
    P.op("dve", lambda e: e.memset(ones.t[:], 1.0), w=[ones])
    P.op("dve", lambda e: e.tensor_copy(maskc.t[:], cstf.t[:, 1, :]), r=[cstf], w=[maskc])
    P.op("dve", lambda e: e.tensor_copy(maskp.t[:], cstf.t[:, 2, :]), r=[cstf], w=[maskp])
    P.op("dve", lambda e: e.tensor_copy(cmask.t[:], cstf.t[:, 3, :]), r=[cstf], w=[cmask])
    P.op("dve", lambda e: e.tensor_copy(rmask.t[:], cstf.t[:, 4, :]), r=[cstf], w=[rmask])
    P.op("dve", lambda e: e.tensor_copy(maskp0.t[:], cstf.t[:, 5, :]), r=[cstf], w=[maskp0])
    P.op("dve", lambda e: e.tensor_copy(hflag.t[:], cstf.t[:, 0, 128:129]), r=[cstf], w=[hflag])
    P.op("act", lambda e: e.activation(out=esink.t[0:1, :].rearrange("p (g n) -> p g n", g=2),
                                       in_=cstf.t[0:1, 6:8, :], func=AF.Exp), r=[cstf], w=[esink])
    P.op("dve", lambda e: e.tensor_tensor(lbt.t[:], vecs.t[:, V_LB + 4:V_LB + 8], vecs.t[:, V_LB:V_LB + 4],
                                          ALU.subtract), r=[vecs], w=[lbt])
    P.op("act", lambda e: e.activation(out=lbt.t[:], in_=lbt.t[:], func=AF.Exp), r=[lbt], w=[lbt])
    P.op("act", lambda e: e.activation(out=lbt.t[:], in_=lbt.t[:], func=AF.Ln, bias=1.0, scale=1.0), r=[lbt], w=[lbt])
    P.op("act", lambda e: e.activation(out=lbt.t[:], in_=lbt.t[:], func=AF.Exp, scale=-1.0), r=[lbt], w=[lbt])
    P.op("dve", lambda e: e.tensor_scalar(lbA.t[:], lbt.t[:], 0.5, 0.5, ALU.mult, ALU.add), r=[lbt], w=[lbA])
    P.op("dve", lambda e: e.tensor_scalar(lbB.t[:], lbt.t[:], -0.5, 0.5, ALU.mult, ALU.add), r=[lbt], w=[lbB])
    P.op("dve", lambda e: e.tensor_scalar(lbnB.t[:], lbt.t[:], 0.5, -0.5, ALU.mult, ALU.add), r=[lbt], w=[lbnB])

    try:
        ckpt('none')
    except StopBuild:
        pass
    banks = [P.ps([128, 512], F32, f"bank{i}") for i in range(5)]
    lbank = [P.ps([128, 512], F32, f"lbank{i}") for i in range(2)]
    tbank = [P.ps([128, 1024], BF16, "tbank")]
    bstate = {"i": 0, "t": 0}

    def bank():
        b = banks[bstate["i"] % len(banks)]
        bstate["i"] += 1
        return b

    def tbk():
        b = tbank[0]
        bstate["t"] += 1
        return b

    ring = [P.sb([128, 1024], BF16, f"ring{i}") for i in range(RING)]
    wseq = []
    wstate = {"next": 0, "issued": 0}

    def wnext(name):
        i = wstate["next"]
        assert wseq[i] == SLOTS[name], (i, name)
        while wstate["issued"] < min(len(wseq), i + RING - 2):
            j = wstate["issued"]
            rb = ring[j % RING]
            sid = wseq[j]
            P.dma("pool", lambda e, rb=rb, sid=sid: e.dma_start(out=rb.t[:], in_=wsl[sid]),
                  ("ring", j % RING), w=[rb])
            wstate["issued"] += 1
        wstate["next"] += 1
        return ring[i % RING]

    groups = []
    n = 0
    for gi in range(PRE // NMAX):
        groups.append(("pre", n, GB, None, gi == PRE // NMAX - 1))
        n += NMAX
    groups.append(("halo", n, 1, None, False))
    n += BLK
    for gi in range(TH // NMAX):
        groups.append(("main", n, GB, gi * NMAX, False))
        n += NMAX
    assert n == NTOT

    mem_names = [f"ck{c}" for c in range(8)] + [f"cv{c}" for c in range(8)]
    full_names = ([f"qa{c}" for c in range(4)] + ["ke0", "ko0", "ke1", "ko1"]
                  + [f"fh{h}" for h in range(4)] + [f"qh{h}" for h in range(4)] + [f"gh{h}" for h in range(4)]
                  + ["vd0", "vd1", "ih0", "ih1", "ih2", "ih3"]
                  + [f"wo{c}" for c in range(8)] + [f"cq{c}" for c in range(8)] + [f"co{c}" for c in range(8)])
    for j in range(NPAIR):
        full_names += [f"up{j}", f"up{NPAIR + j}"]
    full_names += [f"dn{c}_{s}" for c in range(8) for s in range(3)]
    pre_names = [f"fh{h}" for h in range(4)] + ["ih0", "ih1", "ih2", "ih3"]
    pre_last_names = ["ke0", "ko0", "ke1", "ko1"] + [f"fh{h}" for h in range(4)] + ["vd0", "vd1", "ih0", "ih1", "ih2", "ih3"]
    wseq.extend(SLOTS[nm] for nm in mem_names)
    for (kind, n0, nblk, ooff, last) in groups:
        if kind == "pre":
            wseq.extend(SLOTS[nm] for nm in (pre_last_names if last else pre_names))
        else:
            wseq.extend(SLOTS[nm] for nm in full_names)

    Xb = [P.sb([128, 8, NMAX], F32, f"X{i}") for i in range(2)]
    hT = P.sb([128, 8, NMAX], BF16, "hT")
    sq = P.sb([128, 8, NMAX], BF16, "sq")
    rstd = P.sb([128, NMAX], F32, "rstd")
    lnt = P.sb([128, NMAX], F32, "lnt")
    tmpn = [lnt, P.sb([128, NMAX], F32, "tmpn1")]
    S = [P.sb([128, 128], F32, f"S{h}") for h in range(4)]
    for h in range(4):
        P.op("pool", lambda e, h=h: e.memset(S[h].t[:], 0.0), w=[S[h]])
    kT = [[P.sb([128, BLK + NMAX], BF16, f"kT{g}_{par}") for par in range(2)] for g in range(2)]
    Vd = [P.sb([128, GB + 1, 128], BF16, f"Vd{g}") for g in range(2)]
    for g in range(2):
        for par in range(2):
            P.op("pool", lambda e, g=g, par=par: e.memset(kT[g][par].t[:], 0.0), w=[kT[g][par]])
        P.op("pool", lambda e, g=g: e.memset(Vd[g].t[:], 0.0), w=[Vd[g]])
    uh = P.sb([128, 44, 2], F32, "uh")
    P.op("pool", lambda e: e.memset(uh.t[:], 0.0), w=[uh])
    KmT = P.sb([128, 8, NMEM], BF16, "KmT")
    Vm = P.sb([128, 2, D], BF16, "Vm")

    qaT = [P.sb([128, NMAX], BF16, f"qaT{c}") for c in range(4)]
    attnT = P.sb([128, 4, NMAX], BF16, "attnT")
    recT = [P.sb([128, NMAX], BF16, f"recT{h}") for h in range(4)]
    th = [P.sb([128, NMAX], F32, f"th{h}") for h in range(4)]
    qs = [P.sb([128, NMAX], BF16, f"qs{h}") for h in range(4)]
    gs = [P.sb([128, NMAX], BF16, f"gs{h}") for h in range(4)]
    vtok = [P.sb([128, GB, 128], BF16, f"vtok{h}") for h in range(4)]
    gg = [P.sb([128, NMAX], F32, f"gg{i}") for i in range(2)]
    bc = [P.sb([128, NMAX], F32, f"bc{i}") for i in range(2)]
    kk = [P.sb([128, NMAX], F32, f"kk{i}") for i in range(2)]
    qt = [P.sb([128, NMAX], BF16, f"qt{h}") for h in range(4)]
    kt = [P.sb([128, NMAX], BF16, f"kt{h}") for h in range(4)]
    ktok = [P.sb([128, GB, 128], BF16, f"ktok{h}") for h in range(4)]
    Am = [P.sb([128, NMAX], BF16, f"Am{h}") for h in range(4)]
    cr = [P.sb([128, GB], F32, f"cr{h}") for h in range(4)]
    dS = [P.sb([128, GB], F32, f"dS{h}") for h in range(4)]
    dK = [P.sb([128, GB], F32, f"dK{h}") for h in range(4)]
    eC = [P.sb([128, GB], F32, f"eC{h}") for h in range(4)]
    Sb = [[P.sb([128, 128], BF16, f"Sb{h}_{i}") for i in range(2)] for h in range(4)]
    Usb = th
    osq = P.sb([128, NMAX], BF16, "osq")
    r1 = gg[1]
    Pp = [P.sb([128, 512], BF16, f"Pp{i}") for i in range(2)]
    Pc = [P.sb([128, 512], BF16, f"Pc{i}") for i in range(2)]
    rden = gg
    qcT = qt + kt
    _pt = qaT + Am
    PT = [[_pt[2 * h + m] for m in range(2)] for h in range(4)]
    ocT = recT + gs
    rdc = bc
    aT = (qaT + Am + qt + kt + recT + gs)[:NPAIR]
    ug = [P.sb([128, NMAX + 2], F32, f"ug{i}") for i in range(2)]
    uv = [P.sb([128, NMAX + 2], F32, f"uv{i}") for i in range(2)]
    cg = th[0:2]
    cv = th[2:4]

    def mm(out_b, out_ap, lhs_b, lhs_ap, rhs_b, rhs_ap, start, stop):
        P.op("pe", lambda e: e.matmul(out_ap, lhs_ap, rhs_ap, start=start, stop=stop),
             r=[lhs_b, rhs_b], w=[out_b])

    def rstd_from(psb, N, dim):
        P.op("act", lambda e: e.activation(out=lnt.t[:, :N], in_=psb.t[:, :N], func=AF.Ln, bias=EPS,
                                           scale=1.0 / dim), r=[psb], w=[lnt])
        P.op("act", lambda e: e.activation(out=rstd.t[:, :N], in_=lnt.t[:, :N], func=AF.Exp, scale=-0.5),
             r=[lnt], w=[rstd])

    def prenorm(X, N, gcol):
        P.op("act", lambda e: e.activation(out=sq.t[:, :, :N], in_=X.t[:, :, :N], func=AF.Square), r=[X], w=[sq])
        pb = bank()
        for c in range(8):
            mm(pb, pb.t[:, :N], ones, ones.t[:], sq, sq.t[:, c, :N], c == 0, c == 7)
        rstd_from(pb, N, D)
        for c in range(8):
            P.op("dve", lambda e, c=c: e.scalar_tensor_tensor(hT.t[:, c, :N], X.t[:, c, :N],
                                                            vecs.t[:, gcol + c:gcol + c + 1], rstd.t[:, :N],
                                                            ALU.mult, ALU.mult), r=[X, vecs, rstd], w=[hT])

    def proj_fm(name, rhs_list, N, kchunks=8):
        wb = wnext(name)
        pb = bank()
        for k in range(kchunks):
            rb, rap = rhs_list[k]
            mm(pb, pb.t[:, :N], wb, wb.t[:, k * 128:(k + 1) * 128], rb, rap, k == 0, k == kchunks - 1)
        return pb

    def postnorm_residual(X, N, gcol, pbs):
        for c in range(8):
            pb = pbs[c]()
            P.op("act", lambda e, c=c, pb=pb: e.activation(out=stage.t[:, c, :N], in_=pb.t[:, :N], func=AF.Copy),
                 r=[pb], w=[stage])
            P.op("act", lambda e, c=c, pb=pb: e.activation(out=sq.t[:, c, :N], in_=pb.t[:, :N], func=AF.Square),
                 r=[pb], w=[sq])
        pb2 = bank()
        for c in range(8):
            mm(pb2, pb2.t[:, :N], ones, ones.t[:], sq, sq.t[:, c, :N], c == 0, c == 7)
        rstd_from(pb2, N, D)
        for c in range(8):
            tb = tmpn[c % 2]
            P.op("dve", lambda e, c=c, tb=tb: e.scalar_tensor_tensor(tb.t[:, :N], stage.t[:, c, :N],
                                                                     vecs.t[:, gcol + c:gcol + c + 1],
                                                                     rstd.t[:, :N], ALU.mult, ALU.mult),
                 r=[stage, vecs, rstd], w=[tb])
            P.op("pool", lambda e, c=c, tb=tb: e.tensor_tensor(X.t[:, c, :N], X.t[:, c, :N], tb.t[:, :N], ALU.add),
                 r=[X, tb], w=[X])

    hT_chunks = lambda N: [(hT, hT.t[:, k, :N]) for k in range(8)]

    def do_memkv():
        memX = stage
        P.dma("sp", lambda e: e.dma_start(out=memX.t[:, :, :NMEM], in_=memT.rearrange("c p n -> p c n")),
              "memx", w=[memX])
        prenorm(memX, NMEM, V_MEMN)
        for c in range(8):
            pb = proj_fm(f"ck{c}", hT_chunks(NMEM), NMEM)
            P.op("act", lambda e, c=c, pb=pb: e.activation(out=KmT.t[:, c, :], in_=pb.t[:, :NMEM], func=AF.Copy),
                 r=[pb], w=[KmT])
        for c in range(8):
            wb = wnext(f"cv{c}")
            pb = bank()
            for mh in range(2):
                for k in range(8):
                    mm(pb, pb.t[:, mh * 128:(mh + 1) * 128], hT, hT.t[:, k, mh * 128:(mh + 1) * 128],
                       wb, wb.t[:, k * 128:(k + 1) * 128], k == 0, k == 7)
            P.op("act", lambda e, c=c, pb=pb: e.activation(
                out=Vm.t[:, :, c * 128:(c + 1) * 128],
                in_=pb.t[:, 0:256].rearrange("p (m d) -> p m d", m=2), func=AF.Copy), r=[pb], w=[Vm])

    def load_x(gi):
        kind, n0, nblk, ooff, last = groups[gi]
        N = nblk * BLK
        X = Xb[gi % 2]
        P.dma("sp", lambda e: e.dma_start(out=X.t[:, :, :N], in_=xT[:, :, n0:n0 + N].rearrange("c p n -> p c n")),
              ("x", gi % 2), w=[X])

    def hgrn_prep(h, N, nblk, full):
        i2 = h % 2
        g_, bc_, kk_ = gg[i2], bc[i2], kk[i2]
        P.op("act", lambda e: e.activation(out=g_.t[:, :N], in_=th[h].t[:, :N], func=AF.Ln,
                                           bias=lbA.t[:, h:h + 1], scale=lbB.t[:, h:h + 1]),
             r=[th[h], lbA, lbB], w=[g_])
        P.op("pool", lambda e: e.tensor_scalar(kk_.t[:, :N], th[h].t[:, :N], lbnB.t[:, h:h + 1], lbB.t[:, h:h + 1],
                                               ALU.mult, ALU.add), r=[th[h], lbnB, lbB], w=[kk_])
        P.op("dve", lambda e: e.tensor_tensor_scan(bc_.t[:, :N], rmask.t[:, :N], g_.t[:, :N], 0.0, ALU.mult, ALU.add),
             r=[rmask, g_], w=[bc_])
        bc3 = bc_.t[:, :N].rearrange("p (b t) -> p b t", t=BLK)
        P.op("dve", lambda e: e.tensor_copy(cr[h].t[:, :nblk], bc3[:, :, 63]), r=[bc_], w=[cr[h]])
        P.op("act", lambda e: e.activation(out=dS[h].t[:, :nblk], in_=bc3[:, :, 127], func=AF.Exp),
             r=[bc_], w=[dS[h]])
        P.op("act", lambda e: e.activation(out=eC[h].t[:, :nblk], in_=cr[h].t[:, :nblk], func=AF.Exp),
             r=[cr[h]], w=[eC[h]])
        P.op("dve", lambda e: e.tensor_tensor(bc3, bc3, cr[h].t[:, :nblk].rearrange("p (b o) -> p b o", o=1)
                                              .to_broadcast([128, nblk, BLK]), ALU.subtract),
             r=[bc_, cr[h]], w=[bc_])
        P.op("act", lambda e: e.activation(out=dK[h].t[:, :nblk], in_=bc3[:, :, 127], func=AF.Exp),
             r=[bc_], w=[dK[h]])
        P.op("act", lambda e: e.activation(out=g_.t[:, :N], in_=bc_.t[:, :N], func=AF.Exp, scale=-1.0),
             r=[bc_], w=[g_])
        P.op("dve", lambda e: e.tensor_tensor(kt[h].t[:, :N], kk_.t[:, :N], g_.t[:, :N], ALU.mult),
             r=[kk_, g_], w=[kt[h]])
        if full:
            P.op("act", lambda e: e.activation(out=kk_.t[:, :N], in_=bc_.t[:, :N], func=AF.Exp),
                 r=[bc_], w=[kk_])
            P.op("dve", lambda e: e.scalar_tensor_tensor(qt[h].t[:, :N], qs[h].t[:, :N], float(128 ** -0.5),
                                                         kk_.t[:, :N], ALU.mult, ALU.mult),
                 r=[qs[h], kk_], w=[qt[h]])

    def hgrn_core(N, nblk, full):
        for h in range(4):
            tb = tbk()
            for b in range(nblk):
                P.op("pe", lambda e, tb=tb, h=h, b=b: e.transpose(tb.t[:, b * 128:(b + 1) * 128],
                                                                   kt[h].t[:, b * 128:(b + 1) * 128], ident.t[:]),
                     r=[kt[h], ident], w=[tb])
            P.op("act", lambda e, tb=tb, h=h: e.activation(
                out=ktok[h].t[:, :nblk, :], in_=tb.t[:, :N].rearrange("p (b f) -> p b f", f=128), func=AF.Copy),
                r=[tb], w=[ktok[h]])
        for h in range(4):
            ub = bank()
            for b in range(nblk):
                mm(ub, ub.t[:, b * 128:(b + 1) * 128], ktok[h], ktok[h].t[:, b, :], vtok[h], vtok[h].t[:, b, :],
                   True, True)
            P.op("act", lambda e, ub=ub, h=h: e.activation(out=Usb[h].t[:, :N], in_=ub.t[:, :N], func=AF.Copy),
                 r=[ub], w=[Usb[h]])
        if full:
            for h in range(4):
                ab = bank()
                for b in range(nblk):
                    mm(ab, ab.t[:, b * 128:(b + 1) * 128], kt[h], kt[h].t[:, b * 128:(b + 1) * 128],
                       qt[h], qt[h].t[:, b * 128:(b + 1) * 128], True, True)
                P.op("dve", lambda e, ab=ab, h=h: e.tensor_tensor(Am[h].t[:, :N], ab.t[:, :N], cmask.t[:, :N], ALU.mult),
                     r=[ab, cmask], w=[Am[h]])

        def s_update(h, b):
            P.op("pool", lambda e: e.tensor_scalar(S[h].t[:], S[h].t[:], dS[h].t[:, b:b + 1], None, ALU.mult),
                 r=[S[h], dS[h]], w=[S[h]])
            P.op("dve", lambda e: e.scalar_tensor_tensor(S[h].t[:], Usb[h].t[:, b * 128:(b + 1) * 128],
                                                         dK[h].t[:, b:b + 1], S[h].t[:], ALU.mult, ALU.add),
                 r=[Usb[h], dK[h], S[h]], w=[S[h]])

        if not full:
            for b in range(nblk):
                for h in range(4):
                    s_update(h, b)
            return
        for hp in range(2):
            hs = (2 * hp, 2 * hp + 1)
            ob = {h: lbank[i] for i, h in enumerate(hs)}
            for b in range(nblk):
                for h in hs:
                    sb_ = Sb[h][b % 2]
                    P.op("dve", lambda e, h=h, b=b, sb_=sb_: e.tensor_scalar(sb_.t[:], S[h].t[:], eC[h].t[:, b:b + 1],
                                                                             None, ALU.mult),
                         r=[S[h], eC[h]], w=[sb_])
                    o_b = ob[h]
                    mm(o_b, o_b.t[:, b * 128:(b + 1) * 128], vtok[h], vtok[h].t[:, b, :],
                       Am[h], Am[h].t[:, b * 128:(b + 1) * 128], True, False)
                    mm(o_b, o_b.t[:, b * 128:(b + 1) * 128], sb_, sb_.t[:], qt[h], qt[h].t[:, b * 128:(b + 1) * 128],
                       False, True)
                    s_update(h, b)
            for h in hs:
                o_b = ob[h]
                P.op("act", lambda e, o_b=o_b: e.activation(out=osq.t[:, :N], in_=o_b.t[:, :N], func=AF.Square),
                     r=[o_b], w=[osq])
                pb = bank()
                mm(pb, pb.t[:, :N], ones, ones.t[:], osq, osq.t[:, :N], True, True)
                rstd_from(pb, N, 128)
                P.op("dve", lambda e, o_b=o_b: e.tensor_tensor(r1.t[:, :N], o_b.t[:, :N], rstd.t[:, :N], ALU.mult),
                     r=[o_b, rstd], w=[r1])
                P.op("dve", lambda e, h=h: e.scalar_tensor_tensor(recT[h].t[:, :N], r1.t[:, :N],
                                                                  vecs.t[:, V_ONW:V_ONW + 1], gs[h].t[:, :N],
                                                                  ALU.mult, ALU.mult),
                     r=[r1, vecs, gs[h]], w=[recT[h]])

    def tokmajor(names, dsts, N, nblk):
        wbs = [wnext(nm) for nm in names]
        pbs = [bank() for _ in names]
        for b in range(nblk):
            for k in range(8):
                for wb, pb in zip(wbs, pbs):
                    mm(pb, pb.t[:, b * 128:(b + 1) * 128], hT, hT.t[:, k, b * 128:(b + 1) * 128],
                       wb, wb.t[:, k * 128:(k + 1) * 128], k == 0, k == 7)
        for pb, (db, dap) in zip(pbs, dsts):
            P.op("act", lambda e, pb=pb, dap=dap: e.activation(
                out=dap, in_=pb.t[:, :N].rearrange("p (b f) -> p b f", f=128), func=AF.Copy), r=[pb], w=[db])

    def roll_kv(nblk):
        for g in range(2):
            for par in range(2):
                P.op("pool", lambda e, g=g, par=par: e.tensor_copy(kT[g][par].t[:, 0:BLK],
                                                                    kT[g][par].t[:, nblk * BLK:(nblk + 1) * BLK]),
                     r=[kT[g][par]], w=[kT[g][par]])
            P.op("pool", lambda e, g=g: e.tensor_copy(Vd[g].t[:, 0, :], Vd[g].t[:, nblk, :]),
                 r=[Vd[g]], w=[Vd[g]])

    def do_prefix(gi):
        kind, n0, nblk, ooff, last = groups[gi]
        N = nblk * BLK
        X = Xb[gi % 2]
        prenorm(X, N, V_MIXPRE)
        if last:
            for g in range(2):
                for par, nm in enumerate(("ke", "ko")):
                    pb = proj_fm(f"{nm}{g}", hT_chunks(N), N)
                    P.op("act", lambda e, g=g, par=par, pb=pb: e.activation(
                        out=kT[g][par].t[:, BLK:BLK + N], in_=pb.t[:, :N], func=AF.Copy), r=[pb], w=[kT[g][par]])
        for h in range(4):
            pb = proj_fm(f"fh{h}", hT_chunks(N), N)
            P.op("act", lambda e, h=h, pb=pb: e.activation(out=th[h].t[:, :N], in_=pb.t[:, :N], func=AF.Tanh,
                                                           scale=0.5), r=[pb], w=[th[h]])
        if last:
            tokmajor(["vd0", "vd1", "ih0"], [(Vd[0], Vd[0].t[:, 1:1 + nblk, :]), (Vd[1], Vd[1].t[:, 1:1 + nblk, :]),
                                             (vtok[0], vtok[0].t[:, :nblk, :])], N, nblk)
            tokmajor(["ih1", "ih2", "ih3"], [(vtok[h], vtok[h].t[:, :nblk, :]) for h in (1, 2, 3)], N, nblk)
        else:
            tokmajor(["ih0", "ih1"], [(vtok[h], vtok[h].t[:, :nblk, :]) for h in (0, 1)], N, nblk)
            tokmajor(["ih2", "ih3"], [(vtok[h], vtok[h].t[:, :nblk, :]) for h in (2, 3)], N, nblk)
        for h in range(4):
            hgrn_prep(h, N, nblk, False)
        hgrn_core(N, nblk, False)
        if last:
            roll_kv(nblk)

    def do_full(gi):
        kind, n0, nblk, ooff, last = groups[gi]
        N = nblk * BLK
        X = Xb[gi % 2]
        first_main = (kind == "main" and ooff == 0)
        prenorm(X, N, V_MIXPRE)
        for c in range(4):
            pb = proj_fm(f"qa{c}", hT_chunks(N), N)
            P.op("dve", lambda e, c=c, pb=pb: e.tensor_copy(qaT[c].t[:, :N], pb.t[:, :N]), r=[pb], w=[qaT[c]])
        for g in range(2):
            for par, nm in enumerate(("ke", "ko")):
                pb = proj_fm(f"{nm}{g}", hT_chunks(N), N)
                P.op("dve", lambda e, g=g, par=par, pb=pb: e.tensor_copy(kT[g][par].t[:, BLK:BLK + N], pb.t[:, :N]),
                     r=[pb], w=[kT[g][par]])
        for h in range(4):
            pb = proj_fm(f"fh{h}", hT_chunks(N), N)
            P.op("act", lambda e, h=h, pb=pb: e.activation(out=th[h].t[:, :N], in_=pb.t[:, :N], func=AF.Tanh,
                                                           scale=0.5), r=[pb], w=[th[h]])
        for h in range(4):
            pb = proj_fm(f"qh{h}", hT_chunks(N), N)
            P.op("act", lambda e, h=h, pb=pb: e.activation(out=qs[h].t[:, :N], in_=pb.t[:, :N], func=AF.Silu),
                 r=[pb], w=[qs[h]])
        for h in range(4):
            pb = proj_fm(f"gh{h}", hT_chunks(N), N)
            P.op("act", lambda e, h=h, pb=pb: e.activation(out=gs[h].t[:, :N], in_=pb.t[:, :N], func=AF.Silu),
                 r=[pb], w=[gs[h]])
        tokmajor(["vd0", "vd1", "ih0"], [(Vd[0], Vd[0].t[:, 1:1 + nblk, :]), (Vd[1], Vd[1].t[:, 1:1 + nblk, :]),
                                         (vtok[0], vtok[0].t[:, :nblk, :])], N, nblk)
        tokmajor(["ih1", "ih2", "ih3"], [(vtok[h], vtok[h].t[:, :nblk, :]) for h in (1, 2, 3)], N, nblk)
        for b in range(nblk):
            for g in range(2):
                i2 = (b * 2 + g) % 2
                sp_, sc_ = bank(), bank()
                mprev = maskp0 if (first_main and b == 0) else maskp
                mm(sp_, sp_.t[:, :], ident, ident.t[:], mprev, mprev.t[:], True, False)
                mm(sc_, sc_.t[:, :], ident, ident.t[:], maskc, maskc.t[:], True, False)
                for j in range(4):
                    hq = 4 * g + j
                    c = hq // 2
                    kb = kT[g][j % 2]
                    mm(sp_, sp_.t[:, j * 128:(j + 1) * 128], kb, kb.t[:, b * 128:(b + 1) * 128],
                       qaT[c], qaT[c].t[:, b * 128:(b + 1) * 128], False, j == 3)
                    mm(sc_, sc_.t[:, j * 128:(j + 1) * 128], kb, kb.t[:, (b + 1) * 128:(b + 2) * 128],
                       qaT[c], qaT[c].t[:, b * 128:(b + 1) * 128], False, j == 3)
                pp, pc, rd = Pp[i2], Pc[i2], rden[i2]
                P.op("act", lambda e, sp_=sp_, pp=pp: e.activation(out=pp.t[:], in_=sp_.t[:], func=AF.Exp, scale=0.125),
                     r=[sp_], w=[pp])
                P.op("act", lambda e, sc_=sc_, pc=pc: e.activation(out=pc.t[:], in_=sc_.t[:], func=AF.Exp, scale=0.125),
                     r=[sc_], w=[pc])
                ob_, db_ = bank(), bank()
                mm(ob_, ob_.t[:], Vd[g], Vd[g].t[:, b, :], pp, pp.t[:], True, False)
                mm(ob_, ob_.t[:], Vd[g], Vd[g].t[:, b + 1, :], pc, pc.t[:], False, True)
                mm(db_, db_.t[:], ones, ones.t[:], pp, pp.t[:], True, False)
                mm(db_, db_.t[:], ones, ones.t[:], pc, pc.t[:], False, False)
                mm(db_, db_.t[:], ones, ones.t[0:1, :], esink, esink.t[0:1, g * 512:(g + 1) * 512], False, True)
                P.op("act", lambda e, db_=db_, rd=rd: e.activation(out=rd.t[:], in_=db_.t[:], func=AF.Ln),
                     r=[db_], w=[rd])
                P.op("act", lambda e, rd=rd: e.activation(out=rd.t[:], in_=rd.t[:], func=AF.Exp, scale=-1.0),
                     r=[rd], w=[rd])
                for half in range(2):
                    p0 = half * 64
                    P.op("dve", lambda e, half=half, p0=p0, ob_=ob_, rd=rd, g=g, b=b: e.tensor_tensor(
                        attnT.t[p0:p0 + 64, 2 * g:2 * g + 2, b * 128:(b + 1) * 128],
                        ob_.t[p0:p0 + 64, :].rearrange("p (i two q) -> p i two q", two=2, q=128)[:, :, half, :],
                        rd.t[p0:p0 + 64, :].rearrange("p (i two q) -> p i two q", two=2, q=128)[:, :, half, :],
                        ALU.mult), r=[ob_, rd], w=[attnT])
        roll_kv(nblk)
        for h in range(4):
            hgrn_prep(h, N, nblk, True)
        hgrn_core(N, nblk, True)
        mixk = [(attnT, attnT.t[:, c, :N]) for c in range(4)] + [(recT[h], recT[h].t[:, :N]) for h in range(4)]
        postnorm_residual(X, N, V_MIXPOST, [lambda c=c: proj_fm(f"wo{c}", mixk, N) for c in range(8)])

        if 'ca' not in STAGES:
            for c in range(8):
                wnext(f"cq{c}")
            for c in range(8):
                wnext(f"co{c}")
        else:
            do_ca(X, N)
        do_ffn(gi, X, N, kind, ooff, first_main)

    def do_ca(X, N):
        prenorm(X, N, V_CAPRE)
        for c in range(8):
            pb = proj_fm(f"cq{c}", hT_chunks(N), N)
            P.op("dve", lambda e, c=c, pb=pb: e.tensor_copy(qcT[c].t[:, :N], pb.t[:, :N]), r=[pb], w=[qcT[c]])
        for hh in range(4):
            for mh in range(2):
                pb = bank()
                for i, dc in enumerate((2 * hh, 2 * hh + 1)):
                    mm(pb, pb.t[:, :N], KmT, KmT.t[:, dc, mh * 128:(mh + 1) * 128], qcT[dc], qcT[dc].t[:, :N],
                       i == 0, i == 1)
                P.op("act", lambda e, hh=hh, mh=mh, pb=pb: e.activation(out=PT[hh][mh].t[:, :N], in_=pb.t[:, :N],
                                                                        func=AF.Exp, scale=1.0 / 16.0),
                     r=[pb], w=[PT[hh][mh]])
            db_ = bank()
            for mh in range(2):
                mm(db_, db_.t[:, :N], ones, ones.t[:], PT[hh][mh], PT[hh][mh].t[:, :N], mh == 0, mh == 1)
            rd = rdc[hh % 2]
            P.op("act", lambda e, db_=db_, rd=rd: e.activation(out=rd.t[:, :N], in_=db_.t[:, :N], func=AF.Ln),
                 r=[db_], w=[rd])
            P.op("act", lambda e, rd=rd: e.activation(out=rd.t[:, :N], in_=rd.t[:, :N], func=AF.Exp, scale=-1.0),
                 r=[rd], w=[rd])
            for dc in (2 * hh, 2 * hh + 1):
                pb = bank()
                for mh in range(2):
                    mm(pb, pb.t[:, :N], Vm, Vm.t[:, mh, dc * 128:(dc + 1) * 128], PT[hh][mh], PT[hh][mh].t[:, :N],
                       mh == 0, mh == 1)
                P.op("dve", lambda e, dc=dc, pb=pb, rd=rd: e.tensor_tensor(ocT[dc].t[:, :N], pb.t[:, :N], rd.t[:, :N],
                                                                           ALU.mult), r=[pb, rd], w=[ocT[dc]])
        ock = [(ocT[c], ocT[c].t[:, :N]) for c in range(8)]
        postnorm_residual(X, N, V_CAPOST, [lambda c=c: proj_fm(f"co{c}", ock, N) for c in range(8)])

    def do_ffn(gi, X, N, kind, ooff, first_main):
        if 'ffn' not in STAGES:
            for j in range(NPAIR):
                wnext(f"up{j}")
                wnext(f"up{NPAIR + j}")
            for c in range(8):
                for s_ in range(3):
                    wnext(f"dn{c}_{s_}")
            if kind == "main":
                ev = P.dma("sp", lambda e: e.dma_start(out=outT[:, :, ooff:ooff + N].rearrange("c p n -> p c n"),
                                                       in_=X.t[:, :, :N]), ("xo", gi % 2), r=[X])
                P.finals.append(ev)
            return
        prenorm(X, N, V_FFNPRE)
        for j in range(NPAIR):
            i2 = j % 2
            outs = []
            for (ci, ub, cb) in ((j, ug[i2], cg[i2]), (NPAIR + j, uv[i2], cv[i2])):
                pb = proj_fm(f"up{ci}", hT_chunks(N), N)
                P.op("act", lambda e, ci=ci, pb=pb, cb=cb: e.activation(
                    out=cb.t[:, :N], in_=pb.t[:, :N], func=AF.Identity,
                    bias=vecs.t[:, V_CB + ci:V_CB + ci + 1], scale=vecs.t[:, V_CW + 88 + ci:V_CW + 88 + ci + 1]),
                    r=[pb, vecs], w=[cb])
                P.op("act", lambda e, pb=pb, ub=ub: e.activation(out=ub.t[:, 2:2 + N], in_=pb.t[:, :N], func=AF.Copy),
                     r=[pb], w=[ub])
                if first_main:
                    P.op("pool", lambda e, ci=ci, ub=ub: e.tensor_scalar(ub.t[:, 0:2], uh.t[:, ci, :], hflag.t[:, 0:1],
                                                                         None, ALU.mult), r=[uh, hflag], w=[ub])
                else:
                    P.op("pool", lambda e, ci=ci, ub=ub: e.tensor_copy(ub.t[:, 0:2], uh.t[:, ci, :]), r=[uh], w=[ub])
                P.op("pool", lambda e, ci=ci, ub=ub: e.tensor_copy(uh.t[:, ci, :], ub.t[:, N:N + 2]), r=[ub], w=[uh])
                P.op("dve", lambda e, ci=ci, ub=ub, cb=cb: e.scalar_tensor_tensor(
                    cb.t[:, :N], ub.t[:, 0:N], vecs.t[:, V_CW + ci:V_CW + ci + 1], cb.t[:, :N], ALU.mult, ALU.add),
                    r=[ub, vecs, cb], w=[cb])
                P.op("dve", lambda e, ci=ci, ub=ub, cb=cb: e.scalar_tensor_tensor(
                    cb.t[:, :N], ub.t[:, 1:N + 1], vecs.t[:, V_CW + 44 + ci:V_CW + 44 + ci + 1], cb.t[:, :N],
                    ALU.mult, ALU.add), r=[ub, vecs, cb], w=[cb])
            if kind == "main":
                P.op("act", lambda e, i2=i2: e.activation(out=cg[i2].t[:, :N], in_=cg[i2].t[:, :N],
                                                          func=AF.Gelu_apprx_tanh), r=[cg[i2]], w=[cg[i2]])
                P.op("pool", lambda e, i2=i2, j=j: e.tensor_tensor(aT[j].t[:, :N], cg[i2].t[:, :N], cv[i2].t[:, :N],
                                                                   ALU.mult), r=[cg[i2], cv[i2]], w=[aT[j]])
        if kind == "main":
            def dn_chunk(c):
                pb = bank()
                for s in range(3):
                    wb = wnext(f"dn{c}_{s}")
                    for kq in range(8):
                        jj = s * 8 + kq
                        if jj >= NPAIR:
                            break
                        mm(pb, pb.t[:, :N], wb, wb.t[:, kq * 128:(kq + 1) * 128], aT[jj], aT[jj].t[:, :N],
                           jj == 0, jj == NPAIR - 1)
                return pb
            postnorm_residual(X, N, V_FFNPOST, [lambda c=c: dn_chunk(c) for c in range(8)])
            ev = P.dma("sp", lambda e: e.dma_start(out=outT[:, :, ooff:ooff + N].rearrange("c p n -> p c n"),
                                                   in_=X.t[:, :, :N]), ("xo", gi % 2), r=[X])
            P.finals.append(ev)
        else:
            for c in range(8):
                for s in range(3):
                    wnext(f"dn{c}_{s}")

    try:
        ckpt('consts')
        do_memkv()
        ckpt('memkv')
        load_x(0)
        for gi in range(len(groups)):
            if gi + 1 < len(groups):
                load_x(gi + 1)
            if groups[gi][0] == "pre":
                do_prefix(gi)
                ckpt('prefix')
            else:
                do_full(gi)
        assert wstate["next"] == len(wseq), (wstate, len(wseq))
    except StopBuild:
        pass
    P.emit()
    es.close()
    global _LAST
    _LAST = dict(P=P, attnT=attnT, recT=recT, X=Xb, hT=hT, qaT=qaT, kT=kT, Vd=Vd, th=th, qs=qs, gs=gs, qt=qt, kt=kt, S=S, vtok=vtok, stage=stage)
    return nc, P


def run_cores(inputs, PRE=None, trace=False):
    x = np.asarray(inputs["x"], np.float32)
    mem = np.asarray(inputs["mem"], np.float32)
    B, T, _ = x.shape
    TH = T // 2
    if PRE is None:
        PRE = TH
    ws = build_wslots(inputs)
    vecs = build_vecs(inputs)
    sinks = np.asarray(inputs["attn_sinks"][0], np.float32)
    csts = [build_cst(0, sinks), build_cst(1, sinks)]
    nc, P = build_program(TH, PRE)
    in_maps = []
    for c in range(2 * B):
        b, half = c // 2, c % 2
        start = half * TH
        lo = start - PRE - BLK
        NTOT = PRE + BLK + TH
        xs = np.zeros((NTOT, D), np.float32)
        src_lo = max(lo, 0)
        xs[src_lo - lo:] = x[b, src_lo:start + TH]
        xT = np.ascontiguousarray(xs.T).reshape(8, 128, NTOT)
        memT = np.ascontiguousarray(mem[b].T).reshape(8, 128, NMEM)
        in_maps.append({"xT": xT, "memT": memT, "wslots": ws, "vecs": vecs, "cst": csts[half]})
    res = run_bass_kernel_spmd(nc, in_maps, core_ids=list(range(2 * B)), trace=trace)
    out = np.zeros((B, T, D), np.float32)
    for c in range(2 * B):
        b, half = c // 2, c % 2
        o = np.asarray(res.results[c]["outT"]).reshape(D, TH)
        out[b, half * TH:(half + 1) * TH] = o.T
    return out, res


def kernel(**inputs):
    out, _ = run_cores(inputs)
    return out
```

```python
import numpy as np
from contextlib import ExitStack
import concourse.bass as bass
import concourse.mybir as mybir
from concourse.bass_utils import run_bass_kernel_spmd

F32 = mybir.dt.float32
BF16 = mybir.dt.bfloat16
AF = mybir.ActivationFunctionType
ALU = mybir.AluOpType

D = 1024
NMEM = 256
DFF = 2816
NPAIR = 22
EPS = 1e-6
NEG = -30000.0
GB = 4
BLK = 128
STAGES = ('mixer', 'ca', 'ffn')
STOPAT = None


class StopBuild(Exception):
    pass


def ckpt(name):
    if STOPAT == name:
        raise StopBuild(name)
RING = 12


class Buf:
    __slots__ = ("t", "w", "r", "name")

    def __init__(self, t, name=None):
        self.t = t
        self.w = None
        self.r = {}
        self.name = name


class MB:
    def __init__(self, t, n, name):
        self.t = t
        self.c = [Buf(t, f"{name}.{i}") for i in range(n)]
        self.name = name


class Prog:
    ENGS = ("pe", "act", "dve", "pool", "sp")

    def __init__(self, nc, es):
        self.nc, self.es = nc, es
        self.items = {e: [] for e in self.ENGS}
        self.clk = {e: {} for e in self.ENGS}
        self.cnt = {e: 0 for e in self.ENGS}
        self.dcnt = {}
        self.nbuf = 0
        self.finals = []

    def sb(self, shape, dt, name=None):
        self.nbuf += 1
        nm = f"{name or 'b'}_{self.nbuf}"
        t = self.es.enter_context(self.nc.sbuf_tensor(nm, list(shape), dt))
        return Buf(t, nm)

    def mb(self, shape, dt, name):
        b = self.sb(shape, dt, name)
        return MB(b.t, shape[1], b.name)

    def ps(self, shape, dt, name=None):
        self.nbuf += 1
        nm = f"{name or 'p'}_{self.nbuf}"
        t = self.es.enter_context(self.nc.psum_tensor(nm, list(shape), dt))
        return Buf(t, nm)

    def _resolve(self, eng, r, w):
        clk = self.clk[eng]
        need = {}

        def add(kind, ev):
            key, val, snap = ev
            if key == eng and eng == "pe":
                return
            if clk.get(key, 0) >= val:
                return
            if key not in need or need[key][0] < val:
                need[key] = (val, snap)

        for b in r:
            if b.w is not None:
                add("raw", b.w)
        for b in w:
            if b.w is not None:
                add("waw", b.w)
            for ev in b.r.values():
                add("war", ev)
        for key, (val, snap) in need.items():
            if clk.get(key, 0) >= val:
                continue
            self.items[eng].append(("wait", key, val))
            for k2, v2 in snap.items():
                if clk.get(k2, 0) < v2:
                    clk[k2] = v2

    def _commit(self, ev, r, w):
        key = ev[0]
        for b in r:
            old = b.r.get(key)
            if old is None or old[1] < ev[1]:
                b.r[key] = ev
        for b in w:
            b.w = ev
            b.r = {}

    def op(self, eng, fn, r=(), w=()):
        self._resolve(eng, r, w)
        self.cnt[eng] += 1
        val = self.cnt[eng]
        snap = dict(self.clk[eng])
        snap[eng] = val
        ev = (eng, val, snap)
        self.items[eng].append(("op", fn, eng, 1))
        self._commit(ev, r, w)
        return ev

    def dma(self, q, fn, semkey, r=(), w=()):
        self._resolve(q, r, w)
        self.dcnt[semkey] = self.dcnt.get(semkey, 0) + 16
        val = self.dcnt[semkey]
        snap = dict(self.clk[q])
        snap[semkey] = val
        ev = (semkey, val, snap)
        self.items[q].append(("op", fn, semkey, 16))
        self._commit(ev, r, w)
        return ev

    def emit(self):
        nc, es = self.nc, self.es
        for ev in self.finals:
            if self.clk["sp"].get(ev[0], 0) < ev[1]:
                self.items["sp"].append(("wait", ev[0], ev[1]))
                self.clk["sp"][ev[0]] = ev[1]
        keys = []
        for e in self.ENGS:
            for it in self.items[e]:
                k = it[1] if it[0] == "wait" else it[2]
                if k not in keys:
                    keys.append(k)
        sems = {k: es.enter_context(nc.semaphore(f"sem{i}")) for i, k in enumerate(keys)}
        self.nsem = len(sems)
        block = es.enter_context(nc.Block())
        items = self.items

        def runner(name):
            def f(e):
                for it in items[name]:
                    if it[0] == "wait":
                        e.wait_ge(sems[it[1]], it[2])
                    else:
                        it[1](e).then_inc(sems[it[2]], it[3])
            return f

        block.tensor(runner("pe"))
        block.scalar(runner("act"))
        block.vector(runner("dve"))
        block.gpsimd(runner("pool"))
        block.sync(runner("sp"))


def slot_table():
    names = []
    names += [f"qa{c}" for c in range(4)]
    names += [f"ke{g}" for g in range(2)]
    names += [f"ko{g}" for g in range(2)]
    names += [f"vd{g}" for g in range(2)]
    for nm in ("qh", "fh", "ih", "gh"):
        names += [f"{nm}{h}" for h in range(4)]
    names += [f"wo{c}" for c in range(8)]
    names += [f"cq{c}" for c in range(8)]
    names += [f"co{c}" for c in range(8)]
    names += [f"up{c}" for c in range(44)]
    names += [f"dn{c}_{s}" for c in range(8) for s in range(3)]
    names += [f"ck{c}" for c in range(8)]
    names += [f"cv{c}" for c in range(8)]
    return {n: i for i, n in enumerate(names)}


SLOTS = slot_table()
NSLOT = len(SLOTS)


def _tile_cols(W, cols):
    K = W.shape[0]
    kc = K // 128
    out = np.zeros((128, 1024), np.float32)
    sub = W[:, cols]
    out[:, : kc * 128] = sub.reshape(kc, 128, 128).transpose(1, 0, 2).reshape(128, kc * 128)
    return out


def build_wslots(inp):
    w_in = np.asarray(inp["w_in"][0], np.float32)
    ws = np.zeros((NSLOT, 128, 1024), np.float32)
    ar = np.arange(128)
    for c in range(4):
        ws[SLOTS[f"qa{c}"]] = _tile_cols(w_in, c * 128 + ar)
    for g in range(2):
        kd = _tile_cols(w_in, 512 + g * 64 + (ar % 64)).reshape(128, 8, 128)
        ke = kd.copy(); ke[:, :, 64:] = 0.0
        ko = kd.copy(); ko[:, :, :64] = 0.0
        ws[SLOTS[f"ke{g}"]] = ke.reshape(128, 1024)
        ws[SLOTS[f"ko{g}"]] = ko.reshape(128, 1024)
        ws[SLOTS[f"vd{g}"]] = _tile_cols(w_in, 640 + g * 64 + (ar % 64))
    for h in range(4):
        ws[SLOTS[f"qh{h}"]] = _tile_cols(w_in, 768 + h * 128 + ar)
        ws[SLOTS[f"fh{h}"]] = _tile_cols(w_in, 1280 + h * 128 + ar)
        ws[SLOTS[f"ih{h}"]] = _tile_cols(w_in, 1792 + h * 128 + ar)
        ws[SLOTS[f"gh{h}"]] = _tile_cols(w_in, 2304 + h * 128 + ar)
    w_out = np.asarray(inp["w_out"][0], np.float32)
    cq = np.asarray(inp["ca_wq"][0], np.float32)
    ck = np.asarray(inp["ca_wk"][0], np.float32)
    cv = np.asarray(inp["ca_wv"][0], np.float32)
    co = np.asarray(inp["ca_wo"][0], np.float32)
    for c in range(8):
        ws[SLOTS[f"wo{c}"]] = _tile_cols(w_out, c * 128 + ar)
        ws[SLOTS[f"cq{c}"]] = _tile_cols(cq, c * 128 + ar)
        ws[SLOTS[f"co{c}"]] = _tile_cols(co, c * 128 + ar)
        ws[SLOTS[f"ck{c}"]] = _tile_cols(ck, c * 128 + ar)
        ws[SLOTS[f"cv{c}"]] = _tile_cols(cv, c * 128 + ar)
    up = np.asarray(inp["ffn_w_up"][0], np.float32)
    for c in range(44):
        ws[SLOTS[f"up{c}"]] = _tile_cols(up, c * 128 + ar)
    dn = np.asarray(inp["ffn_w_down"][0], np.float32)
    for c in range(8):
        for s in range(3):
            k0 = s * 8 * 128
            k1 = min(DFF, k0 + 1024)
            ws[SLOTS[f"dn{c}_{s}"]] = _tile_cols(dn[k0:k1], c * 128 + ar)
    return ws


V_MIXPRE, V_MIXPOST, V_CAPRE, V_MEMN, V_CAPOST, V_FFNPRE, V_FFNPOST = 0, 8, 16, 24, 32, 40, 48
V_ONW = 56
V_LB = 57
V_CW = 65
V_CB = V_CW + 132
NVEC = V_CB + 44


def build_vecs(inp):
    v = np.zeros((128, NVEC), np.float32)

    def col8(a):
        return np.asarray(a, np.float32).reshape(8, 128).T

    v[:, V_MIXPRE:V_MIXPRE + 8] = col8(inp["mix_pre_norm"][0])
    v[:, V_MIXPOST:V_MIXPOST + 8] = col8(inp["mix_post_norm"][0])
    v[:, V_CAPRE:V_CAPRE + 8] = col8(inp["ca_pre_norm"][0])
    v[:, V_MEMN:V_MEMN + 8] = col8(inp["mem_norm"][0])
    v[:, V_CAPOST:V_CAPOST + 8] = col8(inp["ca_post_norm"][0])
    v[:, V_FFNPRE:V_FFNPRE + 8] = col8(inp["ffn_pre_norm"][0])
    v[:, V_FFNPOST:V_FFNPOST + 8] = col8(inp["ffn_post_norm"][0])
    v[:, V_ONW] = np.asarray(inp["hgrn_out_norm"][0], np.float32)
    lb = np.asarray(inp["hgrn_lb_logits"], np.float32)
    v[:, V_LB:V_LB + 4] = lb[0].reshape(4, 128).T
    v[:, V_LB + 4:V_LB + 8] = lb[1].reshape(4, 128).T
    cw = np.asarray(inp["ffn_conv_w"][0], np.float32)
    for t in range(3):
        v[:, V_CW + t * 44:V_CW + (t + 1) * 44] = cw[t].reshape(44, 128).T
    v[:, V_CB:V_CB + 44] = np.asarray(inp["ffn_conv_b"][0], np.float32).reshape(44, 128).T
    return v


def build_cst(half, sinks):
    c = np.zeros((128, 8, 512), np.float32)
    k = np.arange(128)[:, None]
    q = np.arange(128)[None, :]
    c[:, 0, 0:128] = np.eye(128, dtype=np.float32)
    mc = np.where(k <= q, 0.0, NEG).astype(np.float32)
    mp = np.where(k > q, 0.0, NEG).astype(np.float32)
    cm = (k <= q).astype(np.float32)
    c[:, 1, :] = np.tile(mc, (1, 4))
    c[:, 2, :] = np.tile(mp, (1, 4))
    c[:, 3, :] = np.tile(cm, (1, 4))
    rm = np.ones((128, 512), np.float32)
    rm[:, ::128] = 0.0
    c[:, 4, :] = rm
    if half == 0:
        c[:, 5, :] = NEG
        c[:, 0, 128] = 0.0
    else:
        c[:, 5, :] = c[:, 2, :]
        c[:, 0, 128] = 1.0
    sr = np.repeat(np.asarray(sinks, np.float32), 128)
    c[0, 6, :] = sr[0:512]
    c[0, 7, :] = sr[512:1024]
    return c


def build_program(TH, PRE):
    assert TH % 512 == 0 and PRE % 512 == 0
    NTOT = PRE + BLK + TH
    nc = bass.Bass("TRN2", target_bir_lowering=False)
    xT = nc.dram_tensor("xT", [8, 128, NTOT], F32, kind="ExternalInput").ap()
    memT = nc.dram_tensor("memT", [8, 128, NMEM], F32, kind="ExternalInput").ap()
    wsl = nc.dram_tensor("wslots", [NSLOT, 128, 1024], F32, kind="ExternalInput").ap()
    vecs_d = nc.dram_tensor("vecs", [128, NVEC], F32, kind="ExternalInput").ap()
    cst_d = nc.dram_tensor("cst", [128, 8, 512], F32, kind="ExternalInput").ap()
    outT = nc.dram_tensor("outT", [8, 128, TH], F32, kind="ExternalOutput").ap()

    es = ExitStack()
    P = Prog(nc, es)
    NMAX = GB * BLK

    stage = P.mb([128, 8, NMAX], F32, "stage")
    cstf = stage
    vecs = P.sb([128, NVEC], F32, "vecs")
    P.dma("sp", lambda e: e.dma_start(out=vecs.t[:], in_=vecs_d), "c0", w=[vecs])
    P.dma("sp", lambda e: e.dma_start(out=cstf.t[:], in_=cst_d), "c0", w=cstf.c)

    ident = P.sb([128, 128], BF16, "ident")
    ones = P.sb([128, 128], BF16, "ones")
    maskc = P.sb([128, 512], BF16, "maskc")
    maskp = P.sb([128, 512], BF16, "maskp")
    maskp0 = P.sb([128, 512], BF16, "maskp0")
    cmask = P.sb([128, 512], F32, "cmask")
    rmask = P.sb([128, 512], F32, "rmask")
    esink = P.sb([1, 1024], BF16, "esink")
    hflag = P.sb([128, 1], F32, "hflag")
    lbA = P.sb([128, 4], F32, "lbA")
    lbB = P.sb([128, 4], F32, "lbB")
    lbnB = P.sb([128, 4], F32, "lbnB")
    lbt = P.sb([128, 4], F32, "lbt")

    P.op("dve", lambda e: e.tensor_copy(ident.t[:], cstf.t[:, 0, 0:128]), r=cstf.c, w=[ident])
    P.op("dve", lambda e: e.memset(ones.t[:], 1.0), w=[ones])
    P.op("dve", lambda e: e.tensor_copy(maskc.t[:], cstf.t[:, 1, :]), r=cstf.c, w=[maskc])
    P.op("dve", lambda e: e.tensor_copy(maskp.t[:], cstf.t[:, 2, :]), r=cstf.c, w=[maskp])
    P.op("dve", lambda e: e.tensor_copy(cmask.t[:], cstf.t[:, 3, :]), r=cstf.c, w=[cmask])
    P.op("dve", lambda e: e.tensor_copy(rmask.t[:], cstf.t[:, 4, :]), r=cstf.c, w=[rmask])
    P.op("dve", lambda e: e.tensor_copy(maskp0.t[:], cstf.t[:, 5, :]), r=cstf.c, w=[maskp0])
    P.op("dve", lambda e: e.tensor_copy(hflag.t[:], cstf.t[:, 0, 128:129]), r=cstf.c, w=[hflag])
    P.op("act", lambda e: e.activation(out=esink.t[0:1, :].rearrange("p (g n) -> p g n", g=2),
                                       in_=cstf.t[0:1, 6:8, :], func=AF.Exp), r=cstf.c, w=[esink])
    P.op("dve", lambda e: e.tensor_tensor(lbt.t[:], vecs.t[:, V_LB + 4:V_LB + 8], vecs.t[:, V_LB:V_LB + 4],
                                          ALU.subtract), r=[vecs], w=[lbt])
    P.op("act", lambda e: e.activation(out=lbt.t[:], in_=lbt.t[:], func=AF.Exp), r=[lbt], w=[lbt])
    P.op("act", lambda e: e.activation(out=lbt.t[:], in_=lbt.t[:], func=AF.Ln, bias=1.0, scale=1.0), r=[lbt], w=[lbt])
    P.op("act", lambda e: e.activation(out=lbt.t[:], in_=lbt.t[:], func=AF.Exp, scale=-1.0), r=[lbt], w=[lbt])
    P.op("dve", lambda e: e.tensor_scalar(lbA.t[:], lbt.t[:], 0.5, 0.5, ALU.mult, ALU.add), r=[lbt], w=[lbA])
    P.op("dve", lambda e: e.tensor_scalar(lbB.t[:], lbt.t[:], -0.5, 0.5, ALU.mult, ALU.add), r=[lbt], w=[lbB])
    P.op("dve", lambda e: e.tensor_scalar(lbnB.t[:], lbt.t[:], 0.5, -0.5, ALU.mult, ALU.add), r=[lbt], w=[lbnB])

    try:
        ckpt('none')
    except StopBuild:
        pass
    banks = [P.ps([128, 512], F32, f"bank{i}") for i in range(5)]
    lbank = [P.ps([128, 512], F32, f"lbank{i}") for i in range(2)]
    tbank = [P.ps([128, 1024], BF16, "tbank")]
    bstate = {"i": 0, "t": 0}

    def bank():
        b = banks[bstate["i"] % len(banks)]
        bstate["i"] += 1
        return b

    def tbk():
        b = tbank[0]
        bstate["t"] += 1
        return b

    ring = [P.sb([128, 1024], BF16, f"ring{i}") for i in range(RING)]
    wseq = []
    wstate = {"next": 0, "issued": 0}

    def wnext(name):
        i = wstate["next"]
        assert wseq[i] == SLOTS[name], (i, name)
        while wstate["issued"] < min(len(wseq), i + RING - 2):
            j = wstate["issued"]
            rb = ring[j % RING]
            sid = wseq[j]
            P.dma("pool", lambda e, rb=rb, sid=sid: e.dma_start(out=rb.t[:], in_=wsl[sid]),
                  ("ring", j % RING), w=[rb])
            wstate["issued"] += 1
        wstate["next"] += 1
        return ring[i % RING]

    groups = []
    n = 0
    for gi in range(PRE // NMAX):
        groups.append(("pre", n, GB, None, gi == PRE // NMAX - 1))
        n += NMAX
    groups.append(("halo", n, 1, None, False))
    n += BLK
    for gi in range(TH // NMAX):
        groups.append(("main", n, GB, gi * NMAX, False))
        n += NMAX
    assert n == NTOT

    mem_names = [f"ck{c}" for c in range(8)] + [f"cv{c}" for c in range(8)]
    full_names = ([f"qa{c}" for c in range(4)] + ["ke0", "ko0", "ke1", "ko1"]
                  + [f"fh{h}" for h in range(4)] + [f"qh{h}" for h in range(4)] + [f"gh{h}" for h in range(4)]
                  + ["vd0", "vd1", "ih0", "ih1", "ih2", "ih3"]
                  + [f"wo{c}" for c in range(8)] + [f"cq{c}" for c in range(8)] + [f"co{c}" for c in range(8)])
    for j in range(NPAIR):
        full_names += [f"up{j}", f"up{NPAIR + j}"]
    full_names += [f"dn{c}_{s}" for c in range(8) for s in range(3)]
    pre_names = [f"fh{h}" for h in range(4)] + ["ih0", "ih1", "ih2", "ih3"]
    pre_last_names = ["ke0", "ko0", "ke1", "ko1"] + [f"fh{h}" for h in range(4)] + ["vd0", "vd1", "ih0", "ih1", "ih2", "ih3"]
    wseq.extend(SLOTS[nm] for nm in mem_names)
    for (kind, n0, nblk, ooff, last) in groups:
        if kind == "pre":
            wseq.extend(SLOTS[nm] for nm in (pre_last_names if last else pre_names))
        else:
            wseq.extend(SLOTS[nm] for nm in full_names)

    Xb = [P.mb([128, 8, NMAX], F32, f"X{i}") for i in range(2)]
    hT = P.mb([128, 8, NMAX], BF16, "hT")
    sq = P.mb([128, 8, NMAX], BF16, "sq")
    rstd = P.sb([128, NMAX], F32, "rstd")
    lnt = P.sb([128, NMAX], F32, "lnt")
    tmpn = [lnt, P.sb([128, NMAX], F32, "tmpn1")]
    S = [P.sb([128, 128], F32, f"S{h}") for h in range(4)]
    for h in range(4):
        P.op("pool", lambda e, h=h: e.memset(S[h].t[:], 0.0), w=[S[h]])
    kT = [[P.sb([128, BLK + NMAX], BF16, f"kT{g}_{par}") for par in range(2)] for g in range(2)]
    Vd = [P.sb([128, GB + 1, 128], BF16, f"Vd{g}") for g in range(2)]
    for g in range(2):
        for par in range(2):
            P.op("pool", lambda e, g=g, par=par: e.memset(kT[g][par].t[:], 0.0), w=[kT[g][par]])
        P.op("pool", lambda e, g=g: e.memset(Vd[g].t[:], 0.0), w=[Vd[g]])
    uh = P.sb([128, 44, 2], F32, "uh")
    P.op("pool", lambda e: e.memset(uh.t[:], 0.0), w=[uh])
    KmT = P.sb([128, 8, NMEM], BF16, "KmT")
    Vm = P.sb([128, 2, D], BF16, "Vm")

    qaT = [P.sb([128, NMAX], BF16, f"qaT{c}") for c in range(4)]
    attnT = P.mb([128, 4, NMAX], BF16, "attnT")
    recT = [P.sb([128, NMAX], BF16, f"recT{h}") for h in range(4)]
    th = [P.sb([128, NMAX], F32, f"th{h}") for h in range(4)]
    qs = [P.sb([128, NMAX], BF16, f"qs{h}") for h in range(4)]
    gs = [P.sb([128, NMAX], BF16, f"gs{h}") for h in range(4)]
    vtok = [P.sb([128, GB, 128], BF16, f"vtok{h}") for h in range(4)]
    gg = [P.sb([128, NMAX], F32, f"gg{i}") for i in range(2)]
    bc = [P.sb([128, NMAX], F32, f"bc{i}") for i in range(2)]
    kk = [P.sb([128, NMAX], F32, f"kk{i}") for i in range(2)]
    qt = [P.sb([128, NMAX], BF16, f"qt{h}") for h in range(4)]
    kt = [P.sb([128, NMAX], BF16, f"kt{h}") for h in range(4)]
    ktok = [P.sb([128, GB, 128], BF16, f"ktok{h}") for h in range(4)]
    Am = [P.sb([128, NMAX], BF16, f"Am{h}") for h in range(4)]
    cr = [P.sb([128, GB], F32, f"cr{h}") for h in range(4)]
    dS = [P.sb([128, GB], F32, f"dS{h}") for h in range(4)]
    dK = [P.sb([128, GB], F32, f"dK{h}") for h in range(4)]
    eC = [P.sb([128, GB], F32, f"eC{h}") for h in range(4)]
    Sb = [[P.sb([128, 128], BF16, f"Sb{h}_{i}") for i in range(2)] for h in range(4)]
    Usb = th
    osq = P.sb([128, NMAX], BF16, "osq")
    r1 = gg[1]
    Pp = [P.sb([128, 512], BF16, f"Pp{i}") for i in range(2)]
    Pc = [P.sb([128, 512], BF16, f"Pc{i}") for i in range(2)]
    rden = [P.sb([128, 512], F32, f"rden{i}") for i in range(2)]
    qcT = qt + kt
    _pt = qaT + Am
    PT = [[_pt[2 * h + m] for m in range(2)] for h in range(4)]
    ocT = recT + gs
    rdc = bc
    aT = (qaT + Am + qt + kt + recT + gs)[:NPAIR]
    ug = [P.sb([128, NMAX + 2], F32, f"ug{i}") for i in range(3)]
    uv = [P.sb([128, NMAX + 2], F32, f"uv{i}") for i in range(3)]
    cg = [th[0], th[1], gg[0]]
    cv = [th[2], th[3], gg[1]]

    def mm(out_b, out_ap, lhs_b, lhs_ap, rhs_b, rhs_ap, start, stop):
        P.op("pe", lambda e: e.matmul(out_ap, lhs_ap, rhs_ap, start=start, stop=stop),
             r=[lhs_b, rhs_b], w=[out_b])

    def rstd_from(psb, N, dim):
        P.op("act", lambda e: e.activation(out=lnt.t[:, :N], in_=psb.t[:, :N], func=AF.Ln, bias=EPS,
                                           scale=1.0 / dim), r=[psb], w=[lnt])
        P.op("act", lambda e: e.activation(out=rstd.t[:, :N], in_=lnt.t[:, :N], func=AF.Exp, scale=-0.5),
             r=[lnt], w=[rstd])

    def prenorm(X, N, gcol):
        P.op("act", lambda e: e.activation(out=sq.t[:, :, :N], in_=X.t[:, :, :N], func=AF.Square), r=X.c, w=sq.c)
        pb = bank()
        for c in range(8):
            mm(pb, pb.t[:, :N], ones, ones.t[:], sq.c[c], sq.t[:, c, :N], c == 0, c == 7)
        rstd_from(pb, N, D)
        for c in range(8):
            P.op("dve", lambda e, c=c: e.scalar_tensor_tensor(hT.t[:, c, :N], X.t[:, c, :N],
                                                              vecs.t[:, gcol + c:gcol + c + 1], rstd.t[:, :N],
                                                              ALU.mult, ALU.mult), r=[X.c[c], vecs, rstd], w=[hT.c[c]])

    def proj_fm(name, rhs_list, N, kchunks=8):
        wb = wnext(name)
        pb = bank()
        for k in range(kchunks):
            rb, rap = rhs_list[k]
            mm(pb, pb.t[:, :N], wb, wb.t[:, k * 128:(k + 1) * 128], rb, rap, k == 0, k == kchunks - 1)
        return pb

    def postnorm_residual(X, N, gcol, pbs):
        for c in range(8):
            pb = pbs[c]()
            P.op("act", lambda e, c=c, pb=pb: e.activation(out=stage.t[:, c, :N], in_=pb.t[:, :N], func=AF.Copy),
                 r=[pb], w=[stage.c[c]])
            P.op("act", lambda e, c=c, pb=pb: e.activation(out=sq.t[:, c, :N], in_=pb.t[:, :N], func=AF.Square),
                 r=[pb], w=[sq.c[c]])
        pb2 = bank()
        for c in range(8):
            mm(pb2, pb2.t[:, :N], ones, ones.t[:], sq.c[c], sq.t[:, c, :N], c == 0, c == 7)
        rstd_from(pb2, N, D)
        for c in range(8):
            eng = "pool" if c % 2 == 1 else "dve"
            P.op(eng, lambda e, c=c: e.tensor_tensor(stage.t[:, c, :N], stage.t[:, c, :N], rstd.t[:, :N], ALU.mult),
                 r=[stage.c[c], rstd], w=[stage.c[c]])
        for c in range(8):
            P.op("dve", lambda e, c=c: e.scalar_tensor_tensor(X.t[:, c, :N], stage.t[:, c, :N],
                                                              vecs.t[:, gcol + c:gcol + c + 1], X.t[:, c, :N],
                                                              ALU.mult, ALU.add),
                 r=[stage.c[c], vecs, X.c[c]], w=[X.c[c]])

    hT_chunks = lambda N: [(hT.c[k], hT.t[:, k, :N]) for k in range(8)]

    def do_memkv():
        memX = stage
        P.dma("sp", lambda e: e.dma_start(out=memX.t[:, :, :NMEM], in_=memT.rearrange("c p n -> p c n")),
              "memx", w=memX.c)
        prenorm(memX, NMEM, V_MEMN)
        for c in range(8):
            pb = proj_fm(f"ck{c}", hT_chunks(NMEM), NMEM)
            P.op("act", lambda e, c=c, pb=pb: e.activation(out=KmT.t[:, c, :], in_=pb.t[:, :NMEM], func=AF.Copy),
                 r=[pb], w=[KmT])
        for c in range(8):
            wb = wnext(f"cv{c}")
            pb = bank()
            for mh in range(2):
                for k in range(8):
                    mm(pb, pb.t[:, mh * 128:(mh + 1) * 128], hT.c[k], hT.t[:, k, mh * 128:(mh + 1) * 128],
                       wb, wb.t[:, k * 128:(k + 1) * 128], k == 0, k == 7)
            P.op("act", lambda e, c=c, pb=pb: e.activation(
                out=Vm.t[:, :, c * 128:(c + 1) * 128],
                in_=pb.t[:, 0:256].rearrange("p (m d) -> p m d", m=2), func=AF.Copy), r=[pb], w=[Vm])

    def load_x(gi):
        kind, n0, nblk, ooff, last = groups[gi]
        N = nblk * BLK
        X = Xb[gi % 2]
        P.dma("sp", lambda e: e.dma_start(out=X.t[:, :, :N], in_=xT[:, :, n0:n0 + N].rearrange("c p n -> p c n")),
              ("x", gi % 2), w=X.c)

    class _V:
        def __init__(self, t, k):
            self.t3, self.k = t, k

        def __getitem__(self, idx):
            p, f = idx
            return self.t3[p, self.k, f]

    hsets = [(gg[0], gg[0].t, bc[0], bc[0].t, kk[0], kk[0].t),
             (gg[1], gg[1].t, bc[1], bc[1].t, kk[1], kk[1].t),
             (stage.c[0], _V(stage.t, 0), stage.c[1], _V(stage.t, 1), stage.c[2], _V(stage.t, 2)),
             (stage.c[3], _V(stage.t, 3), stage.c[4], _V(stage.t, 4), stage.c[5], _V(stage.t, 5))]

    def hgrn_prep_all(N, nblk, full):
        HS = range(4)
        gB = [hsets[h][0] for h in HS]; gT = [hsets[h][1] for h in HS]
        bB = [hsets[h][2] for h in HS]; bT = [hsets[h][3] for h in HS]
        kB = [hsets[h][4] for h in HS]; kT_ = [hsets[h][5] for h in HS]
        bc3 = [bT[h][:, :N].rearrange("p (b t) -> p b t", t=BLK) for h in HS]
        for h in HS:
            P.op("act", lambda e, h=h: e.activation(out=gT[h][:, :N], in_=th[h].t[:, :N], func=AF.Ln,
                                                    bias=lbA.t[:, h:h + 1], scale=lbB.t[:, h:h + 1]),
                 r=[th[h], lbA, lbB], w=[gB[h]])
        for h in HS:
            P.op("dve", lambda e, h=h: e.tensor_tensor_scan(bT[h][:, :N], rmask.t[:, :N], gT[h][:, :N], 0.0,
                                                            ALU.mult, ALU.add), r=[rmask, gB[h]], w=[bB[h]])
        for h in HS:
            P.op("act", lambda e, h=h: e.activation(out=kT_[h][:, :N], in_=th[h].t[:, :N], func=AF.Identity,
                                                    bias=lbB.t[:, h:h + 1], scale=lbnB.t[:, h:h + 1]),
                 r=[th[h], lbnB, lbB], w=[kB[h]])
        for h in HS:
            P.op("pool", lambda e, h=h: e.tensor_copy(cr[h].t[:, :nblk], bc3[h][:, :, 63]), r=[bB[h]], w=[cr[h]])
            P.op("act", lambda e, h=h: e.activation(out=dS[h].t[:, :nblk], in_=bc3[h][:, :, 127], func=AF.Exp),
                 r=[bB[h]], w=[dS[h]])
        for h in HS:
            P.op("dve", lambda e, h=h: e.tensor_tensor(bc3[h], bc3[h],
                                                       cr[h].t[:, :nblk].rearrange("p (b o) -> p b o", o=1)
                                                       .to_broadcast([128, nblk, BLK]), ALU.subtract),
                 r=[bB[h], cr[h]], w=[bB[h]])
        for h in HS:
            P.op("act", lambda e, h=h: e.activation(out=gT[h][:, :N], in_=bT[h][:, :N], func=AF.Exp, scale=-1.0),
                 r=[bB[h]], w=[gB[h]])
        for h in HS:
            P.op("dve", lambda e, h=h: e.tensor_tensor(kt[h].t[:, :N], kT_[h][:, :N], gT[h][:, :N], ALU.mult),
                 r=[kB[h], gB[h]], w=[kt[h]])
        for h in HS:
            P.op("act", lambda e, h=h: e.activation(out=dK[h].t[:, :nblk], in_=bc3[h][:, :, 127], func=AF.Exp),
                 r=[bB[h]], w=[dK[h]])
            if full:
                P.op("act", lambda e, h=h: e.activation(out=eC[h].t[:, :nblk], in_=cr[h].t[:, :nblk], func=AF.Exp),
                     r=[cr[h]], w=[eC[h]])
        if full:
            for h in HS:
                P.op("act", lambda e, h=h: e.activation(out=kT_[h][:, :N], in_=bT[h][:, :N], func=AF.Exp),
                     r=[bB[h]], w=[kB[h]])
            for h in HS:
                P.op("dve", lambda e, h=h: e.scalar_tensor_tensor(qt[h].t[:, :N], qs[h].t[:, :N],
                                                                  float(128 ** -0.5), kT_[h][:, :N],
                                                                  ALU.mult, ALU.mult),
                     r=[qs[h], kB[h]], w=[qt[h]])

    def hgrn_core(N, nblk, full):
        for h in range(4):
            tb = tbk()
            for b in range(nblk):
                P.op("pe", lambda e, tb=tb, h=h, b=b: e.transpose(tb.t[:, b * 128:(b + 1) * 128],
                                                                   kt[h].t[:, b * 128:(b + 1) * 128], ident.t[:]),
                     r=[kt[h], ident], w=[tb])
            P.op("act", lambda e, tb=tb, h=h: e.activation(
                out=ktok[h].t[:, :nblk, :], in_=tb.t[:, :N].rearrange("p (b f) -> p b f", f=128), func=AF.Copy),
                r=[tb], w=[ktok[h]])
        for h in range(4):
            ub = bank()
            for b in range(nblk):
                mm(ub, ub.t[:, b * 128:(b + 1) * 128], ktok[h], ktok[h].t[:, b, :], vtok[h], vtok[h].t[:, b, :],
                   True, True)
            P.op("act", lambda e, ub=ub, h=h: e.activation(out=Usb[h].t[:, :N], in_=ub.t[:, :N], func=AF.Copy),
                 r=[ub], w=[Usb[h]])
        if full:
            for h in range(4):
                ab = bank()
                for b in range(nblk):
                    mm(ab, ab.t[:, b * 128:(b + 1) * 128], kt[h], kt[h].t[:, b * 128:(b + 1) * 128],
                       qt[h], qt[h].t[:, b * 128:(b + 1) * 128], True, True)
                P.op("dve", lambda e, ab=ab, h=h: e.tensor_tensor(Am[h].t[:, :N], ab.t[:, :N], cmask.t[:, :N], ALU.mult),
                     r=[ab, cmask], w=[Am[h]])

        def s_update(h, b):
            P.op("act", lambda e: e.activation(out=S[h].t[:], in_=S[h].t[:], func=AF.Identity,
                                               scale=dS[h].t[:, b:b + 1]), r=[S[h], dS[h]], w=[S[h]])
            P.op("dve", lambda e: e.scalar_tensor_tensor(S[h].t[:], Usb[h].t[:, b * 128:(b + 1) * 128],
                                                         dK[h].t[:, b:b + 1], S[h].t[:], ALU.mult, ALU.add),
                 r=[Usb[h], dK[h], S[h]], w=[S[h]])

        if not full:
            for b in range(nblk):
                for h in range(4):
                    s_update(h, b)
            return
        for hp in range(2):
            hs = (2 * hp, 2 * hp + 1)
            ob = {h: lbank[i] for i, h in enumerate(hs)}
            for b in range(nblk):
                for h in hs:
                    sb_ = Sb[h][b % 2]
                    P.op("dve", lambda e, h=h, b=b, sb_=sb_: e.tensor_scalar(sb_.t[:], S[h].t[:], eC[h].t[:, b:b + 1],
                                                                             None, ALU.mult),
                         r=[S[h], eC[h]], w=[sb_])
                    o_b = ob[h]
                    mm(o_b, o_b.t[:, b * 128:(b + 1) * 128], vtok[h], vtok[h].t[:, b, :],
                       Am[h], Am[h].t[:, b * 128:(b + 1) * 128], True, False)
                    mm(o_b, o_b.t[:, b * 128:(b + 1) * 128], sb_, sb_.t[:], qt[h], qt[h].t[:, b * 128:(b + 1) * 128],
                       False, True)
                    s_update(h, b)
            for h in hs:
                o_b = ob[h]
                P.op("act", lambda e, o_b=o_b: e.activation(out=osq.t[:, :N], in_=o_b.t[:, :N], func=AF.Square),
                     r=[o_b], w=[osq])
                pb = bank()
                mm(pb, pb.t[:, :N], ones, ones.t[:], osq, osq.t[:, :N], True, True)
                rstd_from(pb, N, 128)
                P.op("dve", lambda e, o_b=o_b: e.tensor_tensor(r1.t[:, :N], o_b.t[:, :N], rstd.t[:, :N], ALU.mult),
                     r=[o_b, rstd], w=[r1])
                P.op("dve", lambda e, h=h: e.scalar_tensor_tensor(recT[h].t[:, :N], r1.t[:, :N],
                                                                  vecs.t[:, V_ONW:V_ONW + 1], gs[h].t[:, :N],
                                                                  ALU.mult, ALU.mult),
                     r=[r1, vecs, gs[h]], w=[recT[h]])

    def tokmajor(names, dsts, N, nblk):
        wbs = [wnext(nm) for nm in names]
        pbs = [bank() for _ in names]
        for b in range(nblk):
            for k in range(8):
                for wb, pb in zip(wbs, pbs):
                    mm(pb, pb.t[:, b * 128:(b + 1) * 128], hT.c[k], hT.t[:, k, b * 128:(b + 1) * 128],
                       wb, wb.t[:, k * 128:(k + 1) * 128], k == 0, k == 7)
        for pb, (db, dap) in zip(pbs, dsts):
            P.op("act", lambda e, pb=pb, dap=dap: e.activation(
                out=dap, in_=pb.t[:, :N].rearrange("p (b f) -> p b f", f=128), func=AF.Copy), r=[pb], w=[db])

    def roll_kv(nblk):
        for g in range(2):
            for par in range(2):
                P.op("pool", lambda e, g=g, par=par: e.tensor_copy(kT[g][par].t[:, 0:BLK],
                                                                    kT[g][par].t[:, nblk * BLK:(nblk + 1) * BLK]),
                     r=[kT[g][par]], w=[kT[g][par]])
            P.op("pool", lambda e, g=g: e.tensor_copy(Vd[g].t[:, 0, :], Vd[g].t[:, nblk, :]),
                 r=[Vd[g]], w=[Vd[g]])

    def do_prefix(gi):
        kind, n0, nblk, ooff, last = groups[gi]
        N = nblk * BLK
        X = Xb[gi % 2]
        prenorm(X, N, V_MIXPRE)
        if last:
            for g in range(2):
                for par, nm in enumerate(("ke", "ko")):
                    pb = proj_fm(f"{nm}{g}", hT_chunks(N), N)
                    P.op("act", lambda e, g=g, par=par, pb=pb: e.activation(
                        out=kT[g][par].t[:, BLK:BLK + N], in_=pb.t[:, :N], func=AF.Copy), r=[pb], w=[kT[g][par]])
        for h in range(4):
            pb = proj_fm(f"fh{h}", hT_chunks(N), N)
            P.op("act", lambda e, h=h, pb=pb: e.activation(out=th[h].t[:, :N], in_=pb.t[:, :N], func=AF.Tanh,
                                                           scale=0.5), r=[pb], w=[th[h]])
        if last:
            tokmajor(["vd0", "vd1", "ih0"], [(Vd[0], Vd[0].t[:, 1:1 + nblk, :]), (Vd[1], Vd[1].t[:, 1:1 + nblk, :]),
                                             (vtok[0], vtok[0].t[:, :nblk, :])], N, nblk)
            tokmajor(["ih1", "ih2", "ih3"], [(vtok[h], vtok[h].t[:, :nblk, :]) for h in (1, 2, 3)], N, nblk)
        else:
            tokmajor(["ih0", "ih1"], [(vtok[h], vtok[h].t[:, :nblk, :]) for h in (0, 1)], N, nblk)
            tokmajor(["ih2", "ih3"], [(vtok[h], vtok[h].t[:, :nblk, :]) for h in (2, 3)], N, nblk)
        hgrn_prep_all(N, nblk, False)
        hgrn_core(N, nblk, False)
        if last:
            roll_kv(nblk)

    def do_full(gi):
        kind, n0, nblk, ooff, last = groups[gi]
        N = nblk * BLK
        X = Xb[gi % 2]
        first_main = (kind == "main" and ooff == 0)
        prenorm(X, N, V_MIXPRE)
        for c in range(4):
            pb = proj_fm(f"qa{c}", hT_chunks(N), N)
            P.op("dve", lambda e, c=c, pb=pb: e.tensor_copy(qaT[c].t[:, :N], pb.t[:, :N]), r=[pb], w=[qaT[c]])
        for g in range(2):
            for par, nm in enumerate(("ke", "ko")):
                pb = proj_fm(f"{nm}{g}", hT_chunks(N), N)
                P.op("dve", lambda e, g=g, par=par, pb=pb: e.tensor_copy(kT[g][par].t[:, BLK:BLK + N], pb.t[:, :N]),
                     r=[pb], w=[kT[g][par]])
        for h in range(4):
            pb = proj_fm(f"fh{h}", hT_chunks(N), N)
            P.op("act", lambda e, h=h, pb=pb: e.activation(out=th[h].t[:, :N], in_=pb.t[:, :N], func=AF.Tanh,
                                                           scale=0.5), r=[pb], w=[th[h]])
        for h in range(4):
            pb = proj_fm(f"qh{h}", hT_chunks(N), N)
            P.op("act", lambda e, h=h, pb=pb: e.activation(out=qs[h].t[:, :N], in_=pb.t[:, :N], func=AF.Silu),
                 r=[pb], w=[qs[h]])
        for h in range(4):
            pb = proj_fm(f"gh{h}", hT_chunks(N), N)
            P.op("act", lambda e, h=h, pb=pb: e.activation(out=gs[h].t[:, :N], in_=pb.t[:, :N], func=AF.Silu),
                 r=[pb], w=[gs[h]])
        tokmajor(["vd0", "vd1", "ih0"], [(Vd[0], Vd[0].t[:, 1:1 + nblk, :]), (Vd[1], Vd[1].t[:, 1:1 + nblk, :]),
                                         (vtok[0], vtok[0].t[:, :nblk, :])], N, nblk)
        tokmajor(["ih1", "ih2", "ih3"], [(vtok[h], vtok[h].t[:, :nblk, :]) for h in (1, 2, 3)], N, nblk)
        hgrn_prep_all(N, nblk, True)
        for b in range(nblk):
            for g in range(2):
                i2 = (b * 2 + g) % 2
                sp_, sc_ = bank(), bank()
                mprev = maskp0 if (first_main and b == 0) else maskp
                mm(sp_, sp_.t[:, :], ident, ident.t[:], mprev, mprev.t[:], True, False)
                mm(sc_, sc_.t[:, :], ident, ident.t[:], maskc, maskc.t[:], True, False)
                for j in range(4):
                    hq = 4 * g + j
                    c = hq // 2
                    kb = kT[g][j % 2]
                    mm(sp_, sp_.t[:, j * 128:(j + 1) * 128], kb, kb.t[:, b * 128:(b + 1) * 128],
                       qaT[c], qaT[c].t[:, b * 128:(b + 1) * 128], False, j == 3)
                    mm(sc_, sc_.t[:, j * 128:(j + 1) * 128], kb, kb.t[:, (b + 1) * 128:(b + 2) * 128],
                       qaT[c], qaT[c].t[:, b * 128:(b + 1) * 128], False, j == 3)
                pp, pc, rd = Pp[i2], Pc[i2], rden[i2]
                P.op("act", lambda e, sp_=sp_, pp=pp: e.activation(out=pp.t[:], in_=sp_.t[:], func=AF.Exp, scale=0.125),
                     r=[sp_], w=[pp])
                P.op("act", lambda e, sc_=sc_, pc=pc: e.activation(out=pc.t[:], in_=sc_.t[:], func=AF.Exp, scale=0.125),
                     r=[sc_], w=[pc])
                ob_, db_ = bank(), bank()
                mm(ob_, ob_.t[:], Vd[g], Vd[g].t[:, b, :], pp, pp.t[:], True, False)
                mm(ob_, ob_.t[:], Vd[g], Vd[g].t[:, b + 1, :], pc, pc.t[:], False, True)
                mm(db_, db_.t[:], ones, ones.t[:], pp, pp.t[:], True, False)
                mm(db_, db_.t[:], ones, ones.t[:], pc, pc.t[:], False, False)
                mm(db_, db_.t[:], ones, ones.t[0:1, :], esink, esink.t[0:1, g * 512:(g + 1) * 512], False, True)
                P.op("act", lambda e, db_=db_, rd=rd: e.activation(out=rd.t[:], in_=db_.t[:], func=AF.Ln),
                     r=[db_], w=[rd])
                P.op("act", lambda e, rd=rd: e.activation(out=rd.t[:], in_=rd.t[:], func=AF.Exp, scale=-1.0),
                     r=[rd], w=[rd])
                for half in range(2):
                    p0 = half * 64
                    P.op("dve", lambda e, half=half, p0=p0, ob_=ob_, rd=rd, g=g, b=b: e.tensor_tensor(
                        attnT.t[p0:p0 + 64, 2 * g:2 * g + 2, b * 128:(b + 1) * 128],
                        ob_.t[p0:p0 + 64, :].rearrange("p (i two q) -> p i two q", two=2, q=128)[:, :, half, :],
                        rd.t[p0:p0 + 64, :].rearrange("p (i two q) -> p i two q", two=2, q=128)[:, :, half, :],
                        ALU.mult), r=[ob_, rd], w=[attnT.c[2 * g], attnT.c[2 * g + 1]])
        roll_kv(nblk)
        hgrn_core(N, nblk, True)
        mixk = [(attnT.c[c], attnT.t[:, c, :N]) for c in range(4)] + [(recT[h], recT[h].t[:, :N]) for h in range(4)]
        postnorm_residual(X, N, V_MIXPOST, [lambda c=c: proj_fm(f"wo{c}", mixk, N) for c in range(8)])

        if 'ca' not in STAGES:
            for c in range(8):
                wnext(f"cq{c}")
            for c in range(8):
                wnext(f"co{c}")
        else:
            do_ca(X, N)
        do_ffn(gi, X, N, kind, ooff, first_main)

    def do_ca(X, N):
        prenorm(X, N, V_CAPRE)
        for c in range(8):
            pb = proj_fm(f"cq{c}", hT_chunks(N), N)
            P.op("dve", lambda e, c=c, pb=pb: e.tensor_copy(qcT[c].t[:, :N], pb.t[:, :N]), r=[pb], w=[qcT[c]])
        for hh in range(4):
            for mh in range(2):
                pb = bank()
                for i, dc in enumerate((2 * hh, 2 * hh + 1)):
                    mm(pb, pb.t[:, :N], KmT, KmT.t[:, dc, mh * 128:(mh + 1) * 128], qcT[dc], qcT[dc].t[:, :N],
                       i == 0, i == 1)
                P.op("act", lambda e, hh=hh, mh=mh, pb=pb: e.activation(out=PT[hh][mh].t[:, :N], in_=pb.t[:, :N],
                                                                        func=AF.Exp, scale=1.0 / 16.0),
                     r=[pb], w=[PT[hh][mh]])
            db_ = bank()
            for mh in range(2):
                mm(db_, db_.t[:, :N], ones, ones.t[:], PT[hh][mh], PT[hh][mh].t[:, :N], mh == 0, mh == 1)
            rd = rdc[hh % 2]
            P.op("act", lambda e, db_=db_, rd=rd: e.activation(out=rd.t[:, :N], in_=db_.t[:, :N], func=AF.Ln),
                 r=[db_], w=[rd])
            P.op("act", lambda e, rd=rd: e.activation(out=rd.t[:, :N], in_=rd.t[:, :N], func=AF.Exp, scale=-1.0),
                 r=[rd], w=[rd])
            for dc in (2 * hh, 2 * hh + 1):
                pb = bank()
                for mh in range(2):
                    mm(pb, pb.t[:, :N], Vm, Vm.t[:, mh, dc * 128:(dc + 1) * 128], PT[hh][mh], PT[hh][mh].t[:, :N],
                       mh == 0, mh == 1)
                P.op("dve", lambda e, dc=dc, pb=pb, rd=rd: e.tensor_tensor(ocT[dc].t[:, :N], pb.t[:, :N], rd.t[:, :N],
                                                                           ALU.mult), r=[pb, rd], w=[ocT[dc]])
        ock = [(ocT[c], ocT[c].t[:, :N]) for c in range(8)]
        postnorm_residual(X, N, V_CAPOST, [lambda c=c: proj_fm(f"co{c}", ock, N) for c in range(8)])

    def do_ffn(gi, X, N, kind, ooff, first_main):
        if 'ffn' not in STAGES:
            for j in range(NPAIR):
                wnext(f"up{j}")
                wnext(f"up{NPAIR + j}")
            for c in range(8):
                for s_ in range(3):
                    wnext(f"dn{c}_{s_}")
            if kind == "main":
                ev = P.dma("sp", lambda e: e.dma_start(out=outT[:, :, ooff:ooff + N].rearrange("c p n -> p c n"),
                                                       in_=X.t[:, :, :N]), ("xo", gi % 2), r=X.c)
                P.finals.append(ev)
            return
        prenorm(X, N, V_FFNPRE)
        for j in range(NPAIR):
            i2 = j % 3
            outs = []
            for (ci, ub, cb) in ((j, ug[i2], cg[i2]), (NPAIR + j, uv[i2], cv[i2])):
                pb = proj_fm(f"up{ci}", hT_chunks(N), N)
                P.op("act", lambda e, ci=ci, pb=pb, cb=cb: e.activation(
                    out=cb.t[:, :N], in_=pb.t[:, :N], func=AF.Identity,
                    bias=vecs.t[:, V_CB + ci:V_CB + ci + 1], scale=vecs.t[:, V_CW + 88 + ci:V_CW + 88 + ci + 1]),
                    r=[pb, vecs], w=[cb])
                P.op("act", lambda e, pb=pb, ub=ub: e.activation(out=ub.t[:, 2:2 + N], in_=pb.t[:, :N], func=AF.Copy),
                     r=[pb], w=[ub])
                if first_main:
                    P.op("pool", lambda e, ci=ci, ub=ub: e.tensor_scalar(ub.t[:, 0:2], uh.t[:, ci, :], hflag.t[:, 0:1],
                                                                         None, ALU.mult), r=[uh, hflag], w=[ub])
                else:
                    P.op("pool", lambda e, ci=ci, ub=ub: e.tensor_copy(ub.t[:, 0:2], uh.t[:, ci, :]), r=[uh], w=[ub])
                P.op("pool", lambda e, ci=ci, ub=ub: e.tensor_copy(uh.t[:, ci, :], ub.t[:, N:N + 2]), r=[ub], w=[uh])
                P.op("dve", lambda e, ci=ci, ub=ub, cb=cb: e.scalar_tensor_tensor(
                    cb.t[:, :N], ub.t[:, 0:N], vecs.t[:, V_CW + ci:V_CW + ci + 1], cb.t[:, :N], ALU.mult, ALU.add),
                    r=[ub, vecs, cb], w=[cb])
                P.op("dve", lambda e, ci=ci, ub=ub, cb=cb: e.scalar_tensor_tensor(
                    cb.t[:, :N], ub.t[:, 1:N + 1], vecs.t[:, V_CW + 44 + ci:V_CW + 44 + ci + 1], cb.t[:, :N],
                    ALU.mult, ALU.add), r=[ub, vecs, cb], w=[cb])
            if kind == "main":
                P.op("act", lambda e, i2=i2: e.activation(out=cg[i2].t[:, :N], in_=cg[i2].t[:, :N],
                                                          func=AF.Gelu_apprx_tanh), r=[cg[i2]], w=[cg[i2]])
                P.op("dve", lambda e, i2=i2, j=j: e.tensor_tensor(aT[j].t[:, :N], cg[i2].t[:, :N], cv[i2].t[:, :N],
                                                                  ALU.mult), r=[cg[i2], cv[i2]], w=[aT[j]])
        if kind == "main":
            def dn_chunk(c):
                pb = bank()
                for s in range(3):
                    wb = wnext(f"dn{c}_{s}")
                    for kq in range(8):
                        jj = s * 8 + kq
                        if jj >= NPAIR:
                            break
                        mm(pb, pb.t[:, :N], wb, wb.t[:, kq * 128:(kq + 1) * 128], aT[jj], aT[jj].t[:, :N],
                           jj == 0, jj == NPAIR - 1)
                return pb
            postnorm_residual(X, N, V_FFNPOST, [lambda c=c: dn_chunk(c) for c in range(8)])
            ev = P.dma("sp", lambda e: e.dma_start(out=outT[:, :, ooff:ooff + N].rearrange("c p n -> p c n"),
                                                   in_=X.t[:, :, :N]), ("xo", gi % 2), r=X.c)
            P.finals.append(ev)
        else:
            for c in range(8):
                for s in range(3):
                    wnext(f"dn{c}_{s}")

    try:
        ckpt('consts')
        do_memkv()
        ckpt('memkv')
        load_x(0)
        for gi in range(len(groups)):
            if gi + 1 < len(groups):
                load_x(gi + 1)
            if groups[gi][0] == "pre":
                do_prefix(gi)
                ckpt('prefix')
            else:
                do_full(gi)
        assert wstate["next"] == len(wseq), (wstate, len(wseq))
    except StopBuild:
        pass
    P.emit()
    es.close()
    global _LAST
    _LAST = dict(P=P, attnT=attnT, recT=recT, X=Xb, hT=hT, qaT=qaT, kT=kT, Vd=Vd, th=th, qs=qs, gs=gs, qt=qt, kt=kt, S=S, vtok=vtok, stage=stage)
    return nc, P


def run_cores(inputs, PRE=None, trace=False):
    x = np.asarray(inputs["x"], np.float32)
    mem = np.asarray(inputs["mem"], np.float32)
    B, T, _ = x.shape
    TH = T // 2
    if PRE is None:
        PRE = TH
    ws = build_wslots(inputs)
    vecs = build_vecs(inputs)
    sinks = np.asarray(inputs["attn_sinks"][0], np.float32)
    csts = [build_cst(0, sinks), build_cst(1, sinks)]
    nc, P = build_program(TH, PRE)
    in_maps = []
    for c in range(2 * B):
        b, half = c // 2, c % 2
        start = half * TH
        lo = start - PRE - BLK
        NTOT = PRE + BLK + TH
        xs = np.zeros((NTOT, D), np.float32)
        src_lo = max(lo, 0)
        xs[src_lo - lo:] = x[b, src_lo:start + TH]
        xT = np.ascontiguousarray(xs.T).reshape(8, 128, NTOT)
        memT = np.ascontiguousarray(mem[b].T).reshape(8, 128, NMEM)
        in_maps.append({"xT": xT, "memT": memT, "wslots": ws, "vecs": vecs, "cst": csts[half]})
    res = run_bass_kernel_spmd(nc, in_maps, core_ids=list(range(2 * B)), trace=trace)
    out = np.zeros((B, T, D), np.float32)
    for c in range(2 * B):
        b, half = c // 2, c % 2
        o = np.asarray(res.results[c]["outT"]).reshape(D, TH)
        out[b, half * TH:(half + 1) * TH] = o.T
    return out, res


def kernel(**inputs):
    out, _ = run_cores(inputs)
    return out
```

```python
import numpy as np
from contextlib import ExitStack
import concourse.bass as bass
import concourse.mybir as mybir
from concourse.bass_utils import run_bass_kernel_spmd

F32 = mybir.dt.float32
BF16 = mybir.dt.bfloat16
AF = mybir.ActivationFunctionType
ALU = mybir.AluOpType

D = 1024
NMEM = 256
DFF = 2816
NPAIR = 22
EPS = 1e-6
NEG = -30000.0
GB = 4
BLK = 128
STAGES = ('mixer', 'ca', 'ffn')
STOPAT = None


class StopBuild(Exception):
    pass


def ckpt(name):
    if STOPAT == name:
        raise StopBuild(name)
RING = 12


class Buf:
    __slots__ = ("t", "w", "r", "name")

    def __init__(self, t, name=None):
        self.t = t
        self.w = None
        self.r = {}
        self.name = name


class MB:
    def __init__(self, t, n, name):
        self.t = t
        self.c = [Buf(t, f"{name}.{i}") for i in range(n)]
        self.name = name


class Prog:
    ENGS = ("pe", "act", "dve", "pool", "sp")

    def __init__(self, nc, es):
        self.nc, self.es = nc, es
        self.items = {e: [] for e in self.ENGS}
        self.clk = {e: {} for e in self.ENGS}
        self.cnt = {e: 0 for e in self.ENGS}
        self.dcnt = {}
        self.nbuf = 0
        self.finals = []

    def sb(self, shape, dt, name=None):
        self.nbuf += 1
        nm = f"{name or 'b'}_{self.nbuf}"
        t = self.es.enter_context(self.nc.sbuf_tensor(nm, list(shape), dt))
        return Buf(t, nm)

    def mb(self, shape, dt, name):
        b = self.sb(shape, dt, name)
        return MB(b.t, shape[1], b.name)

    def ps(self, shape, dt, name=None):
        self.nbuf += 1
        nm = f"{name or 'p'}_{self.nbuf}"
        t = self.es.enter_context(self.nc.psum_tensor(nm, list(shape), dt))
        return Buf(t, nm)

    def _resolve(self, eng, r, w):
        clk = self.clk[eng]
        need = {}

        def add(kind, ev):
            key, val, snap = ev
            if key == eng and eng == "pe":
                return
            if clk.get(key, 0) >= val:
                return
            if key not in need or need[key][0] < val:
                need[key] = (val, snap)

        for b in r:
            if b.w is not None:
                add("raw", b.w)
        for b in w:
            if b.w is not None:
                add("waw", b.w)
            for ev in b.r.values():
                add("war", ev)
        for key, (val, snap) in need.items():
            if clk.get(key, 0) >= val:
                continue
            self.items[eng].append(("wait", key, val))
            for k2, v2 in snap.items():
                if clk.get(k2, 0) < v2:
                    clk[k2] = v2

    def _commit(self, ev, r, w):
        key = ev[0]
        for b in r:
            old = b.r.get(key)
            if old is None or old[1] < ev[1]:
                b.r[key] = ev
        for b in w:
            b.w = ev
            b.r = {}

    def op(self, eng, fn, r=(), w=()):
        self._resolve(eng, r, w)
        self.cnt[eng] += 1
        val = self.cnt[eng]
        snap = dict(self.clk[eng])
        snap[eng] = val
        ev = (eng, val, snap)
        self.items[eng].append(("op", fn, eng, 1))
        self._commit(ev, r, w)
        return ev

    def dma(self, q, fn, semkey, r=(), w=()):
        self._resolve(q, r, w)
        self.dcnt[semkey] = self.dcnt.get(semkey, 0) + 16
        val = self.dcnt[semkey]
        snap = dict(self.clk[q])
        snap[semkey] = val
        ev = (semkey, val, snap)
        self.items[q].append(("op", fn, semkey, 16))
        self._commit(ev, r, w)
        return ev

    def emit(self):
        nc, es = self.nc, self.es
        for ev in self.finals:
            if self.clk["sp"].get(ev[0], 0) < ev[1]:
                self.items["sp"].append(("wait", ev[0], ev[1]))
                self.clk["sp"][ev[0]] = ev[1]
        keys = []
        for e in self.ENGS:
            for it in self.items[e]:
                k = it[1] if it[0] == "wait" else it[2]
                if k not in keys:
                    keys.append(k)
        sems = {k: es.enter_context(nc.semaphore(f"sem{i}")) for i, k in enumerate(keys)}
        self.nsem = len(sems)
        block = es.enter_context(nc.Block())
        items = self.items

        def runner(name):
            def f(e):
                pend = []
                for it in items[name]:
                    if it[0] == "wait":
                        pend.append(it)
                        continue
                    is_dma = it[3] == 16
                    attach = None
                    if pend and not is_dma:
                        attach = pend.pop()
                    for w_ in pend:
                        e.wait_ge(sems[w_[1]], w_[2])
                    pend = []
                    ins = it[1](e)
                    if attach is not None:
                        ins = ins._wait_ge(sems[attach[1]], attach[2])
                    ins.then_inc(sems[it[2]], it[3])
                for w_ in pend:
                    e.wait_ge(sems[w_[1]], w_[2])
            return f

        block.tensor(runner("pe"))
        block.scalar(runner("act"))
        block.vector(runner("dve"))
        block.gpsimd(runner("pool"))
        block.sync(runner("sp"))


def slot_table():
    names = []
    names += [f"qa{c}" for c in range(4)]
    names += [f"ke{g}" for g in range(2)]
    names += [f"ko{g}" for g in range(2)]
    names += [f"vd{g}" for g in range(2)]
    for nm in ("qh", "fh", "ih", "gh"):
        names += [f"{nm}{h}" for h in range(4)]
    names += [f"wo{c}" for c in range(8)]
    names += [f"cq{c}" for c in range(8)]
    names += [f"co{c}" for c in range(8)]
    names += [f"up{c}" for c in range(44)]
    names += [f"dn{c}_{s}" for c in range(8) for s in range(3)]
    names += [f"ck{c}" for c in range(8)]
    names += [f"cv{c}" for c in range(8)]
    return {n: i for i, n in enumerate(names)}


SLOTS = slot_table()
NSLOT = len(SLOTS)


def _tile_cols(W, cols):
    K = W.shape[0]
    kc = K // 128
    out = np.zeros((128, 1024), np.float32)
    sub = W[:, cols]
    out[:, : kc * 128] = sub.reshape(kc, 128, 128).transpose(1, 0, 2).reshape(128, kc * 128)
    return out


def build_wslots(inp):
    w_in = np.asarray(inp["w_in"][0], np.float32)
    ws = np.zeros((NSLOT, 128, 1024), np.float32)
    ar = np.arange(128)
    for c in range(4):
        ws[SLOTS[f"qa{c}"]] = _tile_cols(w_in, c * 128 + ar)
    for g in range(2):
        kd = _tile_cols(w_in, 512 + g * 64 + (ar % 64)).reshape(128, 8, 128)
        ke = kd.copy(); ke[:, :, 64:] = 0.0
        ko = kd.copy(); ko[:, :, :64] = 0.0
        ws[SLOTS[f"ke{g}"]] = ke.reshape(128, 1024)
        ws[SLOTS[f"ko{g}"]] = ko.reshape(128, 1024)
        ws[SLOTS[f"vd{g}"]] = _tile_cols(w_in, 640 + g * 64 + (ar % 64))
    for h in range(4):
        ws[SLOTS[f"qh{h}"]] = _tile_cols(w_in, 768 + h * 128 + ar)
        ws[SLOTS[f"fh{h}"]] = _tile_cols(w_in, 1280 + h * 128 + ar)
        ws[SLOTS[f"ih{h}"]] = _tile_cols(w_in, 1792 + h * 128 + ar)
        ws[SLOTS[f"gh{h}"]] = _tile_cols(w_in, 2304 + h * 128 + ar)
    w_out = np.asarray(inp["w_out"][0], np.float32)
    cq = np.asarray(inp["ca_wq"][0], np.float32)
    ck = np.asarray(inp["ca_wk"][0], np.float32)
    cv = np.asarray(inp["ca_wv"][0], np.float32)
    co = np.asarray(inp["ca_wo"][0], np.float32)
    for c in range(8):
        ws[SLOTS[f"wo{c}"]] = _tile_cols(w_out, c * 128 + ar)
        ws[SLOTS[f"cq{c}"]] = _tile_cols(cq, c * 128 + ar)
        ws[SLOTS[f"co{c}"]] = _tile_cols(co, c * 128 + ar)
        ws[SLOTS[f"ck{c}"]] = _tile_cols(ck, c * 128 + ar)
        ws[SLOTS[f"cv{c}"]] = _tile_cols(cv, c * 128 + ar)
    up = np.asarray(inp["ffn_w_up"][0], np.float32)
    for c in range(44):
        ws[SLOTS[f"up{c}"]] = _tile_cols(up, c * 128 + ar)
    dn = np.asarray(inp["ffn_w_down"][0], np.float32)
    for c in range(8):
        for s in range(3):
            k0 = s * 8 * 128
            k1 = min(DFF, k0 + 1024)
            ws[SLOTS[f"dn{c}_{s}"]] = _tile_cols(dn[k0:k1], c * 128 + ar)
    return ws


V_MIXPRE, V_MIXPOST, V_CAPRE, V_MEMN, V_CAPOST, V_FFNPRE, V_FFNPOST = 0, 8, 16, 24, 32, 40, 48
V_ONW = 56
V_LB = 57
V_CW = 65
V_CB = V_CW + 132
NVEC = V_CB + 44


def build_vecs(inp):
    v = np.zeros((128, NVEC), np.float32)

    def col8(a):
        return np.asarray(a, np.float32).reshape(8, 128).T

    v[:, V_MIXPRE:V_MIXPRE + 8] = col8(inp["mix_pre_norm"][0])
    v[:, V_MIXPOST:V_MIXPOST + 8] = col8(inp["mix_post_norm"][0])
    v[:, V_CAPRE:V_CAPRE + 8] = col8(inp["ca_pre_norm"][0])
    v[:, V_MEMN:V_MEMN + 8] = col8(inp["mem_norm"][0])
    v[:, V_CAPOST:V_CAPOST + 8] = col8(inp["ca_post_norm"][0])
    v[:, V_FFNPRE:V_FFNPRE + 8] = col8(inp["ffn_pre_norm"][0])
    v[:, V_FFNPOST:V_FFNPOST + 8] = col8(inp["ffn_post_norm"][0])
    v[:, V_ONW] = np.asarray(inp["hgrn_out_norm"][0], np.float32)
    lb = np.asarray(inp["hgrn_lb_logits"], np.float32)
    v[:, V_LB:V_LB + 4] = lb[0].reshape(4, 128).T
    v[:, V_LB + 4:V_LB + 8] = lb[1].reshape(4, 128).T
    cw = np.asarray(inp["ffn_conv_w"][0], np.float32)
    for t in range(3):
        v[:, V_CW + t * 44:V_CW + (t + 1) * 44] = cw[t].reshape(44, 128).T
    v[:, V_CB:V_CB + 44] = np.asarray(inp["ffn_conv_b"][0], np.float32).reshape(44, 128).T
    return v


def build_cst(half, sinks):
    c = np.zeros((128, 8, 512), np.float32)
    k = np.arange(128)[:, None]
    q = np.arange(128)[None, :]
    c[:, 0, 0:128] = np.eye(128, dtype=np.float32)
    mc = np.where(k <= q, 0.0, NEG).astype(np.float32)
    mp = np.where(k > q, 0.0, NEG).astype(np.float32)
    cm = (k <= q).astype(np.float32)
    c[:, 1, :] = np.tile(mc, (1, 4))
    c[:, 2, :] = np.tile(mp, (1, 4))
    c[:, 3, :] = np.tile(cm, (1, 4))
    rm = np.ones((128, 512), np.float32)
    rm[:, ::128] = 0.0
    c[:, 4, :] = rm
    if half == 0:
        c[:, 5, :] = NEG
        c[:, 0, 128] = 0.0
    else:
        c[:, 5, :] = c[:, 2, :]
        c[:, 0, 128] = 1.0
    sr = np.repeat(np.asarray(sinks, np.float32), 128)
    c[0, 6, :] = sr[0:512]
    c[0, 7, :] = sr[512:1024]
    return c


def build_program(TH, PRE):
    assert TH % 512 == 0 and PRE % 512 == 0
    NTOT = PRE + BLK + TH
    nc = bass.Bass("TRN2", target_bir_lowering=False)
    xT = nc.dram_tensor("xT", [8, 128, NTOT], F32, kind="ExternalInput").ap()
    memT = nc.dram_tensor("memT", [8, 128, NMEM], F32, kind="ExternalInput").ap()
    wsl = nc.dram_tensor("wslots", [NSLOT, 128, 1024], F32, kind="ExternalInput").ap()
    vecs_d = nc.dram_tensor("vecs", [128, NVEC], F32, kind="ExternalInput").ap()
    cst_d = nc.dram_tensor("cst", [128, 8, 512], F32, kind="ExternalInput").ap()
    outT = nc.dram_tensor("outT", [8, 128, TH], F32, kind="ExternalOutput").ap()

    es = ExitStack()
    P = Prog(nc, es)
    NMAX = GB * BLK

    stage = P.mb([128, 8, NMAX], F32, "stage")
    cstf = stage
    vecs = P.sb([128, NVEC], F32, "vecs")
    P.dma("sp", lambda e: e.dma_start(out=vecs.t[:], in_=vecs_d), "c0", w=[vecs])
    P.dma("sp", lambda e: e.dma_start(out=cstf.t[:], in_=cst_d), "c0", w=cstf.c)

    ident = P.sb([128, 128], BF16, "ident")
    ones = P.sb([128, 128], BF16, "ones")
    maskc = P.sb([128, 512], BF16, "maskc")
    maskp = P.sb([128, 512], BF16, "maskp")
    maskp0 = P.sb([128, 512], BF16, "maskp0")
    cmask = P.sb([128, 512], F32, "cmask")
    rmask = P.sb([128, 512], F32, "rmask")
    esink = P.sb([1, 1024], BF16, "esink")
    hflag = P.sb([128, 1], F32, "hflag")
    lbA = P.sb([128, 4], F32, "lbA")
    lbB = P.sb([128, 4], F32, "lbB")
    lbnB = P.sb([128, 4], F32, "lbnB")
    lbt = P.sb([128, 4], F32, "lbt")

    P.op("dve", lambda e: e.tensor_copy(ident.t[:], cstf.t[:, 0, 0:128]), r=cstf.c, w=[ident])
    P.op("dve", lambda e: e.memset(ones.t[:], 1.0), w=[ones])
    P.op("dve", lambda e: e.tensor_copy(maskc.t[:], cstf.t[:, 1, :]), r=cstf.c, w=[maskc])
    P.op("dve", lambda e: e.tensor_copy(maskp.t[:], cstf.t[:, 2, :]), r=cstf.c, w=[maskp])
    P.op("dve", lambda e: e.tensor_copy(cmask.t[:], cstf.t[:, 3, :]), r=cstf.c, w=[cmask])
    P.op("dve", lambda e: e.tensor_copy(rmask.t[:], cstf.t[:, 4, :]), r=cstf.c, w=[rmask])
    P.op("dve", lambda e: e.tensor_copy(maskp0.t[:], cstf.t[:, 5, :]), r=cstf.c, w=[maskp0])
    P.op("dve", lambda e: e.tensor_copy(hflag.t[:], cstf.t[:, 0, 128:129]), r=cstf.c, w=[hflag])
    P.op("act", lambda e: e.activation(out=esink.t[0:1, :].rearrange("p (g n) -> p g n", g=2),
                                       in_=cstf.t[0:1, 6:8, :], func=AF.Exp), r=cstf.c, w=[esink])
    P.op("dve", lambda e: e.tensor_tensor(lbt.t[:], vecs.t[:, V_LB + 4:V_LB + 8], vecs.t[:, V_LB:V_LB + 4],
                                          ALU.subtract), r=[vecs], w=[lbt])
    P.op("act", lambda e: e.activation(out=lbt.t[:], in_=lbt.t[:], func=AF.Exp), r=[lbt], w=[lbt])
    P.op("act", lambda e: e.activation(out=lbt.t[:], in_=lbt.t[:], func=AF.Ln, bias=1.0, scale=1.0), r=[lbt], w=[lbt])
    P.op("act", lambda e: e.activation(out=lbt.t[:], in_=lbt.t[:], func=AF.Exp, scale=-1.0), r=[lbt], w=[lbt])
    P.op("dve", lambda e: e.tensor_scalar(lbA.t[:], lbt.t[:], 0.5, 0.5, ALU.mult, ALU.add), r=[lbt], w=[lbA])
    P.op("dve", lambda e: e.tensor_scalar(lbB.t[:], lbt.t[:], -0.5, 0.5, ALU.mult, ALU.add), r=[lbt], w=[lbB])
    P.op("dve", lambda e: e.tensor_scalar(lbnB.t[:], lbt.t[:], 0.5, -0.5, ALU.mult, ALU.add), r=[lbt], w=[lbnB])

    try:
        ckpt('none')
    except StopBuild:
        pass
    banks = [P.ps([128, 512], F32, f"bank{i}") for i in range(5)]
    lbank = [P.ps([128, 512], F32, f"lbank{i}") for i in range(2)]
    tbank = [P.ps([128, 1024], BF16, "tbank")]
    bstate = {"i": 0, "t": 0}

    def bank():
        b = banks[bstate["i"] % len(banks)]
        bstate["i"] += 1
        return b

    def tbk():
        b = tbank[0]
        bstate["t"] += 1
        return b

    ring = [P.sb([128, 1024], BF16, f"ring{i}") for i in range(RING)]
    wseq = []
    wstate = {"next": 0, "issued": 0}

    def wnext(name):
        i = wstate["next"]
        assert wseq[i] == SLOTS[name], (i, name)
        while wstate["issued"] < min(len(wseq), i + RING - 2):
            j = wstate["issued"]
            rb = ring[j % RING]
            sid = wseq[j]
            P.dma("pool", lambda e, rb=rb, sid=sid: e.dma_start(out=rb.t[:], in_=wsl[sid]),
                  ("ring", j % RING), w=[rb])
            wstate["issued"] += 1
        wstate["next"] += 1
        return ring[i % RING]

    groups = []
    n = 0
    for gi in range(PRE // NMAX):
        groups.append(("pre", n, GB, None, gi == PRE // NMAX - 1))
        n += NMAX
    groups.append(("halo", n, 1, None, False))
    n += BLK
    for gi in range(TH // NMAX):
        groups.append(("main", n, GB, gi * NMAX, False))
        n += NMAX
    assert n == NTOT

    mem_names = [f"ck{c}" for c in range(8)] + [f"cv{c}" for c in range(8)]
    full_names = ([f"qa{c}" for c in range(4)] + ["ke0", "ko0", "ke1", "ko1"]
                  + [f"fh{h}" for h in range(4)] + [f"qh{h}" for h in range(4)] + [f"gh{h}" for h in range(4)]
                  + ["vd0", "vd1", "ih0", "ih1", "ih2", "ih3"]
                  + [f"wo{c}" for c in range(8)] + [f"cq{c}" for c in range(8)] + [f"co{c}" for c in range(8)])
    for j in range(NPAIR):
        full_names += [f"up{j}", f"up{NPAIR + j}"]
    full_names += [f"dn{c}_{s}" for c in range(8) for s in range(3)]
    pre_names = [f"fh{h}" for h in range(4)] + ["ih0", "ih1", "ih2", "ih3"]
    pre_last_names = ["ke0", "ko0", "ke1", "ko1"] + [f"fh{h}" for h in range(4)] + ["vd0", "vd1", "ih0", "ih1", "ih2", "ih3"]
    wseq.extend(SLOTS[nm] for nm in mem_names)
    for (kind, n0, nblk, ooff, last) in groups:
        if kind == "pre":
            wseq.extend(SLOTS[nm] for nm in (pre_last_names if last else pre_names))
        else:
            wseq.extend(SLOTS[nm] for nm in full_names)

    Xb = [P.mb([128, 8, NMAX], F32, f"X{i}") for i in range(2)]
    hT = P.mb([128, 8, NMAX], BF16, "hT")
    sq = P.mb([128, 8, NMAX], BF16, "sq")
    rstd = P.sb([128, NMAX], F32, "rstd")
    lnt = P.sb([128, NMAX], F32, "lnt")
    tmpn = [lnt, P.sb([128, NMAX], F32, "tmpn1")]
    S = [P.sb([128, 128], F32, f"S{h}") for h in range(4)]
    for h in range(4):
        P.op("pool", lambda e, h=h: e.memset(S[h].t[:], 0.0), w=[S[h]])
    kT = [[P.sb([128, BLK + NMAX], BF16, f"kT{g}_{par}") for par in range(2)] for g in range(2)]
    Vd = [P.sb([128, GB + 1, 128], BF16, f"Vd{g}") for g in range(2)]
    for g in range(2):
        for par in range(2):
            P.op("pool", lambda e, g=g, par=par: e.memset(kT[g][par].t[:], 0.0), w=[kT[g][par]])
        P.op("pool", lambda e, g=g: e.memset(Vd[g].t[:], 0.0), w=[Vd[g]])
    uh = P.sb([128, 44, 2], F32, "uh")
    P.op("pool", lambda e: e.memset(uh.t[:], 0.0), w=[uh])
    KmT = P.sb([128, 8, NMEM], BF16, "KmT")
    Vm = P.sb([128, 2, D], BF16, "Vm")

    qaT = [P.sb([128, NMAX], BF16, f"qaT{c}") for c in range(4)]
    attnT = P.mb([128, 4, NMAX], BF16, "attnT")
    recT = [P.sb([128, NMAX], BF16, f"recT{h}") for h in range(4)]
    th = [P.sb([128, NMAX], F32, f"th{h}") for h in range(4)]
    qs = [P.sb([128, NMAX], BF16, f"qs{h}") for h in range(4)]
    gs = [P.sb([128, NMAX], BF16, f"gs{h}") for h in range(4)]
    vtok = [P.sb([128, GB, 128], BF16, f"vtok{h}") for h in range(4)]
    gg = [P.sb([128, NMAX], F32, f"gg{i}") for i in range(2)]
    bc = [P.sb([128, NMAX], F32, f"bc{i}") for i in range(2)]
    kk = [P.sb([128, NMAX], F32, f"kk{i}") for i in range(2)]
    qt = [P.sb([128, NMAX], BF16, f"qt{h}") for h in range(4)]
    kt = [P.sb([128, NMAX], BF16, f"kt{h}") for h in range(4)]
    ktok = [P.sb([128, GB, 128], BF16, f"ktok{h}") for h in range(4)]
    Am = [P.sb([128, NMAX], BF16, f"Am{h}") for h in range(4)]
    cr = [P.sb([128, GB], F32, f"cr{h}") for h in range(4)]
    dS = [P.sb([128, GB], F32, f"dS{h}") for h in range(4)]
    dK = [P.sb([128, GB], F32, f"dK{h}") for h in range(4)]
    eC = [P.sb([128, GB], F32, f"eC{h}") for h in range(4)]
    Sb = [[P.sb([128, 128], BF16, f"Sb{h}_{i}") for i in range(2)] for h in range(4)]
    Usb = th
    osq = P.sb([128, NMAX], BF16, "osq")
    r1 = gg[1]
    Pp = [P.sb([128, 512], BF16, f"Pp{i}") for i in range(2)]
    Pc = [P.sb([128, 512], BF16, f"Pc{i}") for i in range(2)]
    rden = [P.sb([128, 512], F32, f"rden{i}") for i in range(2)]
    qcT = qt + kt
    _pt = qaT + Am
    PT = [[_pt[2 * h + m] for m in range(2)] for h in range(4)]
    ocT = recT + gs
    rdc = bc
    aT = (qaT + Am + qt + kt + recT + gs)[:NPAIR]
    ug = [P.sb([128, NMAX + 2], F32, f"ug{i}") for i in range(3)]
    uv = [P.sb([128, NMAX + 2], F32, f"uv{i}") for i in range(3)]
    cg = [th[0], th[1], gg[0]]
    cv = [th[2], th[3], gg[1]]

    def mm(out_b, out_ap, lhs_b, lhs_ap, rhs_b, rhs_ap, start, stop):
        P.op("pe", lambda e: e.matmul(out_ap, lhs_ap, rhs_ap, start=start, stop=stop),
             r=[lhs_b, rhs_b], w=[out_b])

    def rstd_from(psb, N, dim):
        P.op("act", lambda e: e.activation(out=lnt.t[:, :N], in_=psb.t[:, :N], func=AF.Ln, bias=EPS,
                                           scale=1.0 / dim), r=[psb], w=[lnt])
        P.op("act", lambda e: e.activation(out=rstd.t[:, :N], in_=lnt.t[:, :N], func=AF.Exp, scale=-0.5),
             r=[lnt], w=[rstd])

    def prenorm(X, N, gcol):
        P.op("act", lambda e: e.activation(out=sq.t[:, :, :N], in_=X.t[:, :, :N], func=AF.Square), r=X.c, w=sq.c)
        pb = bank()
        for c in range(8):
            mm(pb, pb.t[:, :N], ones, ones.t[:], sq.c[c], sq.t[:, c, :N], c == 0, c == 7)
        rstd_from(pb, N, D)
        for c in range(8):
            P.op("dve", lambda e, c=c: e.scalar_tensor_tensor(hT.t[:, c, :N], X.t[:, c, :N],
                                                              vecs.t[:, gcol + c:gcol + c + 1], rstd.t[:, :N],
                                                              ALU.mult, ALU.mult), r=[X.c[c], vecs, rstd], w=[hT.c[c]])

    def proj_fm(name, rhs_list, N, kchunks=8):
        wb = wnext(name)
        pb = bank()
        for k in range(kchunks):
            rb, rap = rhs_list[k]
            mm(pb, pb.t[:, :N], wb, wb.t[:, k * 128:(k + 1) * 128], rb, rap, k == 0, k == kchunks - 1)
        return pb

    def postnorm_residual(X, N, gcol, pbs):
        for c in range(8):
            pb = pbs[c]()
            P.op("act", lambda e, c=c, pb=pb: e.activation(out=stage.t[:, c, :N], in_=pb.t[:, :N], func=AF.Copy),
                 r=[pb], w=[stage.c[c]])
            P.op("act", lambda e, c=c, pb=pb: e.activation(out=sq.t[:, c, :N], in_=pb.t[:, :N], func=AF.Square),
                 r=[pb], w=[sq.c[c]])
        pb2 = bank()
        for c in range(8):
            mm(pb2, pb2.t[:, :N], ones, ones.t[:], sq.c[c], sq.t[:, c, :N], c == 0, c == 7)
        rstd_from(pb2, N, D)
        for c in range(8):
            eng = "pool" if c % 2 == 1 else "dve"
            P.op(eng, lambda e, c=c: e.tensor_tensor(stage.t[:, c, :N], stage.t[:, c, :N], rstd.t[:, :N], ALU.mult),
                 r=[stage.c[c], rstd], w=[stage.c[c]])
        for c in range(8):
            P.op("dve", lambda e, c=c: e.scalar_tensor_tensor(X.t[:, c, :N], stage.t[:, c, :N],
                                                              vecs.t[:, gcol + c:gcol + c + 1], X.t[:, c, :N],
                                                              ALU.mult, ALU.add),
                 r=[stage.c[c], vecs, X.c[c]], w=[X.c[c]])

    hT_chunks = lambda N: [(hT.c[k], hT.t[:, k, :N]) for k in range(8)]

    def do_memkv():
        memX = stage
        P.dma("sp", lambda e: e.dma_start(out=memX.t[:, :, :NMEM], in_=memT.rearrange("c p n -> p c n")),
              "memx", w=memX.c)
        prenorm(memX, NMEM, V_MEMN)
        for c in range(8):
            pb = proj_fm(f"ck{c}", hT_chunks(NMEM), NMEM)
            P.op("act", lambda e, c=c, pb=pb: e.activation(out=KmT.t[:, c, :], in_=pb.t[:, :NMEM], func=AF.Copy),
                 r=[pb], w=[KmT])
        for c in range(8):
            wb = wnext(f"cv{c}")
            pb = bank()
            for mh in range(2):
                for k in range(8):
                    mm(pb, pb.t[:, mh * 128:(mh + 1) * 128], hT.c[k], hT.t[:, k, mh * 128:(mh + 1) * 128],
                       wb, wb.t[:, k * 128:(k + 1) * 128], k == 0, k == 7)
            P.op("act", lambda e, c=c, pb=pb: e.activation(
                out=Vm.t[:, :, c * 128:(c + 1) * 128],
                in_=pb.t[:, 0:256].rearrange("p (m d) -> p m d", m=2), func=AF.Copy), r=[pb], w=[Vm])

    def load_x(gi):
        kind, n0, nblk, ooff, last = groups[gi]
        N = nblk * BLK
        X = Xb[gi % 2]
        P.dma("sp", lambda e: e.dma_start(out=X.t[:, :, :N], in_=xT[:, :, n0:n0 + N].rearrange("c p n -> p c n")),
              ("x", gi % 2), w=X.c)

    class _V:
        def __init__(self, t, k):
            self.t3, self.k = t, k

        def __getitem__(self, idx):
            p, f = idx
            return self.t3[p, self.k, f]

    hsets = [(gg[0], gg[0].t, bc[0], bc[0].t, kk[0], kk[0].t),
             (gg[1], gg[1].t, bc[1], bc[1].t, kk[1], kk[1].t),
             (stage.c[0], _V(stage.t, 0), stage.c[1], _V(stage.t, 1), stage.c[2], _V(stage.t, 2)),
             (stage.c[3], _V(stage.t, 3), stage.c[4], _V(stage.t, 4), stage.c[5], _V(stage.t, 5))]

    def hgrn_prep_all(N, nblk, full):
        HS = range(4)
        gB = [hsets[h][0] for h in HS]; gT = [hsets[h][1] for h in HS]
        bB = [hsets[h][2] for h in HS]; bT = [hsets[h][3] for h in HS]
        kB = [hsets[h][4] for h in HS]; kT_ = [hsets[h][5] for h in HS]
        bc3 = [bT[h][:, :N].rearrange("p (b t) -> p b t", t=BLK) for h in HS]
        for h in HS:
            P.op("act", lambda e, h=h: e.activation(out=gT[h][:, :N], in_=th[h].t[:, :N], func=AF.Ln,
                                                    bias=lbA.t[:, h:h + 1], scale=lbB.t[:, h:h + 1]),
                 r=[th[h], lbA, lbB], w=[gB[h]])
        for h in HS:
            P.op("dve", lambda e, h=h: e.tensor_tensor_scan(bT[h][:, :N], rmask.t[:, :N], gT[h][:, :N], 0.0,
                                                            ALU.mult, ALU.add), r=[rmask, gB[h]], w=[bB[h]])
        for h in HS:
            P.op("act", lambda e, h=h: e.activation(out=kT_[h][:, :N], in_=th[h].t[:, :N], func=AF.Identity,
                                                    bias=lbB.t[:, h:h + 1], scale=lbnB.t[:, h:h + 1]),
                 r=[th[h], lbnB, lbB], w=[kB[h]])
        for h in HS:
            P.op("pool", lambda e, h=h: e.tensor_copy(cr[h].t[:, :nblk], bc3[h][:, :, 63]), r=[bB[h]], w=[cr[h]])
            P.op("act", lambda e, h=h: e.activation(out=dS[h].t[:, :nblk], in_=bc3[h][:, :, 127], func=AF.Exp),
                 r=[bB[h]], w=[dS[h]])
        for h in HS:
            P.op("dve", lambda e, h=h: e.tensor_tensor(bc3[h], bc3[h],
                                                       cr[h].t[:, :nblk].rearrange("p (b o) -> p b o", o=1)
                                                       .to_broadcast([128, nblk, BLK]), ALU.subtract),
                 r=[bB[h], cr[h]], w=[bB[h]])
        for h in HS:
            P.op("act", lambda e, h=h: e.activation(out=gT[h][:, :N], in_=bT[h][:, :N], func=AF.Exp, scale=-1.0),
                 r=[bB[h]], w=[gB[h]])
        for h in HS:
            P.op("dve", lambda e, h=h: e.tensor_tensor(kt[h].t[:, :N], kT_[h][:, :N], gT[h][:, :N], ALU.mult),
                 r=[kB[h], gB[h]], w=[kt[h]])
        for h in HS:
            P.op("act", lambda e, h=h: e.activation(out=dK[h].t[:, :nblk], in_=bc3[h][:, :, 127], func=AF.Exp),
                 r=[bB[h]], w=[dK[h]])
            if full:
                P.op("act", lambda e, h=h: e.activation(out=eC[h].t[:, :nblk], in_=cr[h].t[:, :nblk], func=AF.Exp),
                     r=[cr[h]], w=[eC[h]])
        if full:
            for h in HS:
                P.op("act", lambda e, h=h: e.activation(out=kT_[h][:, :N], in_=bT[h][:, :N], func=AF.Exp),
                     r=[bB[h]], w=[kB[h]])
            for h in HS:
                P.op("dve", lambda e, h=h: e.scalar_tensor_tensor(qt[h].t[:, :N], qs[h].t[:, :N],
                                                                  float(128 ** -0.5), kT_[h][:, :N],
                                                                  ALU.mult, ALU.mult),
                     r=[qs[h], kB[h]], w=[qt[h]])

    def hgrn_core(N, nblk, full):
        for h in range(4):
            tb = tbk()
            for b in range(nblk):
                P.op("pe", lambda e, tb=tb, h=h, b=b: e.transpose(tb.t[:, b * 128:(b + 1) * 128],
                                                                   kt[h].t[:, b * 128:(b + 1) * 128], ident.t[:]),
                     r=[kt[h], ident], w=[tb])
            P.op("act", lambda e, tb=tb, h=h: e.activation(
                out=ktok[h].t[:, :nblk, :], in_=tb.t[:, :N].rearrange("p (b f) -> p b f", f=128), func=AF.Copy),
                r=[tb], w=[ktok[h]])
        for h in range(4):
            ub = bank()
            for b in range(nblk):
                mm(ub, ub.t[:, b * 128:(b + 1) * 128], ktok[h], ktok[h].t[:, b, :], vtok[h], vtok[h].t[:, b, :],
                   True, True)
            P.op("act", lambda e, ub=ub, h=h: e.activation(out=Usb[h].t[:, :N], in_=ub.t[:, :N], func=AF.Copy),
                 r=[ub], w=[Usb[h]])
        if full:
            for h in range(4):
                ab = bank()
                for b in range(nblk):
                    mm(ab, ab.t[:, b * 128:(b + 1) * 128], kt[h], kt[h].t[:, b * 128:(b + 1) * 128],
                       qt[h], qt[h].t[:, b * 128:(b + 1) * 128], True, True)
                P.op("dve", lambda e, ab=ab, h=h: e.tensor_tensor(Am[h].t[:, :N], ab.t[:, :N], cmask.t[:, :N], ALU.mult),
                     r=[ab, cmask], w=[Am[h]])

        def s_update(h, b):
            P.op("act", lambda e: e.activation(out=S[h].t[:], in_=S[h].t[:], func=AF.Identity,
                                               scale=dS[h].t[:, b:b + 1]), r=[S[h], dS[h]], w=[S[h]])
            P.op("dve", lambda e: e.scalar_tensor_tensor(S[h].t[:], Usb[h].t[:, b * 128:(b + 1) * 128],
                                                         dK[h].t[:, b:b + 1], S[h].t[:], ALU.mult, ALU.add),
                 r=[Usb[h], dK[h], S[h]], w=[S[h]])

        if not full:
            for b in range(nblk):
                for h in range(4):
                    s_update(h, b)
            return
        for hp in range(2):
            hs = (2 * hp, 2 * hp + 1)
            ob = {h: lbank[i] for i, h in enumerate(hs)}
            for b in range(nblk):
                for h in hs:
                    sb_ = Sb[h][b % 2]
                    P.op("dve", lambda e, h=h, b=b, sb_=sb_: e.tensor_scalar(sb_.t[:], S[h].t[:], eC[h].t[:, b:b + 1],
                                                                             None, ALU.mult),
                         r=[S[h], eC[h]], w=[sb_])
                    o_b = ob[h]
                    mm(o_b, o_b.t[:, b * 128:(b + 1) * 128], vtok[h], vtok[h].t[:, b, :],
                       Am[h], Am[h].t[:, b * 128:(b + 1) * 128], True, False)
                    mm(o_b, o_b.t[:, b * 128:(b + 1) * 128], sb_, sb_.t[:], qt[h], qt[h].t[:, b * 128:(b + 1) * 128],
                       False, True)
                    s_update(h, b)
            for h in hs:
                o_b = ob[h]
                P.op("act", lambda e, o_b=o_b: e.activation(out=osq.t[:, :N], in_=o_b.t[:, :N], func=AF.Square),
                     r=[o_b], w=[osq])
                pb = bank()
                mm(pb, pb.t[:, :N], ones, ones.t[:], osq, osq.t[:, :N], True, True)
                rstd_from(pb, N, 128)
                P.op("dve", lambda e, o_b=o_b: e.tensor_tensor(r1.t[:, :N], o_b.t[:, :N], rstd.t[:, :N], ALU.mult),
                     r=[o_b, rstd], w=[r1])
                P.op("dve", lambda e, h=h: e.scalar_tensor_tensor(recT[h].t[:, :N], r1.t[:, :N],
                                                                  vecs.t[:, V_ONW:V_ONW + 1], gs[h].t[:, :N],
                                                                  ALU.mult, ALU.mult),
                     r=[r1, vecs, gs[h]], w=[recT[h]])

    def tokmajor(names, dsts, N, nblk):
        wbs = [wnext(nm) for nm in names]
        pbs = [bank() for _ in names]
        for b in range(nblk):
            for k in range(8):
                for wb, pb in zip(wbs, pbs):
                    mm(pb, pb.t[:, b * 128:(b + 1) * 128], hT.c[k], hT.t[:, k, b * 128:(b + 1) * 128],
                       wb, wb.t[:, k * 128:(k + 1) * 128], k == 0, k == 7)
        for pb, (db, dap) in zip(pbs, dsts):
            P.op("act", lambda e, pb=pb, dap=dap: e.activation(
                out=dap, in_=pb.t[:, :N].rearrange("p (b f) -> p b f", f=128), func=AF.Copy), r=[pb], w=[db])

    def roll_kv(nblk):
        for g in range(2):
            for par in range(2):
                P.op("pool", lambda e, g=g, par=par: e.tensor_copy(kT[g][par].t[:, 0:BLK],
                                                                    kT[g][par].t[:, nblk * BLK:(nblk + 1) * BLK]),
                     r=[kT[g][par]], w=[kT[g][par]])
            P.op("pool", lambda e, g=g: e.tensor_copy(Vd[g].t[:, 0, :], Vd[g].t[:, nblk, :]),
                 r=[Vd[g]], w=[Vd[g]])

    def do_prefix(gi):
        kind, n0, nblk, ooff, last = groups[gi]
        N = nblk * BLK
        X = Xb[gi % 2]
        prenorm(X, N, V_MIXPRE)
        if last:
            for g in range(2):
                for par, nm in enumerate(("ke", "ko")):
                    pb = proj_fm(f"{nm}{g}", hT_chunks(N), N)
                    P.op("act", lambda e, g=g, par=par, pb=pb: e.activation(
                        out=kT[g][par].t[:, BLK:BLK + N], in_=pb.t[:, :N], func=AF.Copy), r=[pb], w=[kT[g][par]])
        for h in range(4):
            pb = proj_fm(f"fh{h}", hT_chunks(N), N)
            P.op("act", lambda e, h=h, pb=pb: e.activation(out=th[h].t[:, :N], in_=pb.t[:, :N], func=AF.Tanh,
                                                           scale=0.5), r=[pb], w=[th[h]])
        if last:
            tokmajor(["vd0", "vd1", "ih0"], [(Vd[0], Vd[0].t[:, 1:1 + nblk, :]), (Vd[1], Vd[1].t[:, 1:1 + nblk, :]),
                                             (vtok[0], vtok[0].t[:, :nblk, :])], N, nblk)
            tokmajor(["ih1", "ih2", "ih3"], [(vtok[h], vtok[h].t[:, :nblk, :]) for h in (1, 2, 3)], N, nblk)
        else:
            tokmajor(["ih0", "ih1"], [(vtok[h], vtok[h].t[:, :nblk, :]) for h in (0, 1)], N, nblk)
            tokmajor(["ih2", "ih3"], [(vtok[h], vtok[h].t[:, :nblk, :]) for h in (2, 3)], N, nblk)
        hgrn_prep_all(N, nblk, False)
        hgrn_core(N, nblk, False)
        if last:
            roll_kv(nblk)

    def do_full(gi):
        kind, n0, nblk, ooff, last = groups[gi]
        N = nblk * BLK
        X = Xb[gi % 2]
        first_main = (kind == "main" and ooff == 0)
        prenorm(X, N, V_MIXPRE)
        for c in range(4):
            pb = proj_fm(f"qa{c}", hT_chunks(N), N)
            P.op("dve", lambda e, c=c, pb=pb: e.tensor_copy(qaT[c].t[:, :N], pb.t[:, :N]), r=[pb], w=[qaT[c]])
        for g in range(2):
            for par, nm in enumerate(("ke", "ko")):
                pb = proj_fm(f"{nm}{g}", hT_chunks(N), N)
                P.op("dve", lambda e, g=g, par=par, pb=pb: e.tensor_copy(kT[g][par].t[:, BLK:BLK + N], pb.t[:, :N]),
                     r=[pb], w=[kT[g][par]])
        for h in range(4):
            pb = proj_fm(f"fh{h}", hT_chunks(N), N)
            P.op("act", lambda e, h=h, pb=pb: e.activation(out=th[h].t[:, :N], in_=pb.t[:, :N], func=AF.Tanh,
                                                           scale=0.5), r=[pb], w=[th[h]])
        for h in range(4):
            pb = proj_fm(f"qh{h}", hT_chunks(N), N)
            P.op("act", lambda e, h=h, pb=pb: e.activation(out=qs[h].t[:, :N], in_=pb.t[:, :N], func=AF.Silu),
                 r=[pb], w=[qs[h]])
        for h in range(4):
            pb = proj_fm(f"gh{h}", hT_chunks(N), N)
            P.op("act", lambda e, h=h, pb=pb: e.activation(out=gs[h].t[:, :N], in_=pb.t[:, :N], func=AF.Silu),
                 r=[pb], w=[gs[h]])
        tokmajor(["vd0", "vd1", "ih0"], [(Vd[0], Vd[0].t[:, 1:1 + nblk, :]), (Vd[1], Vd[1].t[:, 1:1 + nblk, :]),
                                         (vtok[0], vtok[0].t[:, :nblk, :])], N, nblk)
        tokmajor(["ih1", "ih2", "ih3"], [(vtok[h], vtok[h].t[:, :nblk, :]) for h in (1, 2, 3)], N, nblk)
        hgrn_prep_all(N, nblk, True)
        for b in range(nblk):
            for g in range(2):
                i2 = (b * 2 + g) % 2
                sp_, sc_ = bank(), bank()
                mprev = maskp0 if (first_main and b == 0) else maskp
                mm(sp_, sp_.t[:, :], ident, ident.t[:], mprev, mprev.t[:], True, False)
                mm(sc_, sc_.t[:, :], ident, ident.t[:], maskc, maskc.t[:], True, False)
                for j in range(4):
                    hq = 4 * g + j
                    c = hq // 2
                    kb = kT[g][j % 2]
                    mm(sp_, sp_.t[:, j * 128:(j + 1) * 128], kb, kb.t[:, b * 128:(b + 1) * 128],
                       qaT[c], qaT[c].t[:, b * 128:(b + 1) * 128], False, j == 3)
                    mm(sc_, sc_.t[:, j * 128:(j + 1) * 128], kb, kb.t[:, (b + 1) * 128:(b + 2) * 128],
                       qaT[c], qaT[c].t[:, b * 128:(b + 1) * 128], False, j == 3)
                pp, pc, rd = Pp[i2], Pc[i2], rden[i2]
                P.op("act", lambda e, sp_=sp_, pp=pp: e.activation(out=pp.t[:], in_=sp_.t[:], func=AF.Exp, scale=0.125),
                     r=[sp_], w=[pp])
                P.op("act", lambda e, sc_=sc_, pc=pc: e.activation(out=pc.t[:], in_=sc_.t[:], func=AF.Exp, scale=0.125),
                     r=[sc_], w=[pc])
                ob_, db_ = bank(), bank()
                mm(ob_, ob_.t[:], Vd[g], Vd[g].t[:, b, :], pp, pp.t[:], True, False)
                mm(ob_, ob_.t[:], Vd[g], Vd[g].t[:, b + 1, :], pc, pc.t[:], False, True)
                mm(db_, db_.t[:], ones, ones.t[:], pp, pp.t[:], True, False)
                mm(db_, db_.t[:], ones, ones.t[:], pc, pc.t[:], False, False)
                mm(db_, db_.t[:], ones, ones.t[0:1, :], esink, esink.t[0:1, g * 512:(g + 1) * 512], False, True)
                P.op("act", lambda e, db_=db_, rd=rd: e.activation(out=rd.t[:], in_=db_.t[:], func=AF.Ln),
                     r=[db_], w=[rd])
                P.op("act", lambda e, rd=rd: e.activation(out=rd.t[:], in_=rd.t[:], func=AF.Exp, scale=-1.0),
                     r=[rd], w=[rd])
                for half in range(2):
                    p0 = half * 64
                    P.op("dve", lambda e, half=half, p0=p0, ob_=ob_, rd=rd, g=g, b=b: e.tensor_tensor(
                        attnT.t[p0:p0 + 64, 2 * g:2 * g + 2, b * 128:(b + 1) * 128],
                        ob_.t[p0:p0 + 64, :].rearrange("p (i two q) -> p i two q", two=2, q=128)[:, :, half, :],
                        rd.t[p0:p0 + 64, :].rearrange("p (i two q) -> p i two q", two=2, q=128)[:, :, half, :],
                        ALU.mult), r=[ob_, rd], w=[attnT.c[2 * g], attnT.c[2 * g + 1]])
        roll_kv(nblk)
        hgrn_core(N, nblk, True)
        mixk = [(attnT.c[c], attnT.t[:, c, :N]) for c in range(4)] + [(recT[h], recT[h].t[:, :N]) for h in range(4)]
        postnorm_residual(X, N, V_MIXPOST, [lambda c=c: proj_fm(f"wo{c}", mixk, N) for c in range(8)])

        if 'ca' not in STAGES:
            for c in range(8):
                wnext(f"cq{c}")
            for c in range(8):
                wnext(f"co{c}")
        else:
            do_ca(X, N)
        do_ffn(gi, X, N, kind, ooff, first_main)

    def do_ca(X, N):
        prenorm(X, N, V_CAPRE)
        for c in range(8):
            pb = proj_fm(f"cq{c}", hT_chunks(N), N)
            P.op("dve", lambda e, c=c, pb=pb: e.tensor_copy(qcT[c].t[:, :N], pb.t[:, :N]), r=[pb], w=[qcT[c]])
        for hh in range(4):
            for mh in range(2):
                pb = bank()
                for i, dc in enumerate((2 * hh, 2 * hh + 1)):
                    mm(pb, pb.t[:, :N], KmT, KmT.t[:, dc, mh * 128:(mh + 1) * 128], qcT[dc], qcT[dc].t[:, :N],
                       i == 0, i == 1)
                P.op("act", lambda e, hh=hh, mh=mh, pb=pb: e.activation(out=PT[hh][mh].t[:, :N], in_=pb.t[:, :N],
                                                                        func=AF.Exp, scale=1.0 / 16.0),
                     r=[pb], w=[PT[hh][mh]])
            db_ = bank()
            for mh in range(2):
                mm(db_, db_.t[:, :N], ones, ones.t[:], PT[hh][mh], PT[hh][mh].t[:, :N], mh == 0, mh == 1)
            rd = rdc[hh % 2]
            P.op("act", lambda e, db_=db_, rd=rd: e.activation(out=rd.t[:, :N], in_=db_.t[:, :N], func=AF.Ln),
                 r=[db_], w=[rd])
            P.op("act", lambda e, rd=rd: e.activation(out=rd.t[:, :N], in_=rd.t[:, :N], func=AF.Exp, scale=-1.0),
                 r=[rd], w=[rd])
            for dc in (2 * hh, 2 * hh + 1):
                pb = bank()
                for mh in range(2):
                    mm(pb, pb.t[:, :N], Vm, Vm.t[:, mh, dc * 128:(dc + 1) * 128], PT[hh][mh], PT[hh][mh].t[:, :N],
                       mh == 0, mh == 1)
                P.op("dve", lambda e, dc=dc, pb=pb, rd=rd: e.tensor_tensor(ocT[dc].t[:, :N], pb.t[:, :N], rd.t[:, :N],
                                                                           ALU.mult), r=[pb, rd], w=[ocT[dc]])
        ock = [(ocT[c], ocT[c].t[:, :N]) for c in range(8)]
        postnorm_residual(X, N, V_CAPOST, [lambda c=c: proj_fm(f"co{c}", ock, N) for c in range(8)])

    def do_ffn(gi, X, N, kind, ooff, first_main):
        if 'ffn' not in STAGES:
            for j in range(NPAIR):
                wnext(f"up{j}")
                wnext(f"up{NPAIR + j}")
            for c in range(8):
                for s_ in range(3):
                    wnext(f"dn{c}_{s_}")
            if kind == "main":
                ev = P.dma("sp", lambda e: e.dma_start(out=outT[:, :, ooff:ooff + N].rearrange("c p n -> p c n"),
                                                       in_=X.t[:, :, :N]), ("xo", gi % 2), r=X.c)
                P.finals.append(ev)
            return
        prenorm(X, N, V_FFNPRE)
        for j in range(NPAIR):
            i2 = j % 3
            outs = []
            for (ci, ub, cb) in ((j, ug[i2], cg[i2]), (NPAIR + j, uv[i2], cv[i2])):
                pb = proj_fm(f"up{ci}", hT_chunks(N), N)
                P.op("act", lambda e, ci=ci, pb=pb, cb=cb: e.activation(
                    out=cb.t[:, :N], in_=pb.t[:, :N], func=AF.Identity,
                    bias=vecs.t[:, V_CB + ci:V_CB + ci + 1], scale=vecs.t[:, V_CW + 88 + ci:V_CW + 88 + ci + 1]),
                    r=[pb, vecs], w=[cb])
                P.op("act", lambda e, pb=pb, ub=ub: e.activation(out=ub.t[:, 2:2 + N], in_=pb.t[:, :N], func=AF.Copy),
                     r=[pb], w=[ub])
                if first_main:
                    P.op("pool", lambda e, ci=ci, ub=ub: e.tensor_scalar(ub.t[:, 0:2], uh.t[:, ci, :], hflag.t[:, 0:1],
                                                                         None, ALU.mult), r=[uh, hflag], w=[ub])
                else:
                    P.op("pool", lambda e, ci=ci, ub=ub: e.tensor_copy(ub.t[:, 0:2], uh.t[:, ci, :]), r=[uh], w=[ub])
                P.op("pool", lambda e, ci=ci, ub=ub: e.tensor_copy(uh.t[:, ci, :], ub.t[:, N:N + 2]), r=[ub], w=[uh])
                P.op("dve", lambda e, ci=ci, ub=ub, cb=cb: e.scalar_tensor_tensor(
                    cb.t[:, :N], ub.t[:, 0:N], vecs.t[:, V_CW + ci:V_CW + ci + 1], cb.t[:, :N], ALU.mult, ALU.add),
                    r=[ub, vecs, cb], w=[cb])
                P.op("dve", lambda e, ci=ci, ub=ub, cb=cb: e.scalar_tensor_tensor(
                    cb.t[:, :N], ub.t[:, 1:N + 1], vecs.t[:, V_CW + 44 + ci:V_CW + 44 + ci + 1], cb.t[:, :N],
                    ALU.mult, ALU.add), r=[ub, vecs, cb], w=[cb])
            if kind == "main":
                P.op("act", lambda e, i2=i2: e.activation(out=cg[i2].t[:, :N], in_=cg[i2].t[:, :N],
                                                          func=AF.Gelu_apprx_tanh), r=[cg[i2]], w=[cg[i2]])
                P.op("dve", lambda e, i2=i2, j=j: e.tensor_tensor(aT[j].t[:, :N], cg[i2].t[:, :N], cv[i2].t[:, :N],
                                                                  ALU.mult), r=[cg[i2], cv[i2]], w=[aT[j]])
        if kind == "main":
            def dn_chunk(c):
                pb = bank()
                for s in range(3):
                    wb = wnext(f"dn{c}_{s}")
                    for kq in range(8):
                        jj = s * 8 + kq
                        if jj >= NPAIR:
                            break
                        mm(pb, pb.t[:, :N], wb, wb.t[:, kq * 128:(kq + 1) * 128], aT[jj], aT[jj].t[:, :N],
                           jj == 0, jj == NPAIR - 1)
                return pb
            postnorm_residual(X, N, V_FFNPOST, [lambda c=c: dn_chunk(c) for c in range(8)])
            ev = P.dma("sp", lambda e: e.dma_start(out=outT[:, :, ooff:ooff + N].rearrange("c p n -> p c n"),
                                                   in_=X.t[:, :, :N]), ("xo", gi % 2), r=X.c)
            P.finals.append(ev)
        else:
            for c in range(8):
                for s in range(3):
                    wnext(f"dn{c}_{s}")

    try:
        ckpt('consts')
        do_memkv()
        ckpt('memkv')
        load_x(0)
        for gi in range(len(groups)):
            if gi + 1 < len(groups):
                load_x(gi + 1)
            if groups[gi][0] == "pre":
                do_prefix(gi)
                ckpt('prefix')
            else:
                do_full(gi)
        assert wstate["next"] == len(wseq), (wstate, len(wseq))
    except StopBuild:
        pass
    P.emit()
    es.close()
    global _LAST
    _LAST = dict(P=P, attnT=attnT, recT=recT, X=Xb, hT=hT, qaT=qaT, kT=kT, Vd=Vd, th=th, qs=qs, gs=gs, qt=qt, kt=kt, S=S, vtok=vtok, stage=stage)
    return nc, P


def run_cores(inputs, PRE=None, trace=False):
    x = np.asarray(inputs["x"], np.float32)
    mem = np.asarray(inputs["mem"], np.float32)
    B, T, _ = x.shape
    TH = T // 2
    if PRE is None:
        PRE = TH
    ws = build_wslots(inputs)
    vecs = build_vecs(inputs)
    sinks = np.asarray(inputs["attn_sinks"][0], np.float32)
    csts = [build_cst(0, sinks), build_cst(1, sinks)]
    nc, P = build_program(TH, PRE)
    in_maps = []
    for c in range(2 * B):
        b, half = c // 2, c % 2
        start = half * TH
        lo = start - PRE - BLK
        NTOT = PRE + BLK + TH
        xs = np.zeros((NTOT, D), np.float32)
        src_lo = max(lo, 0)
        xs[src_lo - lo:] = x[b, src_lo:start + TH]
        xT = np.ascontiguousarray(xs.T).reshape(8, 128, NTOT)
        memT = np.ascontiguousarray(mem[b].T).reshape(8, 128, NMEM)
        in_maps.append({"xT": xT, "memT": memT, "wslots": ws, "vecs": vecs, "cst": csts[half]})
    res = run_bass_kernel_spmd(nc, in_maps, core_ids=list(range(2 * B)), trace=trace)
    out = np.zeros((B, T, D), np.float32)
    for c in range(2 * B):
        b, half = c // 2, c % 2
        o = np.asarray(res.results[c]["outT"]).reshape(D, TH)
        out[b, half * TH:(half + 1) * TH] = o.T
    return out, res


def kernel(**inputs):
    out, _ = run_cores(inputs)
    return out
```

```python
import numpy as np
from contextlib import ExitStack
import concourse.bass as bass
import concourse.mybir as mybir
from concourse.bass_utils import run_bass_kernel_spmd

F32 = mybir.dt.float32
BF16 = mybir.dt.bfloat16
AF = mybir.ActivationFunctionType
ALU = mybir.AluOpType

D = 1024
NMEM = 256
DFF = 2816
NPAIR = 22
EPS = 1e-6
NEG = -30000.0
GB = 4
BLK = 128
STAGES = ('mixer', 'ca', 'ffn')
STOPAT = None


class StopBuild(Exception):
    pass


def ckpt(name):
    if STOPAT == name:
        raise StopBuild(name)
RING = 12


class Buf:
    __slots__ = ("t", "w", "r", "name")

    def __init__(self, t, name=None):
        self.t = t
        self.w = None
        self.r = {}
        self.name = name


class MB:
    def __init__(self, t, n, name):
        self.t = t
        self.c = [Buf(t, f"{name}.{i}") for i in range(n)]
        self.name = name


class Prog:
    ENGS = ("pe", "act", "dve", "pool", "sp")

    def __init__(self, nc, es):
        self.nc, self.es = nc, es
        self.items = {e: [] for e in self.ENGS}
        self.clk = {e: {} for e in self.ENGS}
        self.cnt = {e: 0 for e in self.ENGS}
        self.dcnt = {}
        self.nbuf = 0
        self.finals = []

    def sb(self, shape, dt, name=None):
        self.nbuf += 1
        nm = f"{name or 'b'}_{self.nbuf}"
        t = self.es.enter_context(self.nc.sbuf_tensor(nm, list(shape), dt))
        return Buf(t, nm)

    def mb(self, shape, dt, name):
        b = self.sb(shape, dt, name)
        return MB(b.t, shape[1], b.name)

    def ps(self, shape, dt, name=None):
        self.nbuf += 1
        nm = f"{name or 'p'}_{self.nbuf}"
        t = self.es.enter_context(self.nc.psum_tensor(nm, list(shape), dt))
        return Buf(t, nm)

    def _resolve(self, eng, r, w):
        clk = self.clk[eng]
        need = {}

        def add(kind, ev):
            key, val, snap = ev
            if key == eng and eng == "pe":
                return
            if clk.get(key, 0) >= val:
                return
            if key not in need or need[key][0] < val:
                need[key] = (val, snap)

        for b in r:
            if b.w is not None:
                add("raw", b.w)
        for b in w:
            if b.w is not None:
                add("waw", b.w)
            for ev in b.r.values():
                add("war", ev)
        for key, (val, snap) in need.items():
            if clk.get(key, 0) >= val:
                continue
            self.items[eng].append(("wait", key, val))
            for k2, v2 in snap.items():
                if clk.get(k2, 0) < v2:
                    clk[k2] = v2

    def _commit(self, ev, r, w):
        key = ev[0]
        for b in r:
            old = b.r.get(key)
            if old is None or old[1] < ev[1]:
                b.r[key] = ev
        for b in w:
            b.w = ev
            b.r = {}

    def op(self, eng, fn, r=(), w=()):
        self._resolve(eng, r, w)
        self.cnt[eng] += 1
        val = self.cnt[eng]
        snap = dict(self.clk[eng])
        snap[eng] = val
        ev = (eng, val, snap)
        self.items[eng].append(("op", fn, eng, 1))
        self._commit(ev, r, w)
        return ev

    def dma(self, q, fn, semkey, r=(), w=()):
        self._resolve(q, r, w)
        self.dcnt[semkey] = self.dcnt.get(semkey, 0) + 16
        val = self.dcnt[semkey]
        snap = dict(self.clk[q])
        snap[semkey] = val
        ev = (semkey, val, snap)
        self.items[q].append(("op", fn, semkey, 16))
        self._commit(ev, r, w)
        return ev

    def emit(self):
        nc, es = self.nc, self.es
        for ev in self.finals:
            if self.clk["sp"].get(ev[0], 0) < ev[1]:
                self.items["sp"].append(("wait", ev[0], ev[1]))
                self.clk["sp"][ev[0]] = ev[1]
        keys = []
        for e in self.ENGS:
            for it in self.items[e]:
                k = it[1] if it[0] == "wait" else it[2]
                if k not in keys:
                    keys.append(k)
        sems = {k: es.enter_context(nc.semaphore(f"sem{i}")) for i, k in enumerate(keys)}
        self.nsem = len(sems)
        block = es.enter_context(nc.Block())
        items = self.items

        def runner(name):
            def f(e):
                pend = []
                for it in items[name]:
                    if it[0] == "wait":
                        pend.append(it)
                        continue
                    is_dma = it[3] == 16
                    attach = None
                    if pend and not is_dma:
                        attach = pend.pop()
                    for w_ in pend:
                        e.wait_ge(sems[w_[1]], w_[2])
                    pend = []
                    ins = it[1](e)
                    if attach is not None:
                        ins = ins._wait_ge(sems[attach[1]], attach[2])
                    ins.then_inc(sems[it[2]], it[3])
                for w_ in pend:
                    e.wait_ge(sems[w_[1]], w_[2])
            return f

        block.tensor(runner("pe"))
        block.scalar(runner("act"))
        block.vector(runner("dve"))
        block.gpsimd(runner("pool"))
        block.sync(runner("sp"))


def slot_table():
    names = []
    names += [f"qa{c}" for c in range(4)]
    names += [f"ke{g}" for g in range(2)]
    names += [f"ko{g}" for g in range(2)]
    names += [f"vd{g}" for g in range(2)]
    for nm in ("qh", "fh", "ih", "gh"):
        names += [f"{nm}{h}" for h in range(4)]
    names += [f"wo{c}" for c in range(8)]
    names += [f"cq{c}" for c in range(8)]
    names += [f"co{c}" for c in range(8)]
    names += [f"up{c}" for c in range(44)]
    names += [f"dn{c}_{s}" for c in range(8) for s in range(3)]
    names += [f"ck{c}" for c in range(8)]
    names += [f"cv{c}" for c in range(8)]
    return {n: i for i, n in enumerate(names)}


SLOTS = slot_table()
NSLOT = len(SLOTS)


def _tile_cols(W, cols):
    K = W.shape[0]
    kc = K // 128
    out = np.zeros((128, 1024), np.float32)
    sub = W[:, cols]
    out[:, : kc * 128] = sub.reshape(kc, 128, 128).transpose(1, 0, 2).reshape(128, kc * 128)
    return out


def build_wslots(inp):
    w_in = np.asarray(inp["w_in"][0], np.float32)
    ws = np.zeros((NSLOT, 128, 1024), np.float32)
    ar = np.arange(128)
    for c in range(4):
        ws[SLOTS[f"qa{c}"]] = _tile_cols(w_in, c * 128 + ar)
    for g in range(2):
        kd = _tile_cols(w_in, 512 + g * 64 + (ar % 64)).reshape(128, 8, 128)
        ke = kd.copy(); ke[:, :, 64:] = 0.0
        ko = kd.copy(); ko[:, :, :64] = 0.0
        ws[SLOTS[f"ke{g}"]] = ke.reshape(128, 1024)
        ws[SLOTS[f"ko{g}"]] = ko.reshape(128, 1024)
        ws[SLOTS[f"vd{g}"]] = _tile_cols(w_in, 640 + g * 64 + (ar % 64))
    for h in range(4):
        ws[SLOTS[f"qh{h}"]] = _tile_cols(w_in, 768 + h * 128 + ar)
        ws[SLOTS[f"fh{h}"]] = _tile_cols(w_in, 1280 + h * 128 + ar)
        ws[SLOTS[f"ih{h}"]] = _tile_cols(w_in, 1792 + h * 128 + ar)
        ws[SLOTS[f"gh{h}"]] = _tile_cols(w_in, 2304 + h * 128 + ar)
    w_out = np.asarray(inp["w_out"][0], np.float32)
    cq = np.asarray(inp["ca_wq"][0], np.float32)
    ck = np.asarray(inp["ca_wk"][0], np.float32)
    cv = np.asarray(inp["ca_wv"][0], np.float32)
    co = np.asarray(inp["ca_wo"][0], np.float32)
    for c in range(8):
        ws[SLOTS[f"wo{c}"]] = _tile_cols(w_out, c * 128 + ar)
        ws[SLOTS[f"cq{c}"]] = _tile_cols(cq, c * 128 + ar)
        ws[SLOTS[f"co{c}"]] = _tile_cols(co, c * 128 + ar)
        ws[SLOTS[f"ck{c}"]] = _tile_cols(ck, c * 128 + ar)
        ws[SLOTS[f"cv{c}"]] = _tile_cols(cv, c * 128 + ar)
    up = np.asarray(inp["ffn_w_up"][0], np.float32)
    for c in range(44):
        ws[SLOTS[f"up{c}"]] = _tile_cols(up, c * 128 + ar)
    dn = np.asarray(inp["ffn_w_down"][0], np.float32)
    for c in range(8):
        for s in range(3):
            k0 = s * 8 * 128
            k1 = min(DFF, k0 + 1024)
            ws[SLOTS[f"dn{c}_{s}"]] = _tile_cols(dn[k0:k1], c * 128 + ar)
    return ws


V_MIXPRE, V_MIXPOST, V_CAPRE, V_MEMN, V_CAPOST, V_FFNPRE, V_FFNPOST = 0, 8, 16, 24, 32, 40, 48
V_ONW = 56
V_LB = 57
V_CW = 65
V_CB = V_CW + 132
NVEC = V_CB + 44


def build_vecs(inp):
    v = np.zeros((128, NVEC), np.float32)

    def col8(a):
        return np.asarray(a, np.float32).reshape(8, 128).T

    v[:, V_MIXPRE:V_MIXPRE + 8] = col8(inp["mix_pre_norm"][0])
    v[:, V_MIXPOST:V_MIXPOST + 8] = col8(inp["mix_post_norm"][0])
    v[:, V_CAPRE:V_CAPRE + 8] = col8(inp["ca_pre_norm"][0])
    v[:, V_MEMN:V_MEMN + 8] = col8(inp["mem_norm"][0])
    v[:, V_CAPOST:V_CAPOST + 8] = col8(inp["ca_post_norm"][0])
    v[:, V_FFNPRE:V_FFNPRE + 8] = col8(inp["ffn_pre_norm"][0])
    v[:, V_FFNPOST:V_FFNPOST + 8] = col8(inp["ffn_post_norm"][0])
    v[:, V_ONW] = np.asarray(inp["hgrn_out_norm"][0], np.float32)
    lb = np.asarray(inp["hgrn_lb_logits"], np.float32)
    v[:, V_LB:V_LB + 4] = lb[0].reshape(4, 128).T
    v[:, V_LB + 4:V_LB + 8] = lb[1].reshape(4, 128).T
    cw = np.asarray(inp["ffn_conv_w"][0], np.float32)
    for t in range(3):
        v[:, V_CW + t * 44:V_CW + (t + 1) * 44] = cw[t].reshape(44, 128).T
    v[:, V_CB:V_CB + 44] = np.asarray(inp["ffn_conv_b"][0], np.float32).reshape(44, 128).T
    return v


def build_cst(half, sinks):
    c = np.zeros((128, 8, 512), np.float32)
    k = np.arange(128)[:, None]
    q = np.arange(128)[None, :]
    c[:, 0, 0:128] = np.eye(128, dtype=np.float32)
    mc = np.where(k <= q, 0.0, NEG).astype(np.float32)
    mp = np.where(k > q, 0.0, NEG).astype(np.float32)
    cm = (k <= q).astype(np.float32)
    c[:, 1, :] = np.tile(mc, (1, 4))
    c[:, 2, :] = np.tile(mp, (1, 4))
    c[:, 3, :] = np.tile(cm, (1, 4))
    rm = np.ones((128, 512), np.float32)
    rm[:, ::128] = 0.0
    c[:, 4, :] = rm
    if half == 0:
        c[:, 5, :] = NEG
        c[:, 0, 128] = 0.0
    else:
        c[:, 5, :] = c[:, 2, :]
        c[:, 0, 128] = 1.0
    sr = np.repeat(np.asarray(sinks, np.float32), 128)
    c[0, 6, :] = sr[0:512]
    c[0, 7, :] = sr[512:1024]
    return c


def build_program(TH, PRE):
    assert TH % 512 == 0 and PRE % 512 == 0
    NTOT = PRE + BLK + TH
    nc = bass.Bass("TRN2", target_bir_lowering=False)
    xT = nc.dram_tensor("xT", [8, 128, NTOT], F32, kind="ExternalInput").ap()
    memT = nc.dram_tensor("memT", [8, 128, NMEM], F32, kind="ExternalInput").ap()
    wsl = nc.dram_tensor("wslots", [NSLOT, 128, 1024], F32, kind="ExternalInput").ap()
    vecs_d = nc.dram_tensor("vecs", [128, NVEC], F32, kind="ExternalInput").ap()
    cst_d = nc.dram_tensor("cst", [128, 8, 512], F32, kind="ExternalInput").ap()
    outT = nc.dram_tensor("outT", [8, 128, TH], F32, kind="ExternalOutput").ap()

    es = ExitStack()
    P = Prog(nc, es)
    NMAX = GB * BLK

    stage = P.mb([128, 8, NMAX], F32, "stage")
    cstf = stage
    vecs = P.sb([128, NVEC], F32, "vecs")
    P.dma("sp", lambda e: e.dma_start(out=vecs.t[:], in_=vecs_d), "c0", w=[vecs])
    P.dma("sp", lambda e: e.dma_start(out=cstf.t[:], in_=cst_d), "c0", w=cstf.c)

    ident = P.sb([128, 128], BF16, "ident")
    ones = P.sb([128, 128], BF16, "ones")
    maskc = P.sb([128, 512], BF16, "maskc")
    maskp = P.sb([128, 512], BF16, "maskp")
    maskp0 = P.sb([128, 512], BF16, "maskp0")
    cmask = P.sb([128, 512], F32, "cmask")
    rmask = P.sb([128, 512], F32, "rmask")
    esink = P.sb([1, 1024], BF16, "esink")
    hflag = P.sb([128, 1], F32, "hflag")
    lbA = P.sb([128, 4], F32, "lbA")
    lbB = P.sb([128, 4], F32, "lbB")
    lbnB = P.sb([128, 4], F32, "lbnB")
    lbt = P.sb([128, 4], F32, "lbt")

    P.op("dve", lambda e: e.tensor_copy(ident.t[:], cstf.t[:, 0, 0:128]), r=cstf.c, w=[ident])
    P.op("dve", lambda e: e.memset(ones.t[:], 1.0), w=[ones])
    P.op("dve", lambda e: e.tensor_copy(maskc.t[:], cstf.t[:, 1, :]), r=cstf.c, w=[maskc])
    P.op("dve", lambda e: e.tensor_copy(maskp.t[:], cstf.t[:, 2, :]), r=cstf.c, w=[maskp])
    P.op("dve", lambda e: e.tensor_copy(cmask.t[:], cstf.t[:, 3, :]), r=cstf.c, w=[cmask])
    P.op("dve", lambda e: e.tensor_copy(rmask.t[:], cstf.t[:, 4, :]), r=cstf.c, w=[rmask])
    P.op("dve", lambda e: e.tensor_copy(maskp0.t[:], cstf.t[:, 5, :]), r=cstf.c, w=[maskp0])
    P.op("dve", lambda e: e.tensor_copy(hflag.t[:], cstf.t[:, 0, 128:129]), r=cstf.c, w=[hflag])
    P.op("act", lambda e: e.activation(out=esink.t[0:1, :].rearrange("p (g n) -> p g n", g=2),
                                       in_=cstf.t[0:1, 6:8, :], func=AF.Exp), r=cstf.c, w=[esink])
    P.op("dve", lambda e: e.tensor_tensor(lbt.t[:], vecs.t[:, V_LB + 4:V_LB + 8], vecs.t[:, V_LB:V_LB + 4],
                                          ALU.subtract), r=[vecs], w=[lbt])
    P.op("act", lambda e: e.activation(out=lbt.t[:], in_=lbt.t[:], func=AF.Exp), r=[lbt], w=[lbt])
    P.op("act", lambda e: e.activation(out=lbt.t[:], in_=lbt.t[:], func=AF.Ln, bias=1.0, scale=1.0), r=[lbt], w=[lbt])
    P.op("act", lambda e: e.activation(out=lbt.t[:], in_=lbt.t[:], func=AF.Exp, scale=-1.0), r=[lbt], w=[lbt])
    P.op("dve", lambda e: e.tensor_scalar(lbA.t[:], lbt.t[:], 0.5, 0.5, ALU.mult, ALU.add), r=[lbt], w=[lbA])
    P.op("dve", lambda e: e.tensor_scalar(lbB.t[:], lbt.t[:], -0.5, 0.5, ALU.mult, ALU.add), r=[lbt], w=[lbB])
    P.op("dve", lambda e: e.tensor_scalar(lbnB.t[:], lbt.t[:], 0.5, -0.5, ALU.mult, ALU.add), r=[lbt], w=[lbnB])

    try:
        ckpt('none')
    except StopBuild:
        pass
    banks = [P.ps([128, 512], F32, f"bank{i}") for i in range(5)]
    lbank = [P.ps([128, 512], F32, f"lbank{i}") for i in range(2)]
    tbank = [P.ps([128, 1024], BF16, "tbank")]
    bstate = {"i": 0, "t": 0}

    def bank():
        b = banks[bstate["i"] % len(banks)]
        bstate["i"] += 1
        return b

    tbh = [Buf(tbank[0].t, "tbh0"), Buf(tbank[0].t, "tbh1")]

    def tbk():
        bstate["t"] += 1
        return tbh[0], 0

    ring = [P.sb([128, 1024], BF16, f"ring{i}") for i in range(RING)]
    wseq = []
    wstate = {"next": 0, "issued": 0}

    def wnext(name):
        i = wstate["next"]
        assert wseq[i] == SLOTS[name], (i, name)
        while wstate["issued"] < min(len(wseq), i + RING - 2):
            j = wstate["issued"]
            rb = ring[j % RING]
            sid = wseq[j]
            P.dma("pool", lambda e, rb=rb, sid=sid: e.dma_start(out=rb.t[:], in_=wsl[sid]),
                  ("ring", j % RING), w=[rb])
            wstate["issued"] += 1
        wstate["next"] += 1
        return ring[i % RING]

    groups = []
    n = 0
    for gi in range(PRE // NMAX):
        groups.append(("pre", n, GB, None, gi == PRE // NMAX - 1))
        n += NMAX
    groups.append(("halo", n, 1, None, False))
    n += BLK
    for gi in range(TH // NMAX):
        groups.append(("main", n, GB, gi * NMAX, False))
        n += NMAX
    assert n == NTOT

    mem_names = [f"ck{c}" for c in range(8)] + [f"cv{c}" for c in range(8)]
    full_names = ([f"qa{c}" for c in range(4)] + ["ke0", "ko0", "ke1", "ko1"]
                  + [f"fh{h}" for h in range(4)] + [f"qh{h}" for h in range(4)] + [f"gh{h}" for h in range(4)]
                  + ["vd0", "vd1", "ih0", "ih1", "ih2", "ih3"]
                  + [f"wo{c}" for c in range(8)] + [f"cq{c}" for c in range(8)] + [f"co{c}" for c in range(8)])
    for j in range(NPAIR):
        full_names += [f"up{j}", f"up{NPAIR + j}"]
    full_names += [f"dn{c}_{s}" for c in range(8) for s in range(3)]
    pre_names = [f"fh{h}" for h in range(4)] + ["ih0", "ih1", "ih2", "ih3"]
    pre_last_names = ["ke0", "ko0", "ke1", "ko1"] + [f"fh{h}" for h in range(4)] + ["vd0", "vd1", "ih0", "ih1", "ih2", "ih3"]
    wseq.extend(SLOTS[nm] for nm in mem_names)
    for (kind, n0, nblk, ooff, last) in groups:
        if kind == "pre":
            wseq.extend(SLOTS[nm] for nm in (pre_last_names if last else pre_names))
        else:
            wseq.extend(SLOTS[nm] for nm in full_names)

    Xb = [P.mb([128, 8, NMAX], F32, f"X{i}") for i in range(2)]
    hT = P.mb([128, 8, NMAX], BF16, "hT")
    sq = P.mb([128, 8, NMAX], BF16, "sq")
    rstd = P.sb([128, NMAX], F32, "rstd")
    lnt = P.sb([128, NMAX], F32, "lnt")
    tmpn = [lnt, P.sb([128, NMAX], F32, "tmpn1")]
    S = [P.sb([128, 128], F32, f"S{h}") for h in range(4)]
    for h in range(4):
        P.op("pool", lambda e, h=h: e.memset(S[h].t[:], 0.0), w=[S[h]])
    kT = [[P.sb([128, BLK + NMAX], BF16, f"kT{g}_{par}") for par in range(2)] for g in range(2)]
    Vd = [P.sb([128, GB + 1, 128], BF16, f"Vd{g}") for g in range(2)]
    for g in range(2):
        for par in range(2):
            P.op("pool", lambda e, g=g, par=par: e.memset(kT[g][par].t[:], 0.0), w=[kT[g][par]])
        P.op("pool", lambda e, g=g: e.memset(Vd[g].t[:], 0.0), w=[Vd[g]])
    uh = P.sb([128, 44, 2], F32, "uh")
    P.op("pool", lambda e: e.memset(uh.t[:], 0.0), w=[uh])
    KmT = P.sb([128, 8, NMEM], BF16, "KmT")
    Vm = P.sb([128, 2, D], BF16, "Vm")

    qaT = [P.sb([128, NMAX], BF16, f"qaT{c}") for c in range(4)]
    attnT = P.mb([128, 4, NMAX], BF16, "attnT")
    recT = [P.sb([128, NMAX], BF16, f"recT{h}") for h in range(4)]
    th = [P.sb([128, NMAX], F32, f"th{h}") for h in range(4)]
    qs = [P.sb([128, NMAX], BF16, f"qs{h}") for h in range(4)]
    gs = [P.sb([128, NMAX], BF16, f"gs{h}") for h in range(4)]
    vtok = [P.sb([128, GB, 128], BF16, f"vtok{h}") for h in range(4)]
    gg = [P.sb([128, NMAX], F32, f"gg{i}") for i in range(2)]
    bc = [P.sb([128, NMAX], F32, f"bc{i}") for i in range(2)]
    kk = [P.sb([128, NMAX], F32, f"kk{i}") for i in range(2)]
    qt = [P.sb([128, NMAX], BF16, f"qt{h}") for h in range(4)]
    kt = [P.sb([128, NMAX], BF16, f"kt{h}") for h in range(4)]
    ktok = [P.sb([128, GB, 128], BF16, f"ktok{h}") for h in range(4)]
    Am = [P.sb([128, NMAX], BF16, f"Am{h}") for h in range(4)]
    cr = [P.sb([128, GB], F32, f"cr{h}") for h in range(4)]
    dS = [P.sb([128, GB], F32, f"dS{h}") for h in range(4)]
    dK = [P.sb([128, GB], F32, f"dK{h}") for h in range(4)]
    eC = [P.sb([128, GB], F32, f"eC{h}") for h in range(4)]
    Sb = [[P.sb([128, 128], BF16, f"Sb{h}_{i}") for i in range(2)] for h in range(4)]
    Usb = th
    osq = P.sb([128, NMAX], BF16, "osq")
    r1 = gg[1]
    Pp = [P.sb([128, 512], BF16, f"Pp{i}") for i in range(2)]
    Pc = [P.sb([128, 512], BF16, f"Pc{i}") for i in range(2)]
    rden = [P.sb([128, 512], F32, f"rden{i}") for i in range(2)]
    qcT = qt + kt
    _pt = qaT + Am
    PT = [[_pt[2 * h + m] for m in range(2)] for h in range(4)]
    ocT = recT + gs
    rdc = bc
    aT = (qaT + Am + qt + kt + recT + gs)[:NPAIR]
    ug = [P.sb([128, NMAX + 2], F32, f"ug{i}") for i in range(3)]
    uv = [P.sb([128, NMAX + 2], F32, f"uv{i}") for i in range(3)]
    cg = [th[0], th[1], gg[0]]
    cv = [th[2], th[3], gg[1]]

    def mm(out_b, out_ap, lhs_b, lhs_ap, rhs_b, rhs_ap, start, stop):
        P.op("pe", lambda e: e.matmul(out_ap, lhs_ap, rhs_ap, start=start, stop=stop),
             r=[lhs_b, rhs_b], w=[out_b])

    def rstd_from(psb, N, dim):
        P.op("act", lambda e: e.activation(out=lnt.t[:, :N], in_=psb.t[:, :N], func=AF.Ln, bias=EPS,
                                           scale=1.0 / dim), r=[psb], w=[lnt])
        P.op("act", lambda e: e.activation(out=rstd.t[:, :N], in_=lnt.t[:, :N], func=AF.Exp, scale=-0.5),
             r=[lnt], w=[rstd])

    def prenorm(X, N, gcol):
        P.op("act", lambda e: e.activation(out=sq.t[:, :, :N], in_=X.t[:, :, :N], func=AF.Square), r=X.c, w=sq.c)
        pb = bank()
        for c in range(8):
            mm(pb, pb.t[:, :N], ones, ones.t[:], sq.c[c], sq.t[:, c, :N], c == 0, c == 7)
        rstd_from(pb, N, D)
        for c in range(8):
            P.op("dve", lambda e, c=c: e.scalar_tensor_tensor(hT.t[:, c, :N], X.t[:, c, :N],
                                                              vecs.t[:, gcol + c:gcol + c + 1], rstd.t[:, :N],
                                                              ALU.mult, ALU.mult), r=[X.c[c], vecs, rstd], w=[hT.c[c]])

    def proj_fm(name, rhs_list, N, kchunks=8):
        wb = wnext(name)
        pb = bank()
        for k in range(kchunks):
            rb, rap = rhs_list[k]
            mm(pb, pb.t[:, :N], wb, wb.t[:, k * 128:(k + 1) * 128], rb, rap, k == 0, k == kchunks - 1)
        return pb

    def postnorm_residual(X, N, gcol, pbs):
        for c in range(8):
            pb = pbs[c]()
            P.op("act", lambda e, c=c, pb=pb: e.activation(out=stage.t[:, c, :N], in_=pb.t[:, :N], func=AF.Copy),
                 r=[pb], w=[stage.c[c]])
            P.op("act", lambda e, c=c, pb=pb: e.activation(out=sq.t[:, c, :N], in_=pb.t[:, :N], func=AF.Square),
                 r=[pb], w=[sq.c[c]])
        pb2 = bank()
        for c in range(8):
            mm(pb2, pb2.t[:, :N], ones, ones.t[:], sq.c[c], sq.t[:, c, :N], c == 0, c == 7)
        rstd_from(pb2, N, D)
        for c in range(8):
            eng = "pool" if c % 2 == 1 else "dve"
            P.op(eng, lambda e, c=c: e.tensor_tensor(stage.t[:, c, :N], stage.t[:, c, :N], rstd.t[:, :N], ALU.mult),
                 r=[stage.c[c], rstd], w=[stage.c[c]])
        for c in range(8):
            P.op("dve", lambda e, c=c: e.scalar_tensor_tensor(X.t[:, c, :N], stage.t[:, c, :N],
                                                              vecs.t[:, gcol + c:gcol + c + 1], X.t[:, c, :N],
                                                              ALU.mult, ALU.add),
                 r=[stage.c[c], vecs, X.c[c]], w=[X.c[c]])

    hT_chunks = lambda N: [(hT.c[k], hT.t[:, k, :N]) for k in range(8)]

    def do_memkv():
        memX = stage
        P.dma("sp", lambda e: e.dma_start(out=memX.t[:, :, :NMEM], in_=memT.rearrange("c p n -> p c n")),
              "memx", w=memX.c)
        prenorm(memX, NMEM, V_MEMN)
        for c in range(8):
            pb = proj_fm(f"ck{c}", hT_chunks(NMEM), NMEM)
            P.op("act", lambda e, c=c, pb=pb: e.activation(out=KmT.t[:, c, :], in_=pb.t[:, :NMEM], func=AF.Copy),
                 r=[pb], w=[KmT])
        for c in range(8):
            wb = wnext(f"cv{c}")
            pb = bank()
            for mh in range(2):
                for k in range(8):
                    mm(pb, pb.t[:, mh * 128:(mh + 1) * 128], hT.c[k], hT.t[:, k, mh * 128:(mh + 1) * 128],
                       wb, wb.t[:, k * 128:(k + 1) * 128], k == 0, k == 7)
            P.op("act", lambda e, c=c, pb=pb: e.activation(
                out=Vm.t[:, :, c * 128:(c + 1) * 128],
                in_=pb.t[:, 0:256].rearrange("p (m d) -> p m d", m=2), func=AF.Copy), r=[pb], w=[Vm])

    def load_x(gi):
        kind, n0, nblk, ooff, last = groups[gi]
        N = nblk * BLK
        X = Xb[gi % 2]
        P.dma("sp", lambda e: e.dma_start(out=X.t[:, :, :N], in_=xT[:, :, n0:n0 + N].rearrange("c p n -> p c n")),
              ("x", gi % 2), w=X.c)

    class _V:
        def __init__(self, t, k):
            self.t3, self.k = t, k

        def __getitem__(self, idx):
            p, f = idx
            return self.t3[p, self.k, f]

    hsets = [(gg[0], gg[0].t, bc[0], bc[0].t, kk[0], kk[0].t),
             (gg[1], gg[1].t, bc[1], bc[1].t, kk[1], kk[1].t),
             (stage.c[0], _V(stage.t, 0), stage.c[1], _V(stage.t, 1), stage.c[2], _V(stage.t, 2)),
             (stage.c[3], _V(stage.t, 3), stage.c[4], _V(stage.t, 4), stage.c[5], _V(stage.t, 5))]

    def hgrn_prep_all(N, nblk, full):
        HS = range(4)
        gB = [hsets[h][0] for h in HS]; gT = [hsets[h][1] for h in HS]
        bB = [hsets[h][2] for h in HS]; bT = [hsets[h][3] for h in HS]
        kB = [hsets[h][4] for h in HS]; kT_ = [hsets[h][5] for h in HS]
        bc3 = [bT[h][:, :N].rearrange("p (b t) -> p b t", t=BLK) for h in HS]
        for h in HS:
            P.op("act", lambda e, h=h: e.activation(out=gT[h][:, :N], in_=th[h].t[:, :N], func=AF.Ln,
                                                    bias=lbA.t[:, h:h + 1], scale=lbB.t[:, h:h + 1]),
                 r=[th[h], lbA, lbB], w=[gB[h]])
        for h in HS:
            P.op("dve", lambda e, h=h: e.tensor_tensor_scan(bT[h][:, :N], rmask.t[:, :N], gT[h][:, :N], 0.0,
                                                            ALU.mult, ALU.add), r=[rmask, gB[h]], w=[bB[h]])
        for h in HS:
            P.op("act", lambda e, h=h: e.activation(out=kT_[h][:, :N], in_=th[h].t[:, :N], func=AF.Identity,
                                                    bias=lbB.t[:, h:h + 1], scale=lbnB.t[:, h:h + 1]),
                 r=[th[h], lbnB, lbB], w=[kB[h]])
        for h in HS:
            P.op("pool", lambda e, h=h: e.tensor_copy(cr[h].t[:, :nblk], bc3[h][:, :, 63]), r=[bB[h]], w=[cr[h]])
            P.op("act", lambda e, h=h: e.activation(out=dS[h].t[:, :nblk], in_=bc3[h][:, :, 127], func=AF.Exp),
                 r=[bB[h]], w=[dS[h]])
        for h in HS:
            P.op("dve", lambda e, h=h: e.tensor_tensor(bc3[h], bc3[h],
                                                       cr[h].t[:, :nblk].rearrange("p (b o) -> p b o", o=1)
                                                       .to_broadcast([128, nblk, BLK]), ALU.subtract),
                 r=[bB[h], cr[h]], w=[bB[h]])
        for h in HS:
            P.op("act", lambda e, h=h: e.activation(out=gT[h][:, :N], in_=bT[h][:, :N], func=AF.Exp, scale=-1.0),
                 r=[bB[h]], w=[gB[h]])
        for h in HS:
            P.op("dve", lambda e, h=h: e.tensor_tensor(kt[h].t[:, :N], kT_[h][:, :N], gT[h][:, :N], ALU.mult),
                 r=[kB[h], gB[h]], w=[kt[h]])
        for h in HS:
            P.op("act", lambda e, h=h: e.activation(out=dK[h].t[:, :nblk], in_=bc3[h][:, :, 127], func=AF.Exp),
                 r=[bB[h]], w=[dK[h]])
            if full:
                P.op("act", lambda e, h=h: e.activation(out=eC[h].t[:, :nblk], in_=cr[h].t[:, :nblk], func=AF.Exp),
                     r=[cr[h]], w=[eC[h]])
        if full:
            for h in HS:
                P.op("act", lambda e, h=h: e.activation(out=kT_[h][:, :N], in_=bT[h][:, :N], func=AF.Exp),
                     r=[bB[h]], w=[kB[h]])
            for h in HS:
                P.op("dve", lambda e, h=h: e.scalar_tensor_tensor(qt[h].t[:, :N], qs[h].t[:, :N],
                                                                  float(128 ** -0.5), kT_[h][:, :N],
                                                                  ALU.mult, ALU.mult),
                     r=[qs[h], kB[h]], w=[qt[h]])

    def hgrn_core(N, nblk, full):
        for h in range(4):
            tb, to = tbk()
            for b in range(nblk):
                P.op("pe", lambda e, tb=tb, to=to, h=h, b=b: e.transpose(
                    tb.t[:, to + b * 128:to + (b + 1) * 128], kt[h].t[:, b * 128:(b + 1) * 128], ident.t[:]),
                     r=[kt[h], ident], w=[tb])
            P.op("act", lambda e, tb=tb, to=to, h=h: e.activation(
                out=ktok[h].t[:, :nblk, :], in_=tb.t[:, to:to + N].rearrange("p (b f) -> p b f", f=128),
                func=AF.Copy), r=[tb], w=[ktok[h]])
        for h in range(4):
            ub = bank()
            for b in range(nblk):
                mm(ub, ub.t[:, b * 128:(b + 1) * 128], ktok[h], ktok[h].t[:, b, :], vtok[h], vtok[h].t[:, b, :],
                   True, True)
            P.op("act", lambda e, ub=ub, h=h: e.activation(out=Usb[h].t[:, :N], in_=ub.t[:, :N], func=AF.Copy),
                 r=[ub], w=[Usb[h]])
        if full:
            for h in range(4):
                ab = bank()
                for b in range(nblk):
                    mm(ab, ab.t[:, b * 128:(b + 1) * 128], kt[h], kt[h].t[:, b * 128:(b + 1) * 128],
                       qt[h], qt[h].t[:, b * 128:(b + 1) * 128], True, True)
                P.op("dve", lambda e, ab=ab, h=h: e.tensor_tensor(Am[h].t[:, :N], ab.t[:, :N], cmask.t[:, :N], ALU.mult),
                     r=[ab, cmask], w=[Am[h]])

        def s_update(h, b):
            P.op("act", lambda e: e.activation(out=S[h].t[:], in_=S[h].t[:], func=AF.Identity,
                                               scale=dS[h].t[:, b:b + 1]), r=[S[h], dS[h]], w=[S[h]])
            P.op("dve", lambda e: e.scalar_tensor_tensor(S[h].t[:], Usb[h].t[:, b * 128:(b + 1) * 128],
                                                         dK[h].t[:, b:b + 1], S[h].t[:], ALU.mult, ALU.add),
                 r=[Usb[h], dK[h], S[h]], w=[S[h]])

        if not full:
            for b in range(nblk):
                for h in range(4):
                    s_update(h, b)
            return
        for hp in range(2):
            hs = (2 * hp, 2 * hp + 1)
            ob = {h: lbank[i] for i, h in enumerate(hs)}
            for b in range(nblk):
                for h in hs:
                    sb_ = Sb[h][b % 2]
                    P.op("dve", lambda e, h=h, b=b, sb_=sb_: e.tensor_scalar(sb_.t[:], S[h].t[:], eC[h].t[:, b:b + 1],
                                                                             None, ALU.mult),
                         r=[S[h], eC[h]], w=[sb_])
                    o_b = ob[h]
                    mm(o_b, o_b.t[:, b * 128:(b + 1) * 128], vtok[h], vtok[h].t[:, b, :],
                       Am[h], Am[h].t[:, b * 128:(b + 1) * 128], True, False)
                    mm(o_b, o_b.t[:, b * 128:(b + 1) * 128], sb_, sb_.t[:], qt[h], qt[h].t[:, b * 128:(b + 1) * 128],
                       False, True)
                    s_update(h, b)
            for h in hs:
                o_b = ob[h]
                P.op("act", lambda e, o_b=o_b, h=h: e.activation(out=th[h].t[:, :N], in_=o_b.t[:, :N], func=AF.Copy),
                     r=[o_b], w=[th[h]])
        HS = range(4)
        lnB = [hsets[h][0] for h in HS]
        lnT = [hsets[h][1] for h in HS]
        for h in HS:
            P.op("act", lambda e, h=h: e.activation(out=Am[h].t[:, :N], in_=th[h].t[:, :N], func=AF.Square),
                 r=[th[h]], w=[Am[h]])
        pbs_ = []
        for h in HS:
            pb = bank()
            mm(pb, pb.t[:, :N], ones, ones.t[:], Am[h], Am[h].t[:, :N], True, True)
            pbs_.append(pb)
        for h in HS:
            P.op("act", lambda e, h=h, pb=pbs_[h]: e.activation(out=lnT[h][:, :N], in_=pb.t[:, :N], func=AF.Ln,
                                                                 bias=EPS, scale=1.0 / 128), r=[pbs_[h]], w=[lnB[h]])
        for h in HS:
            P.op("act", lambda e, h=h: e.activation(out=lnT[h][:, :N], in_=lnT[h][:, :N], func=AF.Exp, scale=-0.5),
                 r=[lnB[h]], w=[lnB[h]])
        for h in HS:
            P.op("dve", lambda e, h=h: e.tensor_tensor(th[h].t[:, :N], th[h].t[:, :N], lnT[h][:, :N], ALU.mult),
                 r=[th[h], lnB[h]], w=[th[h]])
        for h in HS:
            P.op("dve", lambda e, h=h: e.scalar_tensor_tensor(recT[h].t[:, :N], th[h].t[:, :N],
                                                              vecs.t[:, V_ONW:V_ONW + 1], gs[h].t[:, :N],
                                                              ALU.mult, ALU.mult),
                 r=[th[h], vecs, gs[h]], w=[recT[h]])

    def tokmajor(names, dsts, N, nblk):
        wbs = [wnext(nm) for nm in names]
        pbs = [bank() for _ in names]
        for b in range(nblk):
            for k in range(8):
                for wb, pb in zip(wbs, pbs):
                    mm(pb, pb.t[:, b * 128:(b + 1) * 128], hT.c[k], hT.t[:, k, b * 128:(b + 1) * 128],
                       wb, wb.t[:, k * 128:(k + 1) * 128], k == 0, k == 7)
        for pb, (db, dap) in zip(pbs, dsts):
            P.op("act", lambda e, pb=pb, dap=dap: e.activation(
                out=dap, in_=pb.t[:, :N].rearrange("p (b f) -> p b f", f=128), func=AF.Copy), r=[pb], w=[db])

    def roll_kv(nblk):
        for g in range(2):
            for par in range(2):
                P.op("pool", lambda e, g=g, par=par: e.tensor_copy(kT[g][par].t[:, 0:BLK],
                                                                    kT[g][par].t[:, nblk * BLK:(nblk + 1) * BLK]),
                     r=[kT[g][par]], w=[kT[g][par]])
            P.op("pool", lambda e, g=g: e.tensor_copy(Vd[g].t[:, 0, :], Vd[g].t[:, nblk, :]),
                 r=[Vd[g]], w=[Vd[g]])

    def do_prefix(gi):
        kind, n0, nblk, ooff, last = groups[gi]
        N = nblk * BLK
        X = Xb[gi % 2]
        prenorm(X, N, V_MIXPRE)
        if last:
            for g in range(2):
                for par, nm in enumerate(("ke", "ko")):
                    pb = proj_fm(f"{nm}{g}", hT_chunks(N), N)
                    P.op("act", lambda e, g=g, par=par, pb=pb: e.activation(
                        out=kT[g][par].t[:, BLK:BLK + N], in_=pb.t[:, :N], func=AF.Copy), r=[pb], w=[kT[g][par]])
        for h in range(4):
            pb = proj_fm(f"fh{h}", hT_chunks(N), N)
            P.op("act", lambda e, h=h, pb=pb: e.activation(out=th[h].t[:, :N], in_=pb.t[:, :N], func=AF.Tanh,
                                                           scale=0.5), r=[pb], w=[th[h]])
        if last:
            tokmajor(["vd0", "vd1", "ih0"], [(Vd[0], Vd[0].t[:, 1:1 + nblk, :]), (Vd[1], Vd[1].t[:, 1:1 + nblk, :]),
                                             (vtok[0], vtok[0].t[:, :nblk, :])], N, nblk)
            tokmajor(["ih1", "ih2", "ih3"], [(vtok[h], vtok[h].t[:, :nblk, :]) for h in (1, 2, 3)], N, nblk)
        else:
            tokmajor(["ih0", "ih1"], [(vtok[h], vtok[h].t[:, :nblk, :]) for h in (0, 1)], N, nblk)
            tokmajor(["ih2", "ih3"], [(vtok[h], vtok[h].t[:, :nblk, :]) for h in (2, 3)], N, nblk)
        hgrn_prep_all(N, nblk, False)
        hgrn_core(N, nblk, False)
        if last:
            roll_kv(nblk)

    def do_full(gi):
        kind, n0, nblk, ooff, last = groups[gi]
        N = nblk * BLK
        X = Xb[gi % 2]
        first_main = (kind == "main" and ooff == 0)
        prenorm(X, N, V_MIXPRE)
        for c in range(4):
            pb = proj_fm(f"qa{c}", hT_chunks(N), N)
            P.op("dve", lambda e, c=c, pb=pb: e.tensor_copy(qaT[c].t[:, :N], pb.t[:, :N]), r=[pb], w=[qaT[c]])
        for g in range(2):
            for par, nm in enumerate(("ke", "ko")):
                pb = proj_fm(f"{nm}{g}", hT_chunks(N), N)
                P.op("dve", lambda e, g=g, par=par, pb=pb: e.tensor_copy(kT[g][par].t[:, BLK:BLK + N], pb.t[:, :N]),
                     r=[pb], w=[kT[g][par]])
        for h in range(4):
            pb = proj_fm(f"fh{h}", hT_chunks(N), N)
            P.op("act", lambda e, h=h, pb=pb: e.activation(out=th[h].t[:, :N], in_=pb.t[:, :N], func=AF.Tanh,
                                                           scale=0.5), r=[pb], w=[th[h]])
        for h in range(4):
            pb = proj_fm(f"qh{h}", hT_chunks(N), N)
            P.op("act", lambda e, h=h, pb=pb: e.activation(out=qs[h].t[:, :N], in_=pb.t[:, :N], func=AF.Silu),
                 r=[pb], w=[qs[h]])
        for h in range(4):
            pb = proj_fm(f"gh{h}", hT_chunks(N), N)
            P.op("act", lambda e, h=h, pb=pb: e.activation(out=gs[h].t[:, :N], in_=pb.t[:, :N], func=AF.Silu),
                 r=[pb], w=[gs[h]])
        tokmajor(["vd0", "vd1", "ih0"], [(Vd[0], Vd[0].t[:, 1:1 + nblk, :]), (Vd[1], Vd[1].t[:, 1:1 + nblk, :]),
                                         (vtok[0], vtok[0].t[:, :nblk, :])], N, nblk)
        tokmajor(["ih1", "ih2", "ih3"], [(vtok[h], vtok[h].t[:, :nblk, :]) for h in (1, 2, 3)], N, nblk)
        hgrn_prep_all(N, nblk, True)
        for b in range(nblk):
            for g in range(2):
                i2 = (b * 2 + g) % 2
                sp_, sc_ = bank(), bank()
                mprev = maskp0 if (first_main and b == 0) else maskp
                mm(sp_, sp_.t[:, :], ident, ident.t[:], mprev, mprev.t[:], True, False)
                mm(sc_, sc_.t[:, :], ident, ident.t[:], maskc, maskc.t[:], True, False)
                for j in range(4):
                    hq = 4 * g + j
                    c = hq // 2
                    kb = kT[g][j % 2]
                    mm(sp_, sp_.t[:, j * 128:(j + 1) * 128], kb, kb.t[:, b * 128:(b + 1) * 128],
                       qaT[c], qaT[c].t[:, b * 128:(b + 1) * 128], False, j == 3)
                    mm(sc_, sc_.t[:, j * 128:(j + 1) * 128], kb, kb.t[:, (b + 1) * 128:(b + 2) * 128],
                       qaT[c], qaT[c].t[:, b * 128:(b + 1) * 128], False, j == 3)
                pp, pc, rd = Pp[i2], Pc[i2], rden[i2]
                P.op("act", lambda e, sp_=sp_, pp=pp: e.activation(out=pp.t[:], in_=sp_.t[:], func=AF.Exp, scale=0.125),
                     r=[sp_], w=[pp])
                P.op("act", lambda e, sc_=sc_, pc=pc: e.activation(out=pc.t[:], in_=sc_.t[:], func=AF.Exp, scale=0.125),
                     r=[sc_], w=[pc])
                ob_, db_ = bank(), bank()
                mm(ob_, ob_.t[:], Vd[g], Vd[g].t[:, b, :], pp, pp.t[:], True, False)
                mm(ob_, ob_.t[:], Vd[g], Vd[g].t[:, b + 1, :], pc, pc.t[:], False, True)
                mm(db_, db_.t[:], ones, ones.t[:], pp, pp.t[:], True, False)
                mm(db_, db_.t[:], ones, ones.t[:], pc, pc.t[:], False, False)
                mm(db_, db_.t[:], ones, ones.t[0:1, :], esink, esink.t[0:1, g * 512:(g + 1) * 512], False, True)
                P.op("act", lambda e, db_=db_, rd=rd: e.activation(out=rd.t[:], in_=db_.t[:], func=AF.Ln),
                     r=[db_], w=[rd])
                P.op("act", lambda e, rd=rd: e.activation(out=rd.t[:], in_=rd.t[:], func=AF.Exp, scale=-1.0),
                     r=[rd], w=[rd])
                for half in range(2):
                    p0 = half * 64
                    P.op("dve", lambda e, half=half, p0=p0, ob_=ob_, rd=rd, g=g, b=b: e.tensor_tensor(
                        attnT.t[p0:p0 + 64, 2 * g:2 * g + 2, b * 128:(b + 1) * 128],
                        ob_.t[p0:p0 + 64, :].rearrange("p (i two q) -> p i two q", two=2, q=128)[:, :, half, :],
                        rd.t[p0:p0 + 64, :].rearrange("p (i two q) -> p i two q", two=2, q=128)[:, :, half, :],
                        ALU.mult), r=[ob_, rd], w=[attnT.c[2 * g], attnT.c[2 * g + 1]])
        roll_kv(nblk)
        hgrn_core(N, nblk, True)
        mixk = [(attnT.c[c], attnT.t[:, c, :N]) for c in range(4)] + [(recT[h], recT[h].t[:, :N]) for h in range(4)]
        postnorm_residual(X, N, V_MIXPOST, [lambda c=c: proj_fm(f"wo{c}", mixk, N) for c in range(8)])

        if 'ca' not in STAGES:
            for c in range(8):
                wnext(f"cq{c}")
            for c in range(8):
                wnext(f"co{c}")
        else:
            do_ca(X, N)
        do_ffn(gi, X, N, kind, ooff, first_main)

    def do_ca(X, N):
        prenorm(X, N, V_CAPRE)
        for c in range(8):
            pb = proj_fm(f"cq{c}", hT_chunks(N), N)
            P.op("dve", lambda e, c=c, pb=pb: e.tensor_copy(qcT[c].t[:, :N], pb.t[:, :N]), r=[pb], w=[qcT[c]])
        for hh in range(4):
            for mh in range(2):
                pb = bank()
                for i, dc in enumerate((2 * hh, 2 * hh + 1)):
                    mm(pb, pb.t[:, :N], KmT, KmT.t[:, dc, mh * 128:(mh + 1) * 128], qcT[dc], qcT[dc].t[:, :N],
                       i == 0, i == 1)
                P.op("act", lambda e, hh=hh, mh=mh, pb=pb: e.activation(out=PT[hh][mh].t[:, :N], in_=pb.t[:, :N],
                                                                        func=AF.Exp, scale=1.0 / 16.0),
                     r=[pb], w=[PT[hh][mh]])
            db_ = bank()
            for mh in range(2):
                mm(db_, db_.t[:, :N], ones, ones.t[:], PT[hh][mh], PT[hh][mh].t[:, :N], mh == 0, mh == 1)
            rd = rdc[hh % 2]
            P.op("act", lambda e, db_=db_, rd=rd: e.activation(out=rd.t[:, :N], in_=db_.t[:, :N], func=AF.Ln),
                 r=[db_], w=[rd])
            P.op("act", lambda e, rd=rd: e.activation(out=rd.t[:, :N], in_=rd.t[:, :N], func=AF.Exp, scale=-1.0),
                 r=[rd], w=[rd])
            for dc in (2 * hh, 2 * hh + 1):
                pb = bank()
                for mh in range(2):
                    mm(pb, pb.t[:, :N], Vm, Vm.t[:, mh, dc * 128:(dc + 1) * 128], PT[hh][mh], PT[hh][mh].t[:, :N],
                       mh == 0, mh == 1)
                P.op("dve", lambda e, dc=dc, pb=pb, rd=rd: e.tensor_tensor(ocT[dc].t[:, :N], pb.t[:, :N], rd.t[:, :N],
                                                                           ALU.mult), r=[pb, rd], w=[ocT[dc]])
        ock = [(ocT[c], ocT[c].t[:, :N]) for c in range(8)]
        postnorm_residual(X, N, V_CAPOST, [lambda c=c: proj_fm(f"co{c}", ock, N) for c in range(8)])

    def do_ffn(gi, X, N, kind, ooff, first_main):
        if 'ffn' not in STAGES:
            for j in range(NPAIR):
                wnext(f"up{j}")
                wnext(f"up{NPAIR + j}")
            for c in range(8):
                for s_ in range(3):
                    wnext(f"dn{c}_{s_}")
            if kind == "main":
                ev = P.dma("sp", lambda e: e.dma_start(out=outT[:, :, ooff:ooff + N].rearrange("c p n -> p c n"),
                                                       in_=X.t[:, :, :N]), ("xo", gi % 2), r=X.c)
                P.finals.append(ev)
            return
        prenorm(X, N, V_FFNPRE)
        for j in range(NPAIR):
            i2 = j % 3
            outs = []
            for (ci, ub, cb) in ((j, ug[i2], cg[i2]), (NPAIR + j, uv[i2], cv[i2])):
                pb = proj_fm(f"up{ci}", hT_chunks(N), N)
                P.op("act", lambda e, ci=ci, pb=pb, cb=cb: e.activation(
                    out=cb.t[:, :N], in_=pb.t[:, :N], func=AF.Identity,
                    bias=vecs.t[:, V_CB + ci:V_CB + ci + 1], scale=vecs.t[:, V_CW + 88 + ci:V_CW + 88 + ci + 1]),
                    r=[pb, vecs], w=[cb])
                P.op("act", lambda e, pb=pb, ub=ub: e.activation(out=ub.t[:, 2:2 + N], in_=pb.t[:, :N], func=AF.Copy),
                     r=[pb], w=[ub])
                if first_main:
                    P.op("pool", lambda e, ci=ci, ub=ub: e.tensor_scalar(ub.t[:, 0:2], uh.t[:, ci, :], hflag.t[:, 0:1],
                                                                         None, ALU.mult), r=[uh, hflag], w=[ub])
                else:
                    P.op("pool", lambda e, ci=ci, ub=ub: e.tensor_copy(ub.t[:, 0:2], uh.t[:, ci, :]), r=[uh], w=[ub])
                P.op("pool", lambda e, ci=ci, ub=ub: e.tensor_copy(uh.t[:, ci, :], ub.t[:, N:N + 2]), r=[ub], w=[uh])
                P.op("dve", lambda e, ci=ci, ub=ub, cb=cb: e.scalar_tensor_tensor(
                    cb.t[:, :N], ub.t[:, 0:N], vecs.t[:, V_CW + ci:V_CW + ci + 1], cb.t[:, :N], ALU.mult, ALU.add),
                    r=[ub, vecs, cb], w=[cb])
                P.op("dve", lambda e, ci=ci, ub=ub, cb=cb: e.scalar_tensor_tensor(
                    cb.t[:, :N], ub.t[:, 1:N + 1], vecs.t[:, V_CW + 44 + ci:V_CW + 44 + ci + 1], cb.t[:, :N],
                    ALU.mult, ALU.add), r=[ub, vecs, cb], w=[cb])
            if kind == "main":
                P.op("act", lambda e, i2=i2: e.activation(out=cg[i2].t[:, :N], in_=cg[i2].t[:, :N],
                                                          func=AF.Gelu_apprx_tanh), r=[cg[i2]], w=[cg[i2]])
                P.op("dve", lambda e, i2=i2, j=j: e.tensor_tensor(aT[j].t[:, :N], cg[i2].t[:, :N], cv[i2].t[:, :N],
                                                                  ALU.mult), r=[cg[i2], cv[i2]], w=[aT[j]])
        if kind == "main":
            def dn_chunk(c):
                pb = bank()
                for s in range(3):
                    wb = wnext(f"dn{c}_{s}")
                    for kq in range(8):
                        jj = s * 8 + kq
                        if jj >= NPAIR:
                            break
                        mm(pb, pb.t[:, :N], wb, wb.t[:, kq * 128:(kq + 1) * 128], aT[jj], aT[jj].t[:, :N],
                           jj == 0, jj == NPAIR - 1)
                return pb
            postnorm_residual(X, N, V_FFNPOST, [lambda c=c: dn_chunk(c) for c in range(8)])
            ev = P.dma("sp", lambda e: e.dma_start(out=outT[:, :, ooff:ooff + N].rearrange("c p n -> p c n"),
                                                   in_=X.t[:, :, :N]), ("xo", gi % 2), r=X.c)
            P.finals.append(ev)
        else:
            for c in range(8):
                for s in range(3):
                    wnext(f"dn{c}_{s}")

    try:
        ckpt('consts')
        do_memkv()
        ckpt('memkv')
        load_x(0)
        for gi in range(len(groups)):
            if gi + 1 < len(groups):
                load_x(gi + 1)
            if groups[gi][0] == "pre":
                do_prefix(gi)
                ckpt('prefix')
            else:
                do_full(gi)
        assert wstate["next"] == len(wseq), (wstate, len(wseq))
    except StopBuild:
        pass
    P.emit()
    es.close()
    global _LAST
    _LAST = dict(P=P, attnT=attnT, recT=recT, X=Xb, hT=hT, qaT=qaT, kT=kT, Vd=Vd, th=th, qs=qs, gs=gs, qt=qt, kt=kt, S=S, vtok=vtok, stage=stage)
    return nc, P


def run_cores(inputs, PRE=None, trace=False):
    x = np.asarray(inputs["x"], np.float32)
    mem = np.asarray(inputs["mem"], np.float32)
    B, T, _ = x.shape
    TH = T // 2
    if PRE is None:
        PRE = TH
    ws = build_wslots(inputs)
    vecs = build_vecs(inputs)
    sinks = np.asarray(inputs["attn_sinks"][0], np.float32)
    csts = [build_cst(0, sinks), build_cst(1, sinks)]
    nc, P = build_program(TH, PRE)
    in_maps = []
    for c in range(2 * B):
        b, half = c // 2, c % 2
        start = half * TH
        lo = start - PRE - BLK
        NTOT = PRE + BLK + TH
        xs = np.zeros((NTOT, D), np.float32)
        src_lo = max(lo, 0)
        xs[src_lo - lo:] = x[b, src_lo:start + TH]
        xT = np.ascontiguousarray(xs.T).reshape(8, 128, NTOT)
        memT = np.ascontiguousarray(mem[b].T).reshape(8, 128, NMEM)
        in_maps.append({"xT": xT, "memT": memT, "wslots": ws, "vecs": vecs, "cst": csts[half]})
    res = run_bass_kernel_spmd(nc, in_maps, core_ids=list(range(2 * B)), trace=trace)
    out = np.zeros((B, T, D), np.float32)
    for c in range(2 * B):
        b, half = c // 2, c % 2
        o = np.asarray(res.results[c]["outT"]).reshape(D, TH)
        out[b, half * TH:(half + 1) * TH] = o.T
    return out, res


def kernel(**inputs):
    out, _ = run_cores(inputs)
    return out
```

```python
import numpy as np
from contextlib import ExitStack
import concourse.bass as bass
import concourse.mybir as mybir
from concourse.bass_utils import run_bass_kernel_spmd

F32 = mybir.dt.float32
BF16 = mybir.dt.bfloat16
AF = mybir.ActivationFunctionType
ALU = mybir.AluOpType

D = 1024
NMEM = 256
DFF = 2816
NPAIR = 22
EPS = 1e-6
NEG = -30000.0
GB = 4
BLK = 128
STAGES = ('mixer', 'ca', 'ffn')
STOPAT = None


class StopBuild(Exception):
    pass


def ckpt(name):
    if STOPAT == name:
        raise StopBuild(name)
RING = 12


class Buf:
    __slots__ = ("t", "w", "r", "name")

    def __init__(self, t, name=None):
        self.t = t
        self.w = None
        self.r = {}
        self.name = name


class MB:
    def __init__(self, t, n, name):
        self.t = t
        self.c = [Buf(t, f"{name}.{i}") for i in range(n)]
        self.name = name


class Prog:
    ENGS = ("pe", "act", "dve", "pool", "sp")

    def __init__(self, nc, es):
        self.nc, self.es = nc, es
        self.items = {e: [] for e in self.ENGS}
        self.clk = {e: {} for e in self.ENGS}
        self.cnt = {e: 0 for e in self.ENGS}
        self.dcnt = {}
        self.nbuf = 0
        self.finals = []

    def sb(self, shape, dt, name=None):
        self.nbuf += 1
        nm = f"{name or 'b'}_{self.nbuf}"
        t = self.es.enter_context(self.nc.sbuf_tensor(nm, list(shape), dt))
        return Buf(t, nm)

    def mb(self, shape, dt, name):
        b = self.sb(shape, dt, name)
        return MB(b.t, shape[1], b.name)

    def ps(self, shape, dt, name=None):
        self.nbuf += 1
        nm = f"{name or 'p'}_{self.nbuf}"
        t = self.es.enter_context(self.nc.psum_tensor(nm, list(shape), dt))
        return Buf(t, nm)

    def _resolve(self, eng, r, w):
        clk = self.clk[eng]
        need = {}

        def add(kind, ev):
            key, val, snap = ev
            if key == eng and eng == "pe":
                return
            if clk.get(key, 0) >= val:
                return
            if key not in need or need[key][0] < val:
                need[key] = (val, snap)

        for b in r:
            if b.w is not None:
                add("raw", b.w)
        for b in w:
            if b.w is not None:
                add("waw", b.w)
            for ev in b.r.values():
                add("war", ev)
        for key, (val, snap) in need.items():
            if clk.get(key, 0) >= val:
                continue
            self.items[eng].append(("wait", key, val))
            for k2, v2 in snap.items():
                if clk.get(k2, 0) < v2:
                    clk[k2] = v2

    def _commit(self, ev, r, w):
        key = ev[0]
        for b in r:
            old = b.r.get(key)
            if old is None or old[1] < ev[1]:
                b.r[key] = ev
        for b in w:
            b.w = ev
            b.r = {}

    def op(self, eng, fn, r=(), w=()):
        self._resolve(eng, r, w)
        self.cnt[eng] += 1
        val = self.cnt[eng]
        snap = dict(self.clk[eng])
        snap[eng] = val
        ev = (eng, val, snap)
        self.items[eng].append(("op", fn, eng, 1))
        self._commit(ev, r, w)
        return ev

    def dma(self, q, fn, semkey, r=(), w=()):
        self._resolve(q, r, w)
        self.dcnt[semkey] = self.dcnt.get(semkey, 0) + 16
        val = self.dcnt[semkey]
        snap = dict(self.clk[q])
        snap[semkey] = val
        ev = (semkey, val, snap)
        self.items[q].append(("op", fn, semkey, 16))
        self._commit(ev, r, w)
        return ev

    def emit(self):
        nc, es = self.nc, self.es
        for ev in self.finals:
            if self.clk["sp"].get(ev[0], 0) < ev[1]:
                self.items["sp"].append(("wait", ev[0], ev[1]))
                self.clk["sp"][ev[0]] = ev[1]
        keys = []
        for e in self.ENGS:
            for it in self.items[e]:
                k = it[1] if it[0] == "wait" else it[2]
                if k not in keys:
                    keys.append(k)
        sems = {k: es.enter_context(nc.semaphore(f"sem{i}")) for i, k in enumerate(keys)}
        self.nsem = len(sems)
        block = es.enter_context(nc.Block())
        items = self.items

        def runner(name):
            def f(e):
                pend = []
                for it in items[name]:
                    if it[0] == "wait":
                        pend.append(it)
                        continue
                    is_dma = it[3] == 16
                    attach = None
                    if pend and not is_dma:
                        attach = pend.pop()
                    for w_ in pend:
                        e.wait_ge(sems[w_[1]], w_[2])
                    pend = []
                    ins = it[1](e)
                    if attach is not None:
                        ins = ins._wait_ge(sems[attach[1]], attach[2])
                    ins.then_inc(sems[it[2]], it[3])
                for w_ in pend:
                    e.wait_ge(sems[w_[1]], w_[2])
            return f

        block.tensor(runner("pe"))
        block.scalar(runner("act"))
        block.vector(runner("dve"))
        block.gpsimd(runner("pool"))
        block.sync(runner("sp"))


def slot_table():
    names = []
    names += [f"qa{c}" for c in range(4)]
    names += [f"ke{g}" for g in range(2)]
    names += [f"ko{g}" for g in range(2)]
    names += [f"vd{g}" for g in range(2)]
    for nm in ("qh", "fh", "ih", "gh"):
        names += [f"{nm}{h}" for h in range(4)]
    names += [f"wo{c}" for c in range(8)]
    names += [f"cq{c}" for c in range(8)]
    names += [f"co{c}" for c in range(8)]
    names += [f"up{c}" for c in range(44)]
    names += [f"dn{c}_{s}" for c in range(8) for s in range(3)]
    names += [f"ck{c}" for c in range(8)]
    names += [f"cv{c}" for c in range(8)]
    return {n: i for i, n in enumerate(names)}


SLOTS = slot_table()
NSLOT = len(SLOTS)


def _tile_cols(W, cols):
    K = W.shape[0]
    kc = K // 128
    out = np.zeros((128, 1024), np.float32)
    sub = W[:, cols]
    out[:, : kc * 128] = sub.reshape(kc, 128, 128).transpose(1, 0, 2).reshape(128, kc * 128)
    return out


def build_wslots(inp):
    w_in = np.asarray(inp["w_in"][0], np.float32)
    ws = np.zeros((NSLOT, 128, 1024), np.float32)
    ar = np.arange(128)
    for c in range(4):
        ws[SLOTS[f"qa{c}"]] = _tile_cols(w_in, c * 128 + ar)
    for g in range(2):
        kd = _tile_cols(w_in, 512 + g * 64 + (ar % 64)).reshape(128, 8, 128)
        ke = kd.copy(); ke[:, :, 64:] = 0.0
        ko = kd.copy(); ko[:, :, :64] = 0.0
        ws[SLOTS[f"ke{g}"]] = ke.reshape(128, 1024)
        ws[SLOTS[f"ko{g}"]] = ko.reshape(128, 1024)
        ws[SLOTS[f"vd{g}"]] = _tile_cols(w_in, 640 + g * 64 + (ar % 64))
    for h in range(4):
        ws[SLOTS[f"qh{h}"]] = _tile_cols(w_in, 768 + h * 128 + ar)
        ws[SLOTS[f"fh{h}"]] = _tile_cols(w_in, 1280 + h * 128 + ar)
        ws[SLOTS[f"ih{h}"]] = _tile_cols(w_in, 1792 + h * 128 + ar)
        ws[SLOTS[f"gh{h}"]] = _tile_cols(w_in, 2304 + h * 128 + ar)
    w_out = np.asarray(inp["w_out"][0], np.float32)
    cq = np.asarray(inp["ca_wq"][0], np.float32)
    ck = np.asarray(inp["ca_wk"][0], np.float32)
    cv = np.asarray(inp["ca_wv"][0], np.float32)
    co = np.asarray(inp["ca_wo"][0], np.float32)
    for c in range(8):
        ws[SLOTS[f"wo{c}"]] = _tile_cols(w_out, c * 128 + ar)
        ws[SLOTS[f"cq{c}"]] = _tile_cols(cq, c * 128 + ar)
        ws[SLOTS[f"co{c}"]] = _tile_cols(co, c * 128 + ar)
        ws[SLOTS[f"ck{c}"]] = _tile_cols(ck, c * 128 + ar)
        ws[SLOTS[f"cv{c}"]] = _tile_cols(cv, c * 128 + ar)
    up = np.asarray(inp["ffn_w_up"][0], np.float32)
    for c in range(44):
        ws[SLOTS[f"up{c}"]] = _tile_cols(up, c * 128 + ar)
    dn = np.asarray(inp["ffn_w_down"][0], np.float32)
    for c in range(8):
        for s in range(3):
            k0 = s * 8 * 128
            k1 = min(DFF, k0 + 1024)
            ws[SLOTS[f"dn{c}_{s}"]] = _tile_cols(dn[k0:k1], c * 128 + ar)
    return ws


V_MIXPRE, V_MIXPOST, V_CAPRE, V_MEMN, V_CAPOST, V_FFNPRE, V_FFNPOST = 0, 8, 16, 24, 32, 40, 48
V_ONW = 56
V_LB = 57
V_CW = 65
V_CB = V_CW + 132
NVEC = V_CB + 44


def build_vecs(inp):
    v = np.zeros((128, NVEC), np.float32)

    def col8(a):
        return np.asarray(a, np.float32).reshape(8, 128).T

    v[:, V_MIXPRE:V_MIXPRE + 8] = col8(inp["mix_pre_norm"][0])
    v[:, V_MIXPOST:V_MIXPOST + 8] = col8(inp["mix_post_norm"][0])
    v[:, V_CAPRE:V_CAPRE + 8] = col8(inp["ca_pre_norm"][0])
    v[:, V_MEMN:V_MEMN + 8] = col8(inp["mem_norm"][0])
    v[:, V_CAPOST:V_CAPOST + 8] = col8(inp["ca_post_norm"][0])
    v[:, V_FFNPRE:V_FFNPRE + 8] = col8(inp["ffn_pre_norm"][0])
    v[:, V_FFNPOST:V_FFNPOST + 8] = col8(inp["ffn_post_norm"][0])
    v[:, V_ONW] = np.asarray(inp["hgrn_out_norm"][0], np.float32)
    lb = np.asarray(inp["hgrn_lb_logits"], np.float32)
    v[:, V_LB:V_LB + 4] = lb[0].reshape(4, 128).T
    v[:, V_LB + 4:V_LB + 8] = lb[1].reshape(4, 128).T
    cw = np.asarray(inp["ffn_conv_w"][0], np.float32)
    for t in range(3):
        v[:, V_CW + t * 44:V_CW + (t + 1) * 44] = cw[t].reshape(44, 128).T
    v[:, V_CB:V_CB + 44] = np.asarray(inp["ffn_conv_b"][0], np.float32).reshape(44, 128).T
    return v


def build_cst(half, sinks):
    c = np.zeros((128, 8, 512), np.float32)
    k = np.arange(128)[:, None]
    q = np.arange(128)[None, :]
    c[:, 0, 0:128] = np.eye(128, dtype=np.float32)
    mc = np.where(k <= q, 0.0, NEG).astype(np.float32)
    mp = np.where(k > q, 0.0, NEG).astype(np.float32)
    cm = (k <= q).astype(np.float32)
    c[:, 1, :] = np.tile(mc, (1, 4))
    c[:, 2, :] = np.tile(mp, (1, 4))
    c[:, 3, :] = np.tile(cm, (1, 4))
    rm = np.ones((128, 512), np.float32)
    rm[:, ::128] = 0.0
    c[:, 4, :] = rm
    if half == 0:
        c[:, 5, :] = NEG
        c[:, 0, 128] = 0.0
    else:
        c[:, 5, :] = c[:, 2, :]
        c[:, 0, 128] = 1.0
    sr = np.repeat(np.asarray(sinks, np.float32), 128)
    c[0, 6, :] = sr[0:512]
    c[0, 7, :] = sr[512:1024]
    return c


def build_program(TH, PRE):
    assert TH % 512 == 0 and PRE % 512 == 0
    NTOT = PRE + BLK + TH
    nc = bass.Bass("TRN2", target_bir_lowering=False)
    xT = nc.dram_tensor("xT", [8, 128, NTOT], F32, kind="ExternalInput").ap()
    memT = nc.dram_tensor("memT", [8, 128, NMEM], F32, kind="ExternalInput").ap()
    wsl = nc.dram_tensor("wslots", [NSLOT, 128, 1024], F32, kind="ExternalInput").ap()
    vecs_d = nc.dram_tensor("vecs", [128, NVEC], F32, kind="ExternalInput").ap()
    cst_d = nc.dram_tensor("cst", [128, 8, 512], F32, kind="ExternalInput").ap()
    outT = nc.dram_tensor("outT", [8, 128, TH], F32, kind="ExternalOutput").ap()

    es = ExitStack()
    P = Prog(nc, es)
    NMAX = GB * BLK

    stage = P.mb([128, 8, NMAX], F32, "stage")
    cstf = stage
    vecs = P.sb([128, NVEC], F32, "vecs")
    P.dma("sp", lambda e: e.dma_start(out=vecs.t[:], in_=vecs_d), "c0", w=[vecs])
    P.dma("sp", lambda e: e.dma_start(out=cstf.t[:], in_=cst_d), "c0", w=cstf.c)

    ident = P.sb([128, 128], BF16, "ident")
    ones = P.sb([128, 128], BF16, "ones")
    maskc = P.sb([128, 512], BF16, "maskc")
    maskp = P.sb([128, 512], BF16, "maskp")
    maskp0 = P.sb([128, 512], BF16, "maskp0")
    cmask = P.sb([128, 512], F32, "cmask")
    rmask = P.sb([128, 512], F32, "rmask")
    esink = P.sb([1, 1024], BF16, "esink")
    hflag = P.sb([128, 1], F32, "hflag")
    lbA = P.sb([128, 4], F32, "lbA")
    lbB = P.sb([128, 4], F32, "lbB")
    lbnB = P.sb([128, 4], F32, "lbnB")
    lbt = P.sb([128, 4], F32, "lbt")

    P.op("dve", lambda e: e.tensor_copy(ident.t[:], cstf.t[:, 0, 0:128]), r=cstf.c, w=[ident])
    P.op("dve", lambda e: e.memset(ones.t[:], 1.0), w=[ones])
    P.op("dve", lambda e: e.tensor_copy(maskc.t[:], cstf.t[:, 1, :]), r=cstf.c, w=[maskc])
    P.op("dve", lambda e: e.tensor_copy(maskp.t[:], cstf.t[:, 2, :]), r=cstf.c, w=[maskp])
    P.op("dve", lambda e: e.tensor_copy(cmask.t[:], cstf.t[:, 3, :]), r=cstf.c, w=[cmask])
    P.op("dve", lambda e: e.tensor_copy(rmask.t[:], cstf.t[:, 4, :]), r=cstf.c, w=[rmask])
    P.op("dve", lambda e: e.tensor_copy(maskp0.t[:], cstf.t[:, 5, :]), r=cstf.c, w=[maskp0])
    P.op("dve", lambda e: e.tensor_copy(hflag.t[:], cstf.t[:, 0, 128:129]), r=cstf.c, w=[hflag])
    P.op("act", lambda e: e.activation(out=esink.t[0:1, :].rearrange("p (g n) -> p g n", g=2),
                                       in_=cstf.t[0:1, 6:8, :], func=AF.Exp), r=cstf.c, w=[esink])
    P.op("dve", lambda e: e.tensor_tensor(lbt.t[:], vecs.t[:, V_LB + 4:V_LB + 8], vecs.t[:, V_LB:V_LB + 4],
                                          ALU.subtract), r=[vecs], w=[lbt])
    P.op("act", lambda e: e.activation(out=lbt.t[:], in_=lbt.t[:], func=AF.Exp), r=[lbt], w=[lbt])
    P.op("act", lambda e: e.activation(out=lbt.t[:], in_=lbt.t[:], func=AF.Ln, bias=1.0, scale=1.0), r=[lbt], w=[lbt])
    P.op("act", lambda e: e.activation(out=lbt.t[:], in_=lbt.t[:], func=AF.Exp, scale=-1.0), r=[lbt], w=[lbt])
    P.op("dve", lambda e: e.tensor_scalar(lbA.t[:], lbt.t[:], 0.5, 0.5, ALU.mult, ALU.add), r=[lbt], w=[lbA])
    P.op("dve", lambda e: e.tensor_scalar(lbB.t[:], lbt.t[:], -0.5, 0.5, ALU.mult, ALU.add), r=[lbt], w=[lbB])
    P.op("dve", lambda e: e.tensor_scalar(lbnB.t[:], lbt.t[:], 0.5, -0.5, ALU.mult, ALU.add), r=[lbt], w=[lbnB])

    try:
        ckpt('none')
    except StopBuild:
        pass
    banks = [P.ps([128, 512], F32, f"bank{i}") for i in range(5)]
    lbank = [P.ps([128, 512], F32, f"lbank{i}") for i in range(2)]
    tbank = [P.ps([128, 1024], BF16, "tbank")]
    bstate = {"i": 0, "t": 0}

    def bank():
        b = banks[bstate["i"] % len(banks)]
        bstate["i"] += 1
        return b

    tbh = [Buf(tbank[0].t, "tbh0"), Buf(tbank[0].t, "tbh1")]

    def tbk():
        bstate["t"] += 1
        return tbh[0], 0

    ring = [P.sb([128, 1024], BF16, f"ring{i}") for i in range(RING)]
    wseq = []
    wstate = {"next": 0, "issued": 0}

    def wnext(name):
        i = wstate["next"]
        assert wseq[i] == SLOTS[name], (i, name)
        while wstate["issued"] < min(len(wseq), i + RING - 2):
            j = wstate["issued"]
            rb = ring[j % RING]
            sid = wseq[j]
            P.dma("pool", lambda e, rb=rb, sid=sid: e.dma_start(out=rb.t[:], in_=wsl[sid]),
                  ("ring", j % RING), w=[rb])
            wstate["issued"] += 1
        wstate["next"] += 1
        return ring[i % RING]

    groups = []
    n = 0
    for gi in range(PRE // NMAX):
        groups.append(("pre", n, GB, None, gi == PRE // NMAX - 1))
        n += NMAX
    groups.append(("halo", n, 1, None, False))
    n += BLK
    for gi in range(TH // NMAX):
        groups.append(("main", n, GB, gi * NMAX, False))
        n += NMAX
    assert n == NTOT

    mem_names = [f"ck{c}" for c in range(8)] + [f"cv{c}" for c in range(8)]
    full_names = ([f"qa{c}" for c in range(4)] + ["ke0", "ko0", "ke1", "ko1"]
                  + [f"fh{h}" for h in range(4)] + [f"qh{h}" for h in range(4)] + [f"gh{h}" for h in range(4)]
                  + ["vd0", "vd1", "ih0", "ih1", "ih2", "ih3"]
                  + [f"wo{c}" for c in range(8)] + [f"cq{c}" for c in range(8)] + [f"co{c}" for c in range(8)])
    for j in range(NPAIR):
        full_names += [f"up{j}", f"up{NPAIR + j}"]
    full_names += [f"dn{c}_{s}" for c in range(8) for s in range(3)]
    pre_names = [f"fh{h}" for h in range(4)] + ["ih0", "ih1", "ih2", "ih3"]
    pre_last_names = ["ke0", "ko0", "ke1", "ko1"] + [f"fh{h}" for h in range(4)] + ["vd0", "vd1", "ih0", "ih1", "ih2", "ih3"]
    wseq.extend(SLOTS[nm] for nm in mem_names)
    for (kind, n0, nblk, ooff, last) in groups:
        if kind == "pre":
            wseq.extend(SLOTS[nm] for nm in (pre_last_names if last else pre_names))
        else:
            wseq.extend(SLOTS[nm] for nm in full_names)

    Xb = [P.mb([128, 8, NMAX], F32, f"X{i}") for i in range(2)]
    hT = P.mb([128, 8, NMAX], BF16, "hT")
    sq = P.mb([128, 8, NMAX], BF16, "sq")
    rstd = P.sb([128, NMAX], F32, "rstd")
    lnt = P.sb([128, NMAX], F32, "lnt")
    tmpn = [lnt, P.sb([128, NMAX], F32, "tmpn1")]
    S = [P.sb([128, 128], F32, f"S{h}") for h in range(4)]
    for h in range(4):
        P.op("pool", lambda e, h=h: e.memset(S[h].t[:], 0.0), w=[S[h]])
    kT = [[P.sb([128, BLK + NMAX], BF16, f"kT{g}_{par}") for par in range(2)] for g in range(2)]
    Vd = [P.sb([128, GB + 1, 128], BF16, f"Vd{g}") for g in range(2)]
    for g in range(2):
        for par in range(2):
            P.op("pool", lambda e, g=g, par=par: e.memset(kT[g][par].t[:], 0.0), w=[kT[g][par]])
        P.op("pool", lambda e, g=g: e.memset(Vd[g].t[:], 0.0), w=[Vd[g]])
    uh = P.sb([128, 44, 2], F32, "uh")
    P.op("pool", lambda e: e.memset(uh.t[:], 0.0), w=[uh])
    KmT = P.sb([128, 8, NMEM], BF16, "KmT")
    Vm = P.sb([128, 2, D], BF16, "Vm")

    qaT = [P.sb([128, NMAX], BF16, f"qaT{c}") for c in range(4)]
    attnT = P.mb([128, 4, NMAX], BF16, "attnT")
    recT = [P.sb([128, NMAX], BF16, f"recT{h}") for h in range(4)]
    th = [P.sb([128, NMAX], F32, f"th{h}") for h in range(4)]
    qs = [P.sb([128, NMAX], BF16, f"qs{h}") for h in range(4)]
    gs = [P.sb([128, NMAX], BF16, f"gs{h}") for h in range(4)]
    vtok = [P.sb([128, GB, 128], BF16, f"vtok{h}") for h in range(4)]
    gg = [P.sb([128, NMAX], F32, f"gg{i}") for i in range(2)]
    bc = [P.sb([128, NMAX], F32, f"bc{i}") for i in range(2)]
    kk = [P.sb([128, NMAX], F32, f"kk{i}") for i in range(2)]
    qt = [P.sb([128, NMAX], BF16, f"qt{h}") for h in range(4)]
    kt = [P.sb([128, NMAX], BF16, f"kt{h}") for h in range(4)]
    ktok = [P.sb([128, GB, 128], BF16, f"ktok{h}") for h in range(4)]
    Am = [P.sb([128, NMAX], BF16, f"Am{h}") for h in range(4)]
    cr = [P.sb([128, GB], F32, f"cr{h}") for h in range(4)]
    dS = [P.sb([128, GB], F32, f"dS{h}") for h in range(4)]
    dK = [P.sb([128, GB], F32, f"dK{h}") for h in range(4)]
    eC = [P.sb([128, GB], F32, f"eC{h}") for h in range(4)]
    Sb = [[P.sb([128, 128], BF16, f"Sb{h}_{i}") for i in range(2)] for h in range(4)]
    Usb = th
    osq = P.sb([128, NMAX], BF16, "osq")
    r1 = gg[1]
    Pp = [P.sb([128, 512], BF16, f"Pp{i}") for i in range(2)]
    Pc = [P.sb([128, 512], BF16, f"Pc{i}") for i in range(2)]
    rden = [P.sb([128, 512], F32, f"rden{i}") for i in range(2)]
    qcT = qt + kt
    _pt = qaT + Am
    PT = [[_pt[2 * h + m] for m in range(2)] for h in range(4)]
    ocT = recT + gs
    rdc = bc
    aT = (qaT + Am + qt + kt + recT + gs)[:NPAIR]
    ug = [P.sb([128, NMAX + 2], F32, f"ug{i}") for i in range(3)]
    uv = [P.sb([128, NMAX + 2], F32, f"uv{i}") for i in range(3)]
    cg = [th[0], th[1], gg[0]]
    cv = [th[2], th[3], gg[1]]

    def mm(out_b, out_ap, lhs_b, lhs_ap, rhs_b, rhs_ap, start, stop):
        P.op("pe", lambda e: e.matmul(out_ap, lhs_ap, rhs_ap, start=start, stop=stop),
             r=[lhs_b, rhs_b], w=[out_b])

    def rstd_from(psb, N, dim):
        P.op("act", lambda e: e.activation(out=lnt.t[:, :N], in_=psb.t[:, :N], func=AF.Ln, bias=EPS,
                                           scale=1.0 / dim), r=[psb], w=[lnt])
        P.op("act", lambda e: e.activation(out=rstd.t[:, :N], in_=lnt.t[:, :N], func=AF.Exp, scale=-0.5),
             r=[lnt], w=[rstd])

    def prenorm(X, N, gcol):
        for c in range(8):
            P.op("act", lambda e, c=c: e.activation(out=sq.t[:, c, :N], in_=X.t[:, c, :N], func=AF.Square),
                 r=[X.c[c]], w=[sq.c[c]])
        pb = bank()
        for c in range(8):
            mm(pb, pb.t[:, :N], ones, ones.t[:], sq.c[c], sq.t[:, c, :N], c == 0, c == 7)
        rstd_from(pb, N, D)
        for c in range(8):
            P.op("dve", lambda e, c=c: e.scalar_tensor_tensor(hT.t[:, c, :N], X.t[:, c, :N],
                                                              vecs.t[:, gcol + c:gcol + c + 1], rstd.t[:, :N],
                                                              ALU.mult, ALU.mult), r=[X.c[c], vecs, rstd], w=[hT.c[c]])

    def proj_fm(name, rhs_list, N, kchunks=8):
        wb = wnext(name)
        pb = bank()
        for k in range(kchunks):
            rb, rap = rhs_list[k]
            mm(pb, pb.t[:, :N], wb, wb.t[:, k * 128:(k + 1) * 128], rb, rap, k == 0, k == kchunks - 1)
        return pb

    def postnorm_residual(X, N, gcol, pbs):
        for c in range(8):
            pb = pbs[c]()
            P.op("act", lambda e, c=c, pb=pb: e.activation(out=stage.t[:, c, :N], in_=pb.t[:, :N], func=AF.Copy),
                 r=[pb], w=[stage.c[c]])
            P.op("act", lambda e, c=c, pb=pb: e.activation(out=sq.t[:, c, :N], in_=pb.t[:, :N], func=AF.Square),
                 r=[pb], w=[sq.c[c]])
        pb2 = bank()
        for c in range(8):
            mm(pb2, pb2.t[:, :N], ones, ones.t[:], sq.c[c], sq.t[:, c, :N], c == 0, c == 7)
        rstd_from(pb2, N, D)
        for c in range(8):
            eng = "pool" if c % 2 == 1 else "dve"
            P.op(eng, lambda e, c=c: e.tensor_tensor(stage.t[:, c, :N], stage.t[:, c, :N], rstd.t[:, :N], ALU.mult),
                 r=[stage.c[c], rstd], w=[stage.c[c]])
        for c in range(8):
            P.op("dve", lambda e, c=c: e.scalar_tensor_tensor(X.t[:, c, :N], stage.t[:, c, :N],
                                                              vecs.t[:, gcol + c:gcol + c + 1], X.t[:, c, :N],
                                                              ALU.mult, ALU.add),
                 r=[stage.c[c], vecs, X.c[c]], w=[X.c[c]])

    hT_chunks = lambda N: [(hT.c[k], hT.t[:, k, :N]) for k in range(8)]

    def do_memkv():
        memX = stage
        P.dma("sp", lambda e: e.dma_start(out=memX.t[:, :, :NMEM], in_=memT.rearrange("c p n -> p c n")),
              "memx", w=memX.c)
        prenorm(memX, NMEM, V_MEMN)
        for c in range(8):
            pb = proj_fm(f"ck{c}", hT_chunks(NMEM), NMEM)
            P.op("act", lambda e, c=c, pb=pb: e.activation(out=KmT.t[:, c, :], in_=pb.t[:, :NMEM], func=AF.Copy),
                 r=[pb], w=[KmT])
        for c in range(8):
            wb = wnext(f"cv{c}")
            pb = bank()
            for mh in range(2):
                for k in range(8):
                    mm(pb, pb.t[:, mh * 128:(mh + 1) * 128], hT.c[k], hT.t[:, k, mh * 128:(mh + 1) * 128],
                       wb, wb.t[:, k * 128:(k + 1) * 128], k == 0, k == 7)
            P.op("act", lambda e, c=c, pb=pb: e.activation(
                out=Vm.t[:, :, c * 128:(c + 1) * 128],
                in_=pb.t[:, 0:256].rearrange("p (m d) -> p m d", m=2), func=AF.Copy), r=[pb], w=[Vm])

    def load_x(gi):
        kind, n0, nblk, ooff, last = groups[gi]
        N = nblk * BLK
        X = Xb[gi % 2]
        P.dma("sp", lambda e: e.dma_start(out=X.t[:, :, :N], in_=xT[:, :, n0:n0 + N].rearrange("c p n -> p c n")),
              ("x", gi % 2), w=X.c)

    class _V:
        def __init__(self, t, k):
            self.t3, self.k = t, k

        def __getitem__(self, idx):
            p, f = idx
            return self.t3[p, self.k, f]

    hsets = [(gg[0], gg[0].t, bc[0], bc[0].t, kk[0], kk[0].t),
             (gg[1], gg[1].t, bc[1], bc[1].t, kk[1], kk[1].t),
             (stage.c[0], _V(stage.t, 0), stage.c[1], _V(stage.t, 1), stage.c[2], _V(stage.t, 2)),
             (stage.c[3], _V(stage.t, 3), stage.c[4], _V(stage.t, 4), stage.c[5], _V(stage.t, 5))]

    def hgrn_prep_all(N, nblk, full):
        HS = range(4)
        gB = [hsets[h][0] for h in HS]; gT = [hsets[h][1] for h in HS]
        bB = [hsets[h][2] for h in HS]; bT = [hsets[h][3] for h in HS]
        kB = [hsets[h][4] for h in HS]; kT_ = [hsets[h][5] for h in HS]
        bc3 = [bT[h][:, :N].rearrange("p (b t) -> p b t", t=BLK) for h in HS]
        for h in HS:
            P.op("act", lambda e, h=h: e.activation(out=gT[h][:, :N], in_=th[h].t[:, :N], func=AF.Ln,
                                                    bias=lbA.t[:, h:h + 1], scale=lbB.t[:, h:h + 1]),
                 r=[th[h], lbA, lbB], w=[gB[h]])
        for h in HS:
            P.op("dve", lambda e, h=h: e.tensor_tensor_scan(bT[h][:, :N], rmask.t[:, :N], gT[h][:, :N], 0.0,
                                                            ALU.mult, ALU.add), r=[rmask, gB[h]], w=[bB[h]])
        for h in HS:
            P.op("act", lambda e, h=h: e.activation(out=kT_[h][:, :N], in_=th[h].t[:, :N], func=AF.Identity,
                                                    bias=lbB.t[:, h:h + 1], scale=lbnB.t[:, h:h + 1]),
                 r=[th[h], lbnB, lbB], w=[kB[h]])
        for h in HS:
            P.op("pool", lambda e, h=h: e.tensor_copy(cr[h].t[:, :nblk], bc3[h][:, :, 63]), r=[bB[h]], w=[cr[h]])
            P.op("act", lambda e, h=h: e.activation(out=dS[h].t[:, :nblk], in_=bc3[h][:, :, 127], func=AF.Exp),
                 r=[bB[h]], w=[dS[h]])
        for h in HS:
            P.op("dve", lambda e, h=h: e.tensor_tensor(bc3[h], bc3[h],
                                                       cr[h].t[:, :nblk].rearrange("p (b o) -> p b o", o=1)
                                                       .to_broadcast([128, nblk, BLK]), ALU.subtract),
                 r=[bB[h], cr[h]], w=[bB[h]])
        for h in HS:
            P.op("act", lambda e, h=h: e.activation(out=gT[h][:, :N], in_=bT[h][:, :N], func=AF.Exp, scale=-1.0),
                 r=[bB[h]], w=[gB[h]])
        for h in HS:
            P.op("dve", lambda e, h=h: e.tensor_tensor(kt[h].t[:, :N], kT_[h][:, :N], gT[h][:, :N], ALU.mult),
                 r=[kB[h], gB[h]], w=[kt[h]])
        for h in HS:
            P.op("act", lambda e, h=h: e.activation(out=dK[h].t[:, :nblk], in_=bc3[h][:, :, 127], func=AF.Exp),
                 r=[bB[h]], w=[dK[h]])
            if full:
                P.op("act", lambda e, h=h: e.activation(out=eC[h].t[:, :nblk], in_=cr[h].t[:, :nblk], func=AF.Exp),
                     r=[cr[h]], w=[eC[h]])
        if full:
            for h in HS:
                P.op("act", lambda e, h=h: e.activation(out=kT_[h][:, :N], in_=bT[h][:, :N], func=AF.Exp),
                     r=[bB[h]], w=[kB[h]])
            for h in HS:
                P.op("dve", lambda e, h=h: e.scalar_tensor_tensor(qt[h].t[:, :N], qs[h].t[:, :N],
                                                                  float(128 ** -0.5), kT_[h][:, :N],
                                                                  ALU.mult, ALU.mult),
                     r=[qs[h], kB[h]], w=[qt[h]])

    def hgrn_core(N, nblk, full):
        for h in range(4):
            tb, to = tbk()
            for b in range(nblk):
                P.op("pe", lambda e, tb=tb, to=to, h=h, b=b: e.transpose(
                    tb.t[:, to + b * 128:to + (b + 1) * 128], kt[h].t[:, b * 128:(b + 1) * 128], ident.t[:]),
                     r=[kt[h], ident], w=[tb])
            P.op("act", lambda e, tb=tb, to=to, h=h: e.activation(
                out=ktok[h].t[:, :nblk, :], in_=tb.t[:, to:to + N].rearrange("p (b f) -> p b f", f=128),
                func=AF.Copy), r=[tb], w=[ktok[h]])
        for h in range(4):
            ub = bank()
            for b in range(nblk):
                mm(ub, ub.t[:, b * 128:(b + 1) * 128], ktok[h], ktok[h].t[:, b, :], vtok[h], vtok[h].t[:, b, :],
                   True, True)
            P.op("act", lambda e, ub=ub, h=h: e.activation(out=Usb[h].t[:, :N], in_=ub.t[:, :N], func=AF.Copy),
                 r=[ub], w=[Usb[h]])
        if full:
            for h in range(4):
                ab = bank()
                for b in range(nblk):
                    mm(ab, ab.t[:, b * 128:(b + 1) * 128], kt[h], kt[h].t[:, b * 128:(b + 1) * 128],
                       qt[h], qt[h].t[:, b * 128:(b + 1) * 128], True, True)
                P.op("dve", lambda e, ab=ab, h=h: e.tensor_tensor(Am[h].t[:, :N], ab.t[:, :N], cmask.t[:, :N], ALU.mult),
                     r=[ab, cmask], w=[Am[h]])

        def s_update(h, b):
            P.op("act", lambda e: e.activation(out=S[h].t[:], in_=S[h].t[:], func=AF.Identity,
                                               scale=dS[h].t[:, b:b + 1]), r=[S[h], dS[h]], w=[S[h]])
            P.op("dve", lambda e: e.scalar_tensor_tensor(S[h].t[:], Usb[h].t[:, b * 128:(b + 1) * 128],
                                                         dK[h].t[:, b:b + 1], S[h].t[:], ALU.mult, ALU.add),
                 r=[Usb[h], dK[h], S[h]], w=[S[h]])

        if not full:
            for b in range(nblk):
                for h in range(4):
                    s_update(h, b)
            return
        for hp in range(2):
            hs = (2 * hp, 2 * hp + 1)
            ob = {h: lbank[i] for i, h in enumerate(hs)}
            for b in range(nblk):
                for h in hs:
                    sb_ = Sb[h][b % 2]
                    P.op("dve", lambda e, h=h, b=b, sb_=sb_: e.tensor_scalar(sb_.t[:], S[h].t[:], eC[h].t[:, b:b + 1],
                                                                             None, ALU.mult),
                         r=[S[h], eC[h]], w=[sb_])
                    o_b = ob[h]
                    mm(o_b, o_b.t[:, b * 128:(b + 1) * 128], vtok[h], vtok[h].t[:, b, :],
                       Am[h], Am[h].t[:, b * 128:(b + 1) * 128], True, False)
                    mm(o_b, o_b.t[:, b * 128:(b + 1) * 128], sb_, sb_.t[:], qt[h], qt[h].t[:, b * 128:(b + 1) * 128],
                       False, True)
                    s_update(h, b)
            for h in hs:
                o_b = ob[h]
                P.op("act", lambda e, o_b=o_b, h=h: e.activation(out=th[h].t[:, :N], in_=o_b.t[:, :N], func=AF.Copy),
                     r=[o_b], w=[th[h]])
        HS = range(4)
        lnB = [hsets[h][0] for h in HS]
        lnT = [hsets[h][1] for h in HS]
        for h in HS:
            P.op("act", lambda e, h=h: e.activation(out=Am[h].t[:, :N], in_=th[h].t[:, :N], func=AF.Square),
                 r=[th[h]], w=[Am[h]])
        pbs_ = []
        for h in HS:
            pb = bank()
            mm(pb, pb.t[:, :N], ones, ones.t[:], Am[h], Am[h].t[:, :N], True, True)
            pbs_.append(pb)
        for h in HS:
            P.op("act", lambda e, h=h, pb=pbs_[h]: e.activation(out=lnT[h][:, :N], in_=pb.t[:, :N], func=AF.Ln,
                                                                 bias=EPS, scale=1.0 / 128), r=[pbs_[h]], w=[lnB[h]])
        for h in HS:
            P.op("act", lambda e, h=h: e.activation(out=lnT[h][:, :N], in_=lnT[h][:, :N], func=AF.Exp, scale=-0.5),
                 r=[lnB[h]], w=[lnB[h]])
        for h in HS:
            P.op("dve", lambda e, h=h: e.tensor_tensor(th[h].t[:, :N], th[h].t[:, :N], lnT[h][:, :N], ALU.mult),
                 r=[th[h], lnB[h]], w=[th[h]])
        for h in HS:
            P.op("dve", lambda e, h=h: e.scalar_tensor_tensor(recT[h].t[:, :N], th[h].t[:, :N],
                                                              vecs.t[:, V_ONW:V_ONW + 1], gs[h].t[:, :N],
                                                              ALU.mult, ALU.mult),
                 r=[th[h], vecs, gs[h]], w=[recT[h]])

    def tokmajor(names, dsts, N, nblk):
        wbs = [wnext(nm) for nm in names]
        pbs = [bank() for _ in names]
        for b in range(nblk):
            for k in range(8):
                for wb, pb in zip(wbs, pbs):
                    mm(pb, pb.t[:, b * 128:(b + 1) * 128], hT.c[k], hT.t[:, k, b * 128:(b + 1) * 128],
                       wb, wb.t[:, k * 128:(k + 1) * 128], k == 0, k == 7)
        for pb, (db, dap) in zip(pbs, dsts):
            P.op("act", lambda e, pb=pb, dap=dap: e.activation(
                out=dap, in_=pb.t[:, :N].rearrange("p (b f) -> p b f", f=128), func=AF.Copy), r=[pb], w=[db])

    def roll_kv(nblk):
        for g in range(2):
            for par in range(2):
                P.op("pool", lambda e, g=g, par=par: e.tensor_copy(kT[g][par].t[:, 0:BLK],
                                                                    kT[g][par].t[:, nblk * BLK:(nblk + 1) * BLK]),
                     r=[kT[g][par]], w=[kT[g][par]])
            P.op("pool", lambda e, g=g: e.tensor_copy(Vd[g].t[:, 0, :], Vd[g].t[:, nblk, :]),
                 r=[Vd[g]], w=[Vd[g]])

    def do_prefix(gi):
        kind, n0, nblk, ooff, last = groups[gi]
        N = nblk * BLK
        X = Xb[gi % 2]
        prenorm(X, N, V_MIXPRE)
        if last:
            for g in range(2):
                for par, nm in enumerate(("ke", "ko")):
                    pb = proj_fm(f"{nm}{g}", hT_chunks(N), N)
                    P.op("act", lambda e, g=g, par=par, pb=pb: e.activation(
                        out=kT[g][par].t[:, BLK:BLK + N], in_=pb.t[:, :N], func=AF.Copy), r=[pb], w=[kT[g][par]])
        for h in range(4):
            pb = proj_fm(f"fh{h}", hT_chunks(N), N)
            P.op("act", lambda e, h=h, pb=pb: e.activation(out=th[h].t[:, :N], in_=pb.t[:, :N], func=AF.Tanh,
                                                           scale=0.5), r=[pb], w=[th[h]])
        if last:
            tokmajor(["vd0", "vd1", "ih0"], [(Vd[0], Vd[0].t[:, 1:1 + nblk, :]), (Vd[1], Vd[1].t[:, 1:1 + nblk, :]),
                                             (vtok[0], vtok[0].t[:, :nblk, :])], N, nblk)
            tokmajor(["ih1", "ih2", "ih3"], [(vtok[h], vtok[h].t[:, :nblk, :]) for h in (1, 2, 3)], N, nblk)
        else:
            tokmajor(["ih0", "ih1"], [(vtok[h], vtok[h].t[:, :nblk, :]) for h in (0, 1)], N, nblk)
            tokmajor(["ih2", "ih3"], [(vtok[h], vtok[h].t[:, :nblk, :]) for h in (2, 3)], N, nblk)
        hgrn_prep_all(N, nblk, False)
        hgrn_core(N, nblk, False)
        if last:
            roll_kv(nblk)

    def do_full(gi):
        kind, n0, nblk, ooff, last = groups[gi]
        N = nblk * BLK
        X = Xb[gi % 2]
        first_main = (kind == "main" and ooff == 0)
        prenorm(X, N, V_MIXPRE)
        for c in range(4):
            pb = proj_fm(f"qa{c}", hT_chunks(N), N)
            P.op("dve", lambda e, c=c, pb=pb: e.tensor_copy(qaT[c].t[:, :N], pb.t[:, :N]), r=[pb], w=[qaT[c]])
        for g in range(2):
            for par, nm in enumerate(("ke", "ko")):
                pb = proj_fm(f"{nm}{g}", hT_chunks(N), N)
                P.op("dve", lambda e, g=g, par=par, pb=pb: e.tensor_copy(kT[g][par].t[:, BLK:BLK + N], pb.t[:, :N]),
                     r=[pb], w=[kT[g][par]])
        for h in range(4):
            pb = proj_fm(f"fh{h}", hT_chunks(N), N)
            P.op("act", lambda e, h=h, pb=pb: e.activation(out=th[h].t[:, :N], in_=pb.t[:, :N], func=AF.Tanh,
                                                           scale=0.5), r=[pb], w=[th[h]])
        for h in range(4):
            pb = proj_fm(f"qh{h}", hT_chunks(N), N)
            P.op("act", lambda e, h=h, pb=pb: e.activation(out=qs[h].t[:, :N], in_=pb.t[:, :N], func=AF.Silu),
                 r=[pb], w=[qs[h]])
        for h in range(4):
            pb = proj_fm(f"gh{h}", hT_chunks(N), N)
            P.op("act", lambda e, h=h, pb=pb: e.activation(out=gs[h].t[:, :N], in_=pb.t[:, :N], func=AF.Silu),
                 r=[pb], w=[gs[h]])
        tokmajor(["vd0", "vd1", "ih0"], [(Vd[0], Vd[0].t[:, 1:1 + nblk, :]), (Vd[1], Vd[1].t[:, 1:1 + nblk, :]),
                                         (vtok[0], vtok[0].t[:, :nblk, :])], N, nblk)
        tokmajor(["ih1", "ih2", "ih3"], [(vtok[h], vtok[h].t[:, :nblk, :]) for h in (1, 2, 3)], N, nblk)
        hgrn_prep_all(N, nblk, True)
        iters = [(b, g) for b in range(nblk) for g in range(2)]
        Sset = [(banks[0], banks[1]), (banks[2], banks[3])]
        ob_, db_ = banks[4], lbank[0]

        def att_scores(i):
            b, g = iters[i]
            sp_, sc_ = Sset[i % 2]
            mprev = maskp0 if (first_main and b == 0) else maskp
            mm(sp_, sp_.t[:, :], ident, ident.t[:], mprev, mprev.t[:], True, False)
            mm(sc_, sc_.t[:, :], ident, ident.t[:], maskc, maskc.t[:], True, False)
            for j in range(4):
                hq = 4 * g + j
                c = hq // 2
                kb = kT[g][j % 2]
                mm(sp_, sp_.t[:, j * 128:(j + 1) * 128], kb, kb.t[:, b * 128:(b + 1) * 128],
                   qaT[c], qaT[c].t[:, b * 128:(b + 1) * 128], False, j == 3)
                mm(sc_, sc_.t[:, j * 128:(j + 1) * 128], kb, kb.t[:, (b + 1) * 128:(b + 2) * 128],
                   qaT[c], qaT[c].t[:, b * 128:(b + 1) * 128], False, j == 3)

        def att_rest(i):
            b, g = iters[i]
            i2 = i % 2
            sp_, sc_ = Sset[i2]
            pp, pc, rd = Pp[i2], Pc[i2], rden[i2]
            P.op("act", lambda e: e.activation(out=pp.t[:], in_=sp_.t[:], func=AF.Exp, scale=0.125), r=[sp_], w=[pp])
            P.op("act", lambda e: e.activation(out=pc.t[:], in_=sc_.t[:], func=AF.Exp, scale=0.125), r=[sc_], w=[pc])
            if i + 1 < len(iters):
                att_scores(i + 1)
            mm(ob_, ob_.t[:], Vd[g], Vd[g].t[:, b, :], pp, pp.t[:], True, False)
            mm(ob_, ob_.t[:], Vd[g], Vd[g].t[:, b + 1, :], pc, pc.t[:], False, True)
            mm(db_, db_.t[:], ones, ones.t[:], pp, pp.t[:], True, False)
            mm(db_, db_.t[:], ones, ones.t[:], pc, pc.t[:], False, False)
            mm(db_, db_.t[:], ones, ones.t[0:1, :], esink, esink.t[0:1, g * 512:(g + 1) * 512], False, True)
            P.op("act", lambda e: e.activation(out=rd.t[:], in_=db_.t[:], func=AF.Ln), r=[db_], w=[rd])
            P.op("act", lambda e: e.activation(out=rd.t[:], in_=rd.t[:], func=AF.Exp, scale=-1.0), r=[rd], w=[rd])
            for half in range(2):
                p0 = half * 64
                P.op("dve", lambda e, half=half, p0=p0: e.tensor_tensor(
                    attnT.t[p0:p0 + 64, 2 * g:2 * g + 2, b * 128:(b + 1) * 128],
                    ob_.t[p0:p0 + 64, :].rearrange("p (i two q) -> p i two q", two=2, q=128)[:, :, half, :],
                    rd.t[p0:p0 + 64, :].rearrange("p (i two q) -> p i two q", two=2, q=128)[:, :, half, :],
                    ALU.mult), r=[ob_, rd], w=[attnT.c[2 * g], attnT.c[2 * g + 1]])

        att_scores(0)
        for i in range(len(iters)):
            att_rest(i)
        roll_kv(nblk)
        hgrn_core(N, nblk, True)
        mixk = [(attnT.c[c], attnT.t[:, c, :N]) for c in range(4)] + [(recT[h], recT[h].t[:, :N]) for h in range(4)]
        postnorm_residual(X, N, V_MIXPOST, [lambda c=c: proj_fm(f"wo{c}", mixk, N) for c in range(8)])

        if 'ca' not in STAGES:
            for c in range(8):
                wnext(f"cq{c}")
            for c in range(8):
                wnext(f"co{c}")
        else:
            do_ca(X, N)
        do_ffn(gi, X, N, kind, ooff, first_main)

    def do_ca(X, N):
        prenorm(X, N, V_CAPRE)
        for c in range(8):
            pb = proj_fm(f"cq{c}", hT_chunks(N), N)
            P.op("dve", lambda e, c=c, pb=pb: e.tensor_copy(qcT[c].t[:, :N], pb.t[:, :N]), r=[pb], w=[qcT[c]])
        for hh in range(4):
            for mh in range(2):
                pb = bank()
                for i, dc in enumerate((2 * hh, 2 * hh + 1)):
                    mm(pb, pb.t[:, :N], KmT, KmT.t[:, dc, mh * 128:(mh + 1) * 128], qcT[dc], qcT[dc].t[:, :N],
                       i == 0, i == 1)
                P.op("act", lambda e, hh=hh, mh=mh, pb=pb: e.activation(out=PT[hh][mh].t[:, :N], in_=pb.t[:, :N],
                                                                        func=AF.Exp, scale=1.0 / 16.0),
                     r=[pb], w=[PT[hh][mh]])
            db_ = bank()
            for mh in range(2):
                mm(db_, db_.t[:, :N], ones, ones.t[:], PT[hh][mh], PT[hh][mh].t[:, :N], mh == 0, mh == 1)
            rd = rdc[hh % 2]
            P.op("act", lambda e, db_=db_, rd=rd: e.activation(out=rd.t[:, :N], in_=db_.t[:, :N], func=AF.Ln),
                 r=[db_], w=[rd])
            P.op("act", lambda e, rd=rd: e.activation(out=rd.t[:, :N], in_=rd.t[:, :N], func=AF.Exp, scale=-1.0),
                 r=[rd], w=[rd])
            for dc in (2 * hh, 2 * hh + 1):
                pb = bank()
                for mh in range(2):
                    mm(pb, pb.t[:, :N], Vm, Vm.t[:, mh, dc * 128:(dc + 1) * 128], PT[hh][mh], PT[hh][mh].t[:, :N],
                       mh == 0, mh == 1)
                P.op("dve", lambda e, dc=dc, pb=pb, rd=rd: e.tensor_tensor(ocT[dc].t[:, :N], pb.t[:, :N], rd.t[:, :N],
                                                                           ALU.mult), r=[pb, rd], w=[ocT[dc]])
        ock = [(ocT[c], ocT[c].t[:, :N]) for c in range(8)]
        postnorm_residual(X, N, V_CAPOST, [lambda c=c: proj_fm(f"co{c}", ock, N) for c in range(8)])

    def do_ffn(gi, X, N, kind, ooff, first_main):
        if 'ffn' not in STAGES:
            for j in range(NPAIR):
                wnext(f"up{j}")
                wnext(f"up{NPAIR + j}")
            for c in range(8):
                for s_ in range(3):
                    wnext(f"dn{c}_{s_}")
            if kind == "main":
                ev = P.dma("sp", lambda e: e.dma_start(out=outT[:, :, ooff:ooff + N].rearrange("c p n -> p c n"),
                                                       in_=X.t[:, :, :N]), ("xo", gi % 2), r=X.c)
                P.finals.append(ev)
            return
        prenorm(X, N, V_FFNPRE)
        for j in range(NPAIR):
            i2 = j % 3
            outs = []
            for (ci, ub, cb) in ((j, ug[i2], cg[i2]), (NPAIR + j, uv[i2], cv[i2])):
                pb = proj_fm(f"up{ci}", hT_chunks(N), N)
                P.op("act", lambda e, ci=ci, pb=pb, cb=cb: e.activation(
                    out=cb.t[:, :N], in_=pb.t[:, :N], func=AF.Identity,
                    bias=vecs.t[:, V_CB + ci:V_CB + ci + 1], scale=vecs.t[:, V_CW + 88 + ci:V_CW + 88 + ci + 1]),
                    r=[pb, vecs], w=[cb])
                P.op("act", lambda e, pb=pb, ub=ub: e.activation(out=ub.t[:, 2:2 + N], in_=pb.t[:, :N], func=AF.Copy),
                     r=[pb], w=[ub])
                if first_main:
                    P.op("pool", lambda e, ci=ci, ub=ub: e.tensor_scalar(ub.t[:, 0:2], uh.t[:, ci, :], hflag.t[:, 0:1],
                                                                         None, ALU.mult), r=[uh, hflag], w=[ub])
                else:
                    P.op("pool", lambda e, ci=ci, ub=ub: e.tensor_copy(ub.t[:, 0:2], uh.t[:, ci, :]), r=[uh], w=[ub])
                P.op("pool", lambda e, ci=ci, ub=ub: e.tensor_copy(uh.t[:, ci, :], ub.t[:, N:N + 2]), r=[ub], w=[uh])
                P.op("dve", lambda e, ci=ci, ub=ub, cb=cb: e.scalar_tensor_tensor(
                    cb.t[:, :N], ub.t[:, 0:N], vecs.t[:, V_CW + ci:V_CW + ci + 1], cb.t[:, :N], ALU.mult, ALU.add),
                    r=[ub, vecs, cb], w=[cb])
                P.op("dve", lambda e, ci=ci, ub=ub, cb=cb: e.scalar_tensor_tensor(
                    cb.t[:, :N], ub.t[:, 1:N + 1], vecs.t[:, V_CW + 44 + ci:V_CW + 44 + ci + 1], cb.t[:, :N],
                    ALU.mult, ALU.add), r=[ub, vecs, cb], w=[cb])
            if kind == "main":
                P.op("act", lambda e, i2=i2: e.activation(out=cg[i2].t[:, :N], in_=cg[i2].t[:, :N],
                                                          func=AF.Gelu_apprx_tanh), r=[cg[i2]], w=[cg[i2]])
                P.op("dve", lambda e, i2=i2, j=j: e.tensor_tensor(aT[j].t[:, :N], cg[i2].t[:, :N], cv[i2].t[:, :N],
                                                                  ALU.mult), r=[cg[i2], cv[i2]], w=[aT[j]])
        if kind == "main":
            def dn_chunk(c):
                pb = bank()
                for s in range(3):
                    wb = wnext(f"dn{c}_{s}")
                    for kq in range(8):
                        jj = s * 8 + kq
                        if jj >= NPAIR:
                            break
                        mm(pb, pb.t[:, :N], wb, wb.t[:, kq * 128:(kq + 1) * 128], aT[jj], aT[jj].t[:, :N],
                           jj == 0, jj == NPAIR - 1)
                return pb
            postnorm_residual(X, N, V_FFNPOST, [lambda c=c: dn_chunk(c) for c in range(8)])
            ev = P.dma("sp", lambda e: e.dma_start(out=outT[:, :, ooff:ooff + N].rearrange("c p n -> p c n"),
                                                   in_=X.t[:, :, :N]), ("xo", gi % 2), r=X.c)
            P.finals.append(ev)
        else:
            for c in range(8):
                for s in range(3):
                    wnext(f"dn{c}_{s}")

    try:
        ckpt('consts')
        do_memkv()
        ckpt('memkv')
        load_x(0)
        for gi in range(len(groups)):
            if gi + 1 < len(groups):
                load_x(gi + 1)
            if groups[gi][0] == "pre":
                do_prefix(gi)
                ckpt('prefix')
            else:
                do_full(gi)
        assert wstate["next"] == len(wseq), (wstate, len(wseq))
    except StopBuild:
        pass
    P.emit()
    es.close()
    global _LAST
    _LAST = dict(P=P, attnT=attnT, recT=recT, X=Xb, hT=hT, qaT=qaT, kT=kT, Vd=Vd, th=th, qs=qs, gs=gs, qt=qt, kt=kt, S=S, vtok=vtok, stage=stage)
    return nc, P


def run_cores(inputs, PRE=None, trace=False):
    x = np.asarray(inputs["x"], np.float32)
    mem = np.asarray(inputs["mem"], np.float32)
    B, T, _ = x.shape
    TH = T // 2
    if PRE is None:
        PRE = TH
    ws = build_wslots(inputs)
    vecs = build_vecs(inputs)
    sinks = np.asarray(inputs["attn_sinks"][0], np.float32)
    csts = [build_cst(0, sinks), build_cst(1, sinks)]
    nc, P = build_program(TH, PRE)
    in_maps = []
    for c in range(2 * B):
        b, half = c // 2, c % 2
        start = half * TH
        lo = start - PRE - BLK
        NTOT = PRE + BLK + TH
        xs = np.zeros((NTOT, D), np.float32)
        src_lo = max(lo, 0)
        xs[src_lo - lo:] = x[b, src_lo:start + TH]
        xT = np.ascontiguousarray(xs.T).reshape(8, 128, NTOT)
        memT = np.ascontiguousarray(mem[b].T).reshape(8, 128, NMEM)
        in_maps.append({"xT": xT, "memT": memT, "wslots": ws, "vecs": vecs, "cst": csts[half]})
    res = run_bass_kernel_spmd(nc, in_maps, core_ids=list(range(2 * B)), trace=trace)
    out = np.zeros((B, T, D), np.float32)
    for c in range(2 * B):
        b, half = c // 2, c % 2
        o = np.asarray(res.results[c]["outT"]).reshape(D, TH)
        out[b, half * TH:(half + 1) * TH] = o.T
    return out, res


def kernel(**inputs):
    out, _ = run_cores(inputs)
    return out
```

```python
import numpy as np
from contextlib import ExitStack
import concourse.bass as bass
import concourse.mybir as mybir
from concourse.bass_utils import run_bass_kernel_spmd

F32 = mybir.dt.float32
BF16 = mybir.dt.bfloat16
AF = mybir.ActivationFunctionType
ALU = mybir.AluOpType

D = 1024
NMEM = 256
DFF = 2816
NPAIR = 22
EPS = 1e-6
NEG = -30000.0
GB = 4
BLK = 128
STAGES = ('mixer', 'ca', 'ffn')
STOPAT = None


class StopBuild(Exception):
    pass


def ckpt(name):
    if STOPAT == name:
        raise StopBuild(name)
RING = 12


class Buf:
    __slots__ = ("t", "w", "r", "name")

    def __init__(self, t, name=None):
        self.t = t
        self.w = None
        self.r = {}
        self.name = name


class MB:
    def __init__(self, t, n, name):
        self.t = t
        self.c = [Buf(t, f"{name}.{i}") for i in range(n)]
        self.name = name


class Prog:
    ENGS = ("pe", "act", "dve", "pool", "sp")

    def __init__(self, nc, es):
        self.nc, self.es = nc, es
        self.items = {e: [] for e in self.ENGS}
        self.clk = {e: {} for e in self.ENGS}
        self.cnt = {e: 0 for e in self.ENGS}
        self.dcnt = {}
        self.nbuf = 0
        self.finals = []

    def sb(self, shape, dt, name=None):
        self.nbuf += 1
        nm = f"{name or 'b'}_{self.nbuf}"
        t = self.es.enter_context(self.nc.sbuf_tensor(nm, list(shape), dt))
        return Buf(t, nm)

    def mb(self, shape, dt, name):
        b = self.sb(shape, dt, name)
        return MB(b.t, shape[1], b.name)

    def ps(self, shape, dt, name=None):
        self.nbuf += 1
        nm = f"{name or 'p'}_{self.nbuf}"
        t = self.es.enter_context(self.nc.psum_tensor(nm, list(shape), dt))
        return Buf(t, nm)

    def _resolve(self, eng, r, w):
        clk = self.clk[eng]
        need = {}

        def add(kind, ev):
            key, val, snap = ev
            if key == eng and eng == "pe":
                return
            if clk.get(key, 0) >= val:
                return
            if key not in need or need[key][0] < val:
                need[key] = (val, snap)

        for b in r:
            if b.w is not None:
                add("raw", b.w)
        for b in w:
            if b.w is not None:
                add("waw", b.w)
            for ev in b.r.values():
                add("war", ev)
        for key, (val, snap) in need.items():
            if clk.get(key, 0) >= val:
                continue
            self.items[eng].append(("wait", key, val))
            for k2, v2 in snap.items():
                if clk.get(k2, 0) < v2:
                    clk[k2] = v2

    def _commit(self, ev, r, w):
        key = ev[0]
        for b in r:
            old = b.r.get(key)
            if old is None or old[1] < ev[1]:
                b.r[key] = ev
        for b in w:
            b.w = ev
            b.r = {}

    def op(self, eng, fn, r=(), w=()):
        self._resolve(eng, r, w)
        self.cnt[eng] += 1
        val = self.cnt[eng]
        snap = dict(self.clk[eng])
        snap[eng] = val
        ev = (eng, val, snap)
        self.items[eng].append(("op", fn, eng, 1))
        self._commit(ev, r, w)
        return ev

    def dma(self, q, fn, semkey, r=(), w=()):
        self._resolve(q, r, w)
        self.dcnt[semkey] = self.dcnt.get(semkey, 0) + 16
        val = self.dcnt[semkey]
        snap = dict(self.clk[q])
        snap[semkey] = val
        ev = (semkey, val, snap)
        self.items[q].append(("op", fn, semkey, 16))
        self._commit(ev, r, w)
        return ev

    def emit(self):
        nc, es = self.nc, self.es
        for ev in self.finals:
            if self.clk["sp"].get(ev[0], 0) < ev[1]:
                self.items["sp"].append(("wait", ev[0], ev[1]))
                self.clk["sp"][ev[0]] = ev[1]
        keys = []
        for e in self.ENGS:
            for it in self.items[e]:
                k = it[1] if it[0] == "wait" else it[2]
                if k not in keys:
                    keys.append(k)
        sems = {k: es.enter_context(nc.semaphore(f"sem{i}")) for i, k in enumerate(keys)}
        self.nsem = len(sems)
        block = es.enter_context(nc.Block())
        items = self.items

        def runner(name):
            def f(e):
                pend = []
                for it in items[name]:
                    if it[0] == "wait":
                        pend.append(it)
                        continue
                    is_dma = it[3] == 16
                    attach = None
                    if pend and not is_dma:
                        attach = pend.pop()
                    for w_ in pend:
                        e.wait_ge(sems[w_[1]], w_[2])
                    pend = []
                    ins = it[1](e)
                    if attach is not None:
                        ins = ins._wait_ge(sems[attach[1]], attach[2])
                    ins.then_inc(sems[it[2]], it[3])
                for w_ in pend:
                    e.wait_ge(sems[w_[1]], w_[2])
            return f

        block.tensor(runner("pe"))
        block.scalar(runner("act"))
        block.vector(runner("dve"))
        block.gpsimd(runner("pool"))
        block.sync(runner("sp"))


def slot_table():
    names = []
    names += [f"qa{c}" for c in range(4)]
    names += [f"ke{g}" for g in range(2)]
    names += [f"ko{g}" for g in range(2)]
    names += [f"vd{g}" for g in range(2)]
    for nm in ("qh", "fh", "ih", "gh"):
        names += [f"{nm}{h}" for h in range(4)]
    names += [f"wo{c}" for c in range(8)]
    names += [f"cq{c}" for c in range(8)]
    names += [f"co{c}" for c in range(8)]
    names += [f"up{c}" for c in range(44)]
    names += [f"dn{c}_{s}" for c in range(8) for s in range(3)]
    names += [f"ck{c}" for c in range(8)]
    names += [f"cv{c}" for c in range(8)]
    return {n: i for i, n in enumerate(names)}


SLOTS = slot_table()
NSLOT = len(SLOTS)


def _tile_cols(W, cols):
    K = W.shape[0]
    kc = K // 128
    out = np.zeros((128, 1024), np.float32)
    sub = W[:, cols]
    out[:, : kc * 128] = sub.reshape(kc, 128, 128).transpose(1, 0, 2).reshape(128, kc * 128)
    return out


def build_wslots(inp):
    w_in = np.asarray(inp["w_in"][0], np.float32)
    ws = np.zeros((NSLOT, 128, 1024), np.float32)
    ar = np.arange(128)
    for c in range(4):
        ws[SLOTS[f"qa{c}"]] = _tile_cols(w_in, c * 128 + ar)
    for g in range(2):
        kd = _tile_cols(w_in, 512 + g * 64 + (ar % 64)).reshape(128, 8, 128)
        ke = kd.copy(); ke[:, :, 64:] = 0.0
        ko = kd.copy(); ko[:, :, :64] = 0.0
        ws[SLOTS[f"ke{g}"]] = ke.reshape(128, 1024)
        ws[SLOTS[f"ko{g}"]] = ko.reshape(128, 1024)
        ws[SLOTS[f"vd{g}"]] = _tile_cols(w_in, 640 + g * 64 + (ar % 64))
    for h in range(4):
        ws[SLOTS[f"qh{h}"]] = _tile_cols(w_in, 768 + h * 128 + ar)
        ws[SLOTS[f"fh{h}"]] = _tile_cols(w_in, 1280 + h * 128 + ar)
        ws[SLOTS[f"ih{h}"]] = _tile_cols(w_in, 1792 + h * 128 + ar)
        ws[SLOTS[f"gh{h}"]] = _tile_cols(w_in, 2304 + h * 128 + ar)
    w_out = np.asarray(inp["w_out"][0], np.float32)
    cq = np.asarray(inp["ca_wq"][0], np.float32)
    ck = np.asarray(inp["ca_wk"][0], np.float32)
    cv = np.asarray(inp["ca_wv"][0], np.float32)
    co = np.asarray(inp["ca_wo"][0], np.float32)
    for c in range(8):
        ws[SLOTS[f"wo{c}"]] = _tile_cols(w_out, c * 128 + ar)
        ws[SLOTS[f"cq{c}"]] = _tile_cols(cq, c * 128 + ar)
        ws[SLOTS[f"co{c}"]] = _tile_cols(co, c * 128 + ar)
        ws[SLOTS[f"ck{c}"]] = _tile_cols(ck, c * 128 + ar)
        ws[SLOTS[f"cv{c}"]] = _tile_cols(cv, c * 128 + ar)
    up = np.asarray(inp["ffn_w_up"][0], np.float32)
    for c in range(44):
        ws[SLOTS[f"up{c}"]] = _tile_cols(up, c * 128 + ar)
    dn = np.asarray(inp["ffn_w_down"][0], np.float32)
    for c in range(8):
        for s in range(3):
            k0 = s * 8 * 128
            k1 = min(DFF, k0 + 1024)
            ws[SLOTS[f"dn{c}_{s}"]] = _tile_cols(dn[k0:k1], c * 128 + ar)
    return ws


V_MIXPRE, V_MIXPOST, V_CAPRE, V_MEMN, V_CAPOST, V_FFNPRE, V_FFNPOST = 0, 8, 16, 24, 32, 40, 48
V_ONW = 56
V_LB = 57
V_CW = 65
V_CB = V_CW + 132
NVEC = V_CB + 44


def build_vecs(inp):
    v = np.zeros((128, NVEC), np.float32)

    def col8(a):
        return np.asarray(a, np.float32).reshape(8, 128).T

    v[:, V_MIXPRE:V_MIXPRE + 8] = col8(inp["mix_pre_norm"][0])
    v[:, V_MIXPOST:V_MIXPOST + 8] = col8(inp["mix_post_norm"][0])
    v[:, V_CAPRE:V_CAPRE + 8] = col8(inp["ca_pre_norm"][0])
    v[:, V_MEMN:V_MEMN + 8] = col8(inp["mem_norm"][0])
    v[:, V_CAPOST:V_CAPOST + 8] = col8(inp["ca_post_norm"][0])
    v[:, V_FFNPRE:V_FFNPRE + 8] = col8(inp["ffn_pre_norm"][0])
    v[:, V_FFNPOST:V_FFNPOST + 8] = col8(inp["ffn_post_norm"][0])
    v[:, V_ONW] = np.asarray(inp["hgrn_out_norm"][0], np.float32)
    lb = np.asarray(inp["hgrn_lb_logits"], np.float32)
    v[:, V_LB:V_LB + 4] = lb[0].reshape(4, 128).T
    v[:, V_LB + 4:V_LB + 8] = lb[1].reshape(4, 128).T
    cw = np.asarray(inp["ffn_conv_w"][0], np.float32)
    for t in range(3):
        v[:, V_CW + t * 44:V_CW + (t + 1) * 44] = cw[t].reshape(44, 128).T
    v[:, V_CB:V_CB + 44] = np.asarray(inp["ffn_conv_b"][0], np.float32).reshape(44, 128).T
    return v


def build_cst(half, sinks):
    c = np.zeros((128, 8, 512), np.float32)
    k = np.arange(128)[:, None]
    q = np.arange(128)[None, :]
    c[:, 0, 0:128] = np.eye(128, dtype=np.float32)
    mc = np.where(k <= q, 0.0, NEG).astype(np.float32)
    mp = np.where(k > q, 0.0, NEG).astype(np.float32)
    cm = (k <= q).astype(np.float32)
    c[:, 1, :] = np.tile(mc, (1, 4))
    c[:, 2, :] = np.tile(mp, (1, 4))
    c[:, 3, :] = np.tile(cm, (1, 4))
    rm = np.ones((128, 512), np.float32)
    rm[:, ::128] = 0.0
    c[:, 4, :] = rm
    if half == 0:
        c[:, 5, :] = NEG
        c[:, 0, 128] = 0.0
    else:
        c[:, 5, :] = c[:, 2, :]
        c[:, 0, 128] = 1.0
    sr = np.repeat(np.asarray(sinks, np.float32), 128)
    c[0, 6, :] = sr[0:512]
    c[0, 7, :] = sr[512:1024]
    return c


def build_program(TH, PRE):
    assert TH % 512 == 0 and PRE % 512 == 0
    NTOT = PRE + BLK + TH
    nc = bass.Bass("TRN2", target_bir_lowering=False)
    xT = nc.dram_tensor("xT", [8, 128, NTOT], F32, kind="ExternalInput").ap()
    memT = nc.dram_tensor("memT", [8, 128, NMEM], F32, kind="ExternalInput").ap()
    wsl = nc.dram_tensor("wslots", [NSLOT, 128, 1024], F32, kind="ExternalInput").ap()
    vecs_d = nc.dram_tensor("vecs", [128, NVEC], F32, kind="ExternalInput").ap()
    cst_d = nc.dram_tensor("cst", [128, 8, 512], F32, kind="ExternalInput").ap()
    outT = nc.dram_tensor("outT", [8, 128, TH], F32, kind="ExternalOutput").ap()

    es = ExitStack()
    P = Prog(nc, es)
    NMAX = GB * BLK

    stage = P.mb([128, 8, NMAX], F32, "stage")
    cstf = stage
    vecs = P.sb([128, NVEC], F32, "vecs")
    P.dma("sp", lambda e: e.dma_start(out=vecs.t[:], in_=vecs_d), "c0", w=[vecs])
    P.dma("sp", lambda e: e.dma_start(out=cstf.t[:], in_=cst_d), "c0", w=cstf.c)

    ident = P.sb([128, 128], BF16, "ident")
    ones = P.sb([128, 128], BF16, "ones")
    maskc = P.sb([128, 512], BF16, "maskc")
    maskp = P.sb([128, 512], BF16, "maskp")
    maskp0 = P.sb([128, 512], BF16, "maskp0")
    cmask = P.sb([128, 512], F32, "cmask")
    rmask = P.sb([128, 512], F32, "rmask")
    esink = P.sb([1, 1024], BF16, "esink")
    hflag = P.sb([128, 1], F32, "hflag")
    lbA = P.sb([128, 4], F32, "lbA")
    lbB = P.sb([128, 4], F32, "lbB")
    lbnB = P.sb([128, 4], F32, "lbnB")
    lbt = P.sb([128, 4], F32, "lbt")

    P.op("dve", lambda e: e.tensor_copy(ident.t[:], cstf.t[:, 0, 0:128]), r=cstf.c, w=[ident])
    P.op("dve", lambda e: e.memset(ones.t[:], 1.0), w=[ones])
    P.op("dve", lambda e: e.tensor_copy(maskc.t[:], cstf.t[:, 1, :]), r=cstf.c, w=[maskc])
    P.op("dve", lambda e: e.tensor_copy(maskp.t[:], cstf.t[:, 2, :]), r=cstf.c, w=[maskp])
    P.op("dve", lambda e: e.tensor_copy(cmask.t[:], cstf.t[:, 3, :]), r=cstf.c, w=[cmask])
    P.op("dve", lambda e: e.tensor_copy(rmask.t[:], cstf.t[:, 4, :]), r=cstf.c, w=[rmask])
    P.op("dve", lambda e: e.tensor_copy(maskp0.t[:], cstf.t[:, 5, :]), r=cstf.c, w=[maskp0])
    P.op("dve", lambda e: e.tensor_copy(hflag.t[:], cstf.t[:, 0, 128:129]), r=cstf.c, w=[hflag])
    P.op("act", lambda e: e.activation(out=esink.t[0:1, :].rearrange("p (g n) -> p g n", g=2),
                                       in_=cstf.t[0:1, 6:8, :], func=AF.Exp), r=cstf.c, w=[esink])
    P.op("dve", lambda e: e.tensor_tensor(lbt.t[:], vecs.t[:, V_LB + 4:V_LB + 8], vecs.t[:, V_LB:V_LB + 4],
                                          ALU.subtract), r=[vecs], w=[lbt])
    P.op("act", lambda e: e.activation(out=lbt.t[:], in_=lbt.t[:], func=AF.Exp), r=[lbt], w=[lbt])
    P.op("act", lambda e: e.activation(out=lbt.t[:], in_=lbt.t[:], func=AF.Ln, bias=1.0, scale=1.0), r=[lbt], w=[lbt])
    P.op("act", lambda e: e.activation(out=lbt.t[:], in_=lbt.t[:], func=AF.Exp, scale=-1.0), r=[lbt], w=[lbt])
    P.op("dve", lambda e: e.tensor_scalar(lbA.t[:], lbt.t[:], 0.5, 0.5, ALU.mult, ALU.add), r=[lbt], w=[lbA])
    P.op("dve", lambda e: e.tensor_scalar(lbB.t[:], lbt.t[:], -0.5, 0.5, ALU.mult, ALU.add), r=[lbt], w=[lbB])
    P.op("dve", lambda e: e.tensor_scalar(lbnB.t[:], lbt.t[:], 0.5, -0.5, ALU.mult, ALU.add), r=[lbt], w=[lbnB])

    try:
        ckpt('none')
    except StopBuild:
        pass
    banks = [P.ps([128, 512], F32, f"bank{i}") for i in range(5)]
    lbank = [P.ps([128, 512], F32, f"lbank{i}") for i in range(2)]
    tbank = [P.ps([128, 1024], BF16, "tbank")]
    bstate = {"i": 0, "t": 0}

    def bank():
        b = banks[bstate["i"] % len(banks)]
        bstate["i"] += 1
        return b

    tbh = [Buf(tbank[0].t, "tbh0"), Buf(tbank[0].t, "tbh1")]

    def tbk():
        bstate["t"] += 1
        return tbh[0], 0

    ring = [P.sb([128, 1024], BF16, f"ring{i}") for i in range(RING)]
    wseq = []
    wstate = {"next": 0, "issued": 0}

    def wnext(name):
        i = wstate["next"]
        assert wseq[i] == SLOTS[name], (i, name)
        while wstate["issued"] < min(len(wseq), i + RING - 2):
            j = wstate["issued"]
            rb = ring[j % RING]
            sid = wseq[j]
            P.dma("pool", lambda e, rb=rb, sid=sid: e.dma_start(out=rb.t[:], in_=wsl[sid]),
                  ("ring", j % RING), w=[rb])
            wstate["issued"] += 1
        wstate["next"] += 1
        return ring[i % RING]

    groups = []
    n = 0
    for gi in range(PRE // NMAX):
        groups.append(("pre", n, GB, None, gi == PRE // NMAX - 1))
        n += NMAX
    groups.append(("halo", n, 1, None, False))
    n += BLK
    for gi in range(TH // NMAX):
        groups.append(("main", n, GB, gi * NMAX, False))
        n += NMAX
    assert n == NTOT

    mem_names = [f"ck{c}" for c in range(8)] + [f"cv{c}" for c in range(8)]
    full_names = ([f"qa{c}" for c in range(4)] + ["ke0", "ko0", "ke1", "ko1"]
                  + [f"fh{h}" for h in range(4)] + [f"qh{h}" for h in range(4)] + [f"gh{h}" for h in range(4)]
                  + ["vd0", "vd1", "ih0", "ih1", "ih2", "ih3"]
                  + [f"wo{c}" for c in range(8)] + [f"cq{c}" for c in range(8)] + [f"co{c}" for c in range(8)])
    for j in range(NPAIR):
        full_names += [f"up{j}", f"up{NPAIR + j}"]
    full_names += [f"dn{c}_{s}" for c in range(8) for s in range(3)]
    pre_names = [f"fh{h}" for h in range(4)] + ["ih0", "ih1", "ih2", "ih3"]
    pre_last_names = ["ke0", "ko0", "ke1", "ko1"] + [f"fh{h}" for h in range(4)] + ["vd0", "vd1", "ih0", "ih1", "ih2", "ih3"]
    wseq.extend(SLOTS[nm] for nm in mem_names)
    for (kind, n0, nblk, ooff, last) in groups:
        if kind == "pre":
            wseq.extend(SLOTS[nm] for nm in (pre_last_names if last else pre_names))
        else:
            wseq.extend(SLOTS[nm] for nm in full_names)

    Xb = [P.mb([128, 8, NMAX], F32, f"X{i}") for i in range(2)]
    hT = P.mb([128, 8, NMAX], BF16, "hT")
    sq = P.mb([128, 8, NMAX], BF16, "sq")
    rstd = P.sb([128, NMAX], F32, "rstd")
    lnt = P.sb([128, NMAX], F32, "lnt")
    tmpn = [lnt, P.sb([128, NMAX], F32, "tmpn1")]
    S = [P.sb([128, 128], F32, f"S{h}") for h in range(4)]
    for h in range(4):
        P.op("pool", lambda e, h=h: e.memset(S[h].t[:], 0.0), w=[S[h]])
    kT = [[P.sb([128, BLK + NMAX], BF16, f"kT{g}_{par}") for par in range(2)] for g in range(2)]
    Vd = [P.sb([128, GB + 1, 128], BF16, f"Vd{g}") for g in range(2)]
    for g in range(2):
        for par in range(2):
            P.op("pool", lambda e, g=g, par=par: e.memset(kT[g][par].t[:], 0.0), w=[kT[g][par]])
        P.op("pool", lambda e, g=g: e.memset(Vd[g].t[:], 0.0), w=[Vd[g]])
    uh = P.sb([128, 44, 2], F32, "uh")
    P.op("pool", lambda e: e.memset(uh.t[:], 0.0), w=[uh])
    KmT = P.sb([128, 8, NMEM], BF16, "KmT")
    Vm = P.sb([128, 2, D], BF16, "Vm")

    qaT = [P.sb([128, NMAX], BF16, f"qaT{c}") for c in range(4)]
    attnT = P.mb([128, 4, NMAX], BF16, "attnT")
    recT = [P.sb([128, NMAX], BF16, f"recT{h}") for h in range(4)]
    th = [P.sb([128, NMAX], F32, f"th{h}") for h in range(4)]
    qs = [P.sb([128, NMAX], BF16, f"qs{h}") for h in range(4)]
    gs = [P.sb([128, NMAX], BF16, f"gs{h}") for h in range(4)]
    vtok = [P.sb([128, GB, 128], BF16, f"vtok{h}") for h in range(4)]
    gg = [P.sb([128, NMAX], F32, f"gg{i}") for i in range(2)]
    bc = [P.sb([128, NMAX], F32, f"bc{i}") for i in range(2)]
    kk = [P.sb([128, NMAX], F32, f"kk{i}") for i in range(2)]
    qt = [P.sb([128, NMAX], BF16, f"qt{h}") for h in range(4)]
    kt = [P.sb([128, NMAX], BF16, f"kt{h}") for h in range(4)]
    ktok = [P.sb([128, GB, 128], BF16, f"ktok{h}") for h in range(4)]
    Am = [P.sb([128, NMAX], BF16, f"Am{h}") for h in range(4)]
    cr = [P.sb([128, GB], F32, f"cr{h}") for h in range(4)]
    dS = [P.sb([128, GB], F32, f"dS{h}") for h in range(4)]
    dK = [P.sb([128, GB], F32, f"dK{h}") for h in range(4)]
    eC = [P.sb([128, GB], F32, f"eC{h}") for h in range(4)]
    Sb = [[P.sb([128, 128], BF16, f"Sb{h}_{i}") for i in range(2)] for h in range(4)]
    Usb = th
    osq = P.sb([128, NMAX], BF16, "osq")
    r1 = gg[1]
    Pp = [P.sb([128, 512], BF16, f"Pp{i}") for i in range(2)]
    Pc = [P.sb([128, 512], BF16, f"Pc{i}") for i in range(2)]
    rden = [P.sb([128, 512], F32, f"rden{i}") for i in range(2)]
    qcT = qt + kt
    _pt = qaT + Am
    PT = [[_pt[2 * h + m] for m in range(2)] for h in range(4)]
    ocT = recT + gs
    rdc = bc
    aT = (qaT + Am + qt + kt + recT + gs)[:NPAIR]
    ug = [P.sb([128, NMAX + 2], F32, f"ug{i}") for i in range(3)]
    uv = [P.sb([128, NMAX + 2], F32, f"uv{i}") for i in range(3)]
    cg = [th[0], th[1], gg[0]]
    cv = [th[2], th[3], gg[1]]

    def mm(out_b, out_ap, lhs_b, lhs_ap, rhs_b, rhs_ap, start, stop):
        P.op("pe", lambda e: e.matmul(out_ap, lhs_ap, rhs_ap, start=start, stop=stop),
             r=[lhs_b, rhs_b], w=[out_b])

    def rstd_from(psb, N, dim):
        P.op("act", lambda e: e.activation(out=lnt.t[:, :N], in_=psb.t[:, :N], func=AF.Ln, bias=EPS,
                                           scale=1.0 / dim), r=[psb], w=[lnt])
        P.op("act", lambda e: e.activation(out=rstd.t[:, :N], in_=lnt.t[:, :N], func=AF.Exp, scale=-0.5),
             r=[lnt], w=[rstd])

    def prenorm(X, N, gcol):
        for c in range(8):
            P.op("act", lambda e, c=c: e.activation(out=sq.t[:, c, :N], in_=X.t[:, c, :N], func=AF.Square),
                 r=[X.c[c]], w=[sq.c[c]])
        pb = bank()
        for c in range(8):
            mm(pb, pb.t[:, :N], ones, ones.t[:], sq.c[c], sq.t[:, c, :N], c == 0, c == 7)
        rstd_from(pb, N, D)
        for c in range(8):
            P.op("dve", lambda e, c=c: e.scalar_tensor_tensor(hT.t[:, c, :N], X.t[:, c, :N],
                                                              vecs.t[:, gcol + c:gcol + c + 1], rstd.t[:, :N],
                                                              ALU.mult, ALU.mult), r=[X.c[c], vecs, rstd], w=[hT.c[c]])

    def proj_fm(name, rhs_list, N, kchunks=8):
        wb = wnext(name)
        pb = bank()
        for k in range(kchunks):
            rb, rap = rhs_list[k]
            mm(pb, pb.t[:, :N], wb, wb.t[:, k * 128:(k + 1) * 128], rb, rap, k == 0, k == kchunks - 1)
        return pb

    def postnorm_residual(X, N, gcol, pbs):
        for c in range(8):
            pb = pbs[c]()
            P.op("act", lambda e, c=c, pb=pb: e.activation(out=stage.t[:, c, :N], in_=pb.t[:, :N], func=AF.Copy),
                 r=[pb], w=[stage.c[c]])
            P.op("act", lambda e, c=c, pb=pb: e.activation(out=sq.t[:, c, :N], in_=pb.t[:, :N], func=AF.Square),
                 r=[pb], w=[sq.c[c]])
        pb2 = bank()
        for c in range(8):
            mm(pb2, pb2.t[:, :N], ones, ones.t[:], sq.c[c], sq.t[:, c, :N], c == 0, c == 7)
        rstd_from(pb2, N, D)
        for c in range(8):
            eng = "pool" if c % 2 == 1 else "dve"
            P.op(eng, lambda e, c=c: e.tensor_tensor(stage.t[:, c, :N], stage.t[:, c, :N], rstd.t[:, :N], ALU.mult),
                 r=[stage.c[c], rstd], w=[stage.c[c]])
        for c in range(8):
            P.op("dve", lambda e, c=c: e.scalar_tensor_tensor(X.t[:, c, :N], stage.t[:, c, :N],
                                                              vecs.t[:, gcol + c:gcol + c + 1], X.t[:, c, :N],
                                                              ALU.mult, ALU.add),
                 r=[stage.c[c], vecs, X.c[c]], w=[X.c[c]])

    hT_chunks = lambda N: [(hT.c[k], hT.t[:, k, :N]) for k in range(8)]

    def do_memkv():
        memX = stage
        P.dma("sp", lambda e: e.dma_start(out=memX.t[:, :, :NMEM], in_=memT.rearrange("c p n -> p c n")),
              "memx", w=memX.c)
        prenorm(memX, NMEM, V_MEMN)
        for c in range(8):
            pb = proj_fm(f"ck{c}", hT_chunks(NMEM), NMEM)
            P.op("act", lambda e, c=c, pb=pb: e.activation(out=KmT.t[:, c, :], in_=pb.t[:, :NMEM], func=AF.Copy),
                 r=[pb], w=[KmT])
        for c in range(8):
            wb = wnext(f"cv{c}")
            pb = bank()
            for mh in range(2):
                for k in range(8):
                    mm(pb, pb.t[:, mh * 128:(mh + 1) * 128], hT.c[k], hT.t[:, k, mh * 128:(mh + 1) * 128],
                       wb, wb.t[:, k * 128:(k + 1) * 128], k == 0, k == 7)
            P.op("act", lambda e, c=c, pb=pb: e.activation(
                out=Vm.t[:, :, c * 128:(c + 1) * 128],
                in_=pb.t[:, 0:256].rearrange("p (m d) -> p m d", m=2), func=AF.Copy), r=[pb], w=[Vm])

    def load_x(gi):
        kind, n0, nblk, ooff, last = groups[gi]
        N = nblk * BLK
        X = Xb[gi % 2]
        P.dma("sp", lambda e: e.dma_start(out=X.t[:, :, :N], in_=xT[:, :, n0:n0 + N].rearrange("c p n -> p c n")),
              ("x", gi % 2), w=X.c)

    class _V:
        def __init__(self, t, k):
            self.t3, self.k = t, k

        def __getitem__(self, idx):
            p, f = idx
            return self.t3[p, self.k, f]

    hsets = [(gg[0], gg[0].t, bc[0], bc[0].t, kk[0], kk[0].t),
             (gg[1], gg[1].t, bc[1], bc[1].t, kk[1], kk[1].t),
             (stage.c[0], _V(stage.t, 0), stage.c[1], _V(stage.t, 1), stage.c[2], _V(stage.t, 2)),
             (stage.c[3], _V(stage.t, 3), stage.c[4], _V(stage.t, 4), stage.c[5], _V(stage.t, 5))]

    def hgrn_prep_all(N, nblk, full):
        HS = range(4)
        gB = [hsets[h][0] for h in HS]; gT = [hsets[h][1] for h in HS]
        bB = [hsets[h][2] for h in HS]; bT = [hsets[h][3] for h in HS]
        kB = [hsets[h][4] for h in HS]; kT_ = [hsets[h][5] for h in HS]
        bc3 = [bT[h][:, :N].rearrange("p (b t) -> p b t", t=BLK) for h in HS]
        for h in HS:
            P.op("act", lambda e, h=h: e.activation(out=gT[h][:, :N], in_=th[h].t[:, :N], func=AF.Ln,
                                                    bias=lbA.t[:, h:h + 1], scale=lbB.t[:, h:h + 1]),
                 r=[th[h], lbA, lbB], w=[gB[h]])
        for h in HS:
            P.op("dve", lambda e, h=h: e.tensor_tensor_scan(bT[h][:, :N], rmask.t[:, :N], gT[h][:, :N], 0.0,
                                                            ALU.mult, ALU.add), r=[rmask, gB[h]], w=[bB[h]])
        for h in HS:
            P.op("act", lambda e, h=h: e.activation(out=kT_[h][:, :N], in_=th[h].t[:, :N], func=AF.Identity,
                                                    bias=lbB.t[:, h:h + 1], scale=lbnB.t[:, h:h + 1]),
                 r=[th[h], lbnB, lbB], w=[kB[h]])
        for h in HS:
            P.op("pool", lambda e, h=h: e.tensor_copy(cr[h].t[:, :nblk], bc3[h][:, :, 63]), r=[bB[h]], w=[cr[h]])
            P.op("act", lambda e, h=h: e.activation(out=dS[h].t[:, :nblk], in_=bc3[h][:, :, 127], func=AF.Exp),
                 r=[bB[h]], w=[dS[h]])
        for h in HS:
            P.op("dve", lambda e, h=h: e.tensor_tensor(bc3[h], bc3[h],
                                                       cr[h].t[:, :nblk].rearrange("p (b o) -> p b o", o=1)
                                                       .to_broadcast([128, nblk, BLK]), ALU.subtract),
                 r=[bB[h], cr[h]], w=[bB[h]])
        for h in HS:
            P.op("act", lambda e, h=h: e.activation(out=gT[h][:, :N], in_=bT[h][:, :N], func=AF.Exp, scale=-1.0),
                 r=[bB[h]], w=[gB[h]])
        for h in HS:
            P.op("dve", lambda e, h=h: e.tensor_tensor(kt[h].t[:, :N], kT_[h][:, :N], gT[h][:, :N], ALU.mult),
                 r=[kB[h], gB[h]], w=[kt[h]])
        for h in HS:
            P.op("act", lambda e, h=h: e.activation(out=dK[h].t[:, :nblk], in_=bc3[h][:, :, 127], func=AF.Exp),
                 r=[bB[h]], w=[dK[h]])
            if full:
                P.op("act", lambda e, h=h: e.activation(out=eC[h].t[:, :nblk], in_=cr[h].t[:, :nblk], func=AF.Exp),
                     r=[cr[h]], w=[eC[h]])
        if full:
            for h in HS:
                P.op("act", lambda e, h=h: e.activation(out=kT_[h][:, :N], in_=bT[h][:, :N], func=AF.Exp),
                     r=[bB[h]], w=[kB[h]])
            for h in HS:
                P.op("dve", lambda e, h=h: e.scalar_tensor_tensor(qt[h].t[:, :N], qs[h].t[:, :N],
                                                                  float(128 ** -0.5), kT_[h][:, :N],
                                                                  ALU.mult, ALU.mult),
                     r=[qs[h], kB[h]], w=[qt[h]])

    def hgrn_core(N, nblk, full):
        for h in range(4):
            tb, to = tbk()
            for b in range(nblk):
                P.op("pe", lambda e, tb=tb, to=to, h=h, b=b: e.transpose(
                    tb.t[:, to + b * 128:to + (b + 1) * 128], kt[h].t[:, b * 128:(b + 1) * 128], ident.t[:]),
                     r=[kt[h], ident], w=[tb])
            P.op("act", lambda e, tb=tb, to=to, h=h: e.activation(
                out=ktok[h].t[:, :nblk, :], in_=tb.t[:, to:to + N].rearrange("p (b f) -> p b f", f=128),
                func=AF.Copy), r=[tb], w=[ktok[h]])
        for h in range(4):
            ub = bank()
            for b in range(nblk):
                mm(ub, ub.t[:, b * 128:(b + 1) * 128], ktok[h], ktok[h].t[:, b, :], vtok[h], vtok[h].t[:, b, :],
                   True, True)
            if full:
                P.op("act", lambda e, ub=ub, h=h: e.activation(out=Usb[h].t[:, :N], in_=ub.t[:, :N], func=AF.Copy),
                     r=[ub], w=[Usb[h]])
            else:
                P.op("dve", lambda e, ub=ub, h=h: e.tensor_copy(Usb[h].t[:, :N], ub.t[:, :N]), r=[ub], w=[Usb[h]])
        if full:
            for h in range(4):
                ab = bank()
                for b in range(nblk):
                    mm(ab, ab.t[:, b * 128:(b + 1) * 128], kt[h], kt[h].t[:, b * 128:(b + 1) * 128],
                       qt[h], qt[h].t[:, b * 128:(b + 1) * 128], True, True)
                P.op("dve", lambda e, ab=ab, h=h: e.tensor_tensor(Am[h].t[:, :N], ab.t[:, :N], cmask.t[:, :N], ALU.mult),
                     r=[ab, cmask], w=[Am[h]])

        def s_update(h, b):
            if full:
                P.op("act", lambda e: e.activation(out=S[h].t[:], in_=S[h].t[:], func=AF.Identity,
                                                   scale=dS[h].t[:, b:b + 1]), r=[S[h], dS[h]], w=[S[h]])
            else:
                P.op("dve", lambda e: e.tensor_scalar(S[h].t[:], S[h].t[:], dS[h].t[:, b:b + 1], None, ALU.mult),
                     r=[S[h], dS[h]], w=[S[h]])
            P.op("dve", lambda e: e.scalar_tensor_tensor(S[h].t[:], Usb[h].t[:, b * 128:(b + 1) * 128],
                                                         dK[h].t[:, b:b + 1], S[h].t[:], ALU.mult, ALU.add),
                 r=[Usb[h], dK[h], S[h]], w=[S[h]])

        if not full:
            for b in range(nblk):
                for h in range(4):
                    s_update(h, b)
            return
        for hp in range(2):
            hs = (2 * hp, 2 * hp + 1)
            ob = {h: lbank[i] for i, h in enumerate(hs)}
            for b in range(nblk):
                for h in hs:
                    sb_ = Sb[h][b % 2]
                    P.op("dve", lambda e, h=h, b=b, sb_=sb_: e.tensor_scalar(sb_.t[:], S[h].t[:], eC[h].t[:, b:b + 1],
                                                                             None, ALU.mult),
                         r=[S[h], eC[h]], w=[sb_])
                    o_b = ob[h]
                    mm(o_b, o_b.t[:, b * 128:(b + 1) * 128], vtok[h], vtok[h].t[:, b, :],
                       Am[h], Am[h].t[:, b * 128:(b + 1) * 128], True, False)
                    mm(o_b, o_b.t[:, b * 128:(b + 1) * 128], sb_, sb_.t[:], qt[h], qt[h].t[:, b * 128:(b + 1) * 128],
                       False, True)
                    s_update(h, b)
            for h in hs:
                o_b = ob[h]
                P.op("act", lambda e, o_b=o_b, h=h: e.activation(out=th[h].t[:, :N], in_=o_b.t[:, :N], func=AF.Copy),
                     r=[o_b], w=[th[h]])
        HS = range(4)
        lnB = [hsets[h][0] for h in HS]
        lnT = [hsets[h][1] for h in HS]
        for h in HS:
            P.op("act", lambda e, h=h: e.activation(out=Am[h].t[:, :N], in_=th[h].t[:, :N], func=AF.Square),
                 r=[th[h]], w=[Am[h]])
        pbs_ = []
        for h in HS:
            pb = bank()
            mm(pb, pb.t[:, :N], ones, ones.t[:], Am[h], Am[h].t[:, :N], True, True)
            pbs_.append(pb)
        for h in HS:
            P.op("act", lambda e, h=h, pb=pbs_[h]: e.activation(out=lnT[h][:, :N], in_=pb.t[:, :N], func=AF.Ln,
                                                                 bias=EPS, scale=1.0 / 128), r=[pbs_[h]], w=[lnB[h]])
        for h in HS:
            P.op("act", lambda e, h=h: e.activation(out=lnT[h][:, :N], in_=lnT[h][:, :N], func=AF.Exp, scale=-0.5),
                 r=[lnB[h]], w=[lnB[h]])
        for h in HS:
            P.op("dve", lambda e, h=h: e.tensor_tensor(th[h].t[:, :N], th[h].t[:, :N], lnT[h][:, :N], ALU.mult),
                 r=[th[h], lnB[h]], w=[th[h]])
        for h in HS:
            P.op("dve", lambda e, h=h: e.scalar_tensor_tensor(recT[h].t[:, :N], th[h].t[:, :N],
                                                              vecs.t[:, V_ONW:V_ONW + 1], gs[h].t[:, :N],
                                                              ALU.mult, ALU.mult),
                 r=[th[h], vecs, gs[h]], w=[recT[h]])

    def tokmajor(names, dsts, N, nblk, evac="act"):
        wbs = [wnext(nm) for nm in names]
        pbs = [bank() for _ in names]
        for b in range(nblk):
            for k in range(8):
                for wb, pb in zip(wbs, pbs):
                    mm(pb, pb.t[:, b * 128:(b + 1) * 128], hT.c[k], hT.t[:, k, b * 128:(b + 1) * 128],
                       wb, wb.t[:, k * 128:(k + 1) * 128], k == 0, k == 7)
        for pb, (db, dap) in zip(pbs, dsts):
            if evac == "act":
                P.op("act", lambda e, pb=pb, dap=dap: e.activation(
                    out=dap, in_=pb.t[:, :N].rearrange("p (b f) -> p b f", f=128), func=AF.Copy), r=[pb], w=[db])
            else:
                P.op("dve", lambda e, pb=pb, dap=dap: e.tensor_copy(
                    dap, pb.t[:, :N].rearrange("p (b f) -> p b f", f=128)), r=[pb], w=[db])

    def roll_kv(nblk):
        for g in range(2):
            for par in range(2):
                P.op("pool", lambda e, g=g, par=par: e.tensor_copy(kT[g][par].t[:, 0:BLK],
                                                                    kT[g][par].t[:, nblk * BLK:(nblk + 1) * BLK]),
                     r=[kT[g][par]], w=[kT[g][par]])
            P.op("pool", lambda e, g=g: e.tensor_copy(Vd[g].t[:, 0, :], Vd[g].t[:, nblk, :]),
                 r=[Vd[g]], w=[Vd[g]])

    def do_prefix(gi):
        kind, n0, nblk, ooff, last = groups[gi]
        N = nblk * BLK
        X = Xb[gi % 2]
        prenorm(X, N, V_MIXPRE)
        if last:
            for g in range(2):
                for par, nm in enumerate(("ke", "ko")):
                    pb = proj_fm(f"{nm}{g}", hT_chunks(N), N)
                    P.op("act", lambda e, g=g, par=par, pb=pb: e.activation(
                        out=kT[g][par].t[:, BLK:BLK + N], in_=pb.t[:, :N], func=AF.Copy), r=[pb], w=[kT[g][par]])
        for h in range(4):
            pb = proj_fm(f"fh{h}", hT_chunks(N), N)
            P.op("act", lambda e, h=h, pb=pb: e.activation(out=th[h].t[:, :N], in_=pb.t[:, :N], func=AF.Tanh,
                                                           scale=0.5), r=[pb], w=[th[h]])
        if last:
            tokmajor(["vd0", "vd1", "ih0"], [(Vd[0], Vd[0].t[:, 1:1 + nblk, :]), (Vd[1], Vd[1].t[:, 1:1 + nblk, :]),
                                             (vtok[0], vtok[0].t[:, :nblk, :])], N, nblk, evac="dve")
            tokmajor(["ih1", "ih2", "ih3"], [(vtok[h], vtok[h].t[:, :nblk, :]) for h in (1, 2, 3)], N, nblk, evac="dve")
        else:
            tokmajor(["ih0", "ih1"], [(vtok[h], vtok[h].t[:, :nblk, :]) for h in (0, 1)], N, nblk, evac="dve")
            tokmajor(["ih2", "ih3"], [(vtok[h], vtok[h].t[:, :nblk, :]) for h in (2, 3)], N, nblk, evac="dve")
        hgrn_prep_all(N, nblk, False)
        hgrn_core(N, nblk, False)
        if last:
            roll_kv(nblk)

    def do_full(gi):
        kind, n0, nblk, ooff, last = groups[gi]
        N = nblk * BLK
        X = Xb[gi % 2]
        first_main = (kind == "main" and ooff == 0)
        prenorm(X, N, V_MIXPRE)
        for c in range(4):
            pb = proj_fm(f"qa{c}", hT_chunks(N), N)
            P.op("dve", lambda e, c=c, pb=pb: e.tensor_copy(qaT[c].t[:, :N], pb.t[:, :N]), r=[pb], w=[qaT[c]])
        for g in range(2):
            for par, nm in enumerate(("ke", "ko")):
                pb = proj_fm(f"{nm}{g}", hT_chunks(N), N)
                P.op("dve", lambda e, g=g, par=par, pb=pb: e.tensor_copy(kT[g][par].t[:, BLK:BLK + N], pb.t[:, :N]),
                     r=[pb], w=[kT[g][par]])
        for h in range(4):
            pb = proj_fm(f"fh{h}", hT_chunks(N), N)
            P.op("act", lambda e, h=h, pb=pb: e.activation(out=th[h].t[:, :N], in_=pb.t[:, :N], func=AF.Tanh,
                                                           scale=0.5), r=[pb], w=[th[h]])
        for h in range(4):
            pb = proj_fm(f"qh{h}", hT_chunks(N), N)
            P.op("act", lambda e, h=h, pb=pb: e.activation(out=qs[h].t[:, :N], in_=pb.t[:, :N], func=AF.Silu),
                 r=[pb], w=[qs[h]])
        for h in range(4):
            pb = proj_fm(f"gh{h}", hT_chunks(N), N)
            P.op("act", lambda e, h=h, pb=pb: e.activation(out=gs[h].t[:, :N], in_=pb.t[:, :N], func=AF.Silu),
                 r=[pb], w=[gs[h]])
        tokmajor(["vd0", "vd1", "ih0"], [(Vd[0], Vd[0].t[:, 1:1 + nblk, :]), (Vd[1], Vd[1].t[:, 1:1 + nblk, :]),
                                         (vtok[0], vtok[0].t[:, :nblk, :])], N, nblk)
        tokmajor(["ih1", "ih2", "ih3"], [(vtok[h], vtok[h].t[:, :nblk, :]) for h in (1, 2, 3)], N, nblk)
        hgrn_prep_all(N, nblk, True)
        iters = [(b, g) for b in range(nblk) for g in range(2)]
        Sset = [(banks[0], banks[1]), (banks[2], banks[3])]
        ob_, db_ = banks[4], lbank[0]

        def att_scores(i):
            b, g = iters[i]
            sp_, sc_ = Sset[i % 2]
            mprev = maskp0 if (first_main and b == 0) else maskp
            mm(sp_, sp_.t[:, :], ident, ident.t[:], mprev, mprev.t[:], True, False)
            mm(sc_, sc_.t[:, :], ident, ident.t[:], maskc, maskc.t[:], True, False)
            for j in range(4):
                hq = 4 * g + j
                c = hq // 2
                kb = kT[g][j % 2]
                mm(sp_, sp_.t[:, j * 128:(j + 1) * 128], kb, kb.t[:, b * 128:(b + 1) * 128],
                   qaT[c], qaT[c].t[:, b * 128:(b + 1) * 128], False, j == 3)
                mm(sc_, sc_.t[:, j * 128:(j + 1) * 128], kb, kb.t[:, (b + 1) * 128:(b + 2) * 128],
                   qaT[c], qaT[c].t[:, b * 128:(b + 1) * 128], False, j == 3)

        def att_rest(i):
            b, g = iters[i]
            i2 = i % 2
            sp_, sc_ = Sset[i2]
            pp, pc, rd = Pp[i2], Pc[i2], rden[i2]
            P.op("act", lambda e: e.activation(out=pp.t[:], in_=sp_.t[:], func=AF.Exp, scale=0.125), r=[sp_], w=[pp])
            P.op("act", lambda e: e.activation(out=pc.t[:], in_=sc_.t[:], func=AF.Exp, scale=0.125), r=[sc_], w=[pc])
            if i + 1 < len(iters):
                att_scores(i + 1)
            mm(ob_, ob_.t[:], Vd[g], Vd[g].t[:, b, :], pp, pp.t[:], True, False)
            mm(ob_, ob_.t[:], Vd[g], Vd[g].t[:, b + 1, :], pc, pc.t[:], False, True)
            mm(db_, db_.t[:], ones, ones.t[:], pp, pp.t[:], True, False)
            mm(db_, db_.t[:], ones, ones.t[:], pc, pc.t[:], False, False)
            mm(db_, db_.t[:], ones, ones.t[0:1, :], esink, esink.t[0:1, g * 512:(g + 1) * 512], False, True)
            P.op("act", lambda e: e.activation(out=rd.t[:], in_=db_.t[:], func=AF.Ln), r=[db_], w=[rd])
            P.op("act", lambda e: e.activation(out=rd.t[:], in_=rd.t[:], func=AF.Exp, scale=-1.0), r=[rd], w=[rd])
            for half in range(2):
                p0 = half * 64
                P.op("dve", lambda e, half=half, p0=p0: e.tensor_tensor(
                    attnT.t[p0:p0 + 64, 2 * g:2 * g + 2, b * 128:(b + 1) * 128],
                    ob_.t[p0:p0 + 64, :].rearrange("p (i two q) -> p i two q", two=2, q=128)[:, :, half, :],
                    rd.t[p0:p0 + 64, :].rearrange("p (i two q) -> p i two q", two=2, q=128)[:, :, half, :],
                    ALU.mult), r=[ob_, rd], w=[attnT.c[2 * g], attnT.c[2 * g + 1]])

        att_scores(0)
        for i in range(len(iters)):
            att_rest(i)
        roll_kv(nblk)
        hgrn_core(N, nblk, True)
        mixk = [(attnT.c[c], attnT.t[:, c, :N]) for c in range(4)] + [(recT[h], recT[h].t[:, :N]) for h in range(4)]
        postnorm_residual(X, N, V_MIXPOST, [lambda c=c: proj_fm(f"wo{c}", mixk, N) for c in range(8)])

        if 'ca' not in STAGES:
            for c in range(8):
                wnext(f"cq{c}")
            for c in range(8):
                wnext(f"co{c}")
        else:
            do_ca(X, N)
        do_ffn(gi, X, N, kind, ooff, first_main)

    def do_ca(X, N):
        prenorm(X, N, V_CAPRE)
        for c in range(8):
            pb = proj_fm(f"cq{c}", hT_chunks(N), N)
            P.op("dve", lambda e, c=c, pb=pb: e.tensor_copy(qcT[c].t[:, :N], pb.t[:, :N]), r=[pb], w=[qcT[c]])
        for hh in range(4):
            for mh in range(2):
                pb = bank()
                for i, dc in enumerate((2 * hh, 2 * hh + 1)):
                    mm(pb, pb.t[:, :N], KmT, KmT.t[:, dc, mh * 128:(mh + 1) * 128], qcT[dc], qcT[dc].t[:, :N],
                       i == 0, i == 1)
                P.op("act", lambda e, hh=hh, mh=mh, pb=pb: e.activation(out=PT[hh][mh].t[:, :N], in_=pb.t[:, :N],
                                                                        func=AF.Exp, scale=1.0 / 16.0),
                     r=[pb], w=[PT[hh][mh]])
            db_ = bank()
            for mh in range(2):
                mm(db_, db_.t[:, :N], ones, ones.t[:], PT[hh][mh], PT[hh][mh].t[:, :N], mh == 0, mh == 1)
            rd = rdc[hh % 2]
            P.op("act", lambda e, db_=db_, rd=rd: e.activation(out=rd.t[:, :N], in_=db_.t[:, :N], func=AF.Ln),
                 r=[db_], w=[rd])
            P.op("act", lambda e, rd=rd: e.activation(out=rd.t[:, :N], in_=rd.t[:, :N], func=AF.Exp, scale=-1.0),
                 r=[rd], w=[rd])
            for dc in (2 * hh, 2 * hh + 1):
                pb = bank()
                for mh in range(2):
                    mm(pb, pb.t[:, :N], Vm, Vm.t[:, mh, dc * 128:(dc + 1) * 128], PT[hh][mh], PT[hh][mh].t[:, :N],
                       mh == 0, mh == 1)
                P.op("dve", lambda e, dc=dc, pb=pb, rd=rd: e.tensor_tensor(ocT[dc].t[:, :N], pb.t[:, :N], rd.t[:, :N],
                                                                           ALU.mult), r=[pb, rd], w=[ocT[dc]])
        ock = [(ocT[c], ocT[c].t[:, :N]) for c in range(8)]
        postnorm_residual(X, N, V_CAPOST, [lambda c=c: proj_fm(f"co{c}", ock, N) for c in range(8)])

    def do_ffn(gi, X, N, kind, ooff, first_main):
        if 'ffn' not in STAGES:
            for j in range(NPAIR):
                wnext(f"up{j}")
                wnext(f"up{NPAIR + j}")
            for c in range(8):
                for s_ in range(3):
                    wnext(f"dn{c}_{s_}")
            if kind == "main":
                ev = P.dma("sp", lambda e: e.dma_start(out=outT[:, :, ooff:ooff + N].rearrange("c p n -> p c n"),
                                                       in_=X.t[:, :, :N]), ("xo", gi % 2), r=X.c)
                P.finals.append(ev)
            return
        prenorm(X, N, V_FFNPRE)
        for j in range(NPAIR):
            i2 = j % 3
            outs = []
            for (ci, ub, cb) in ((j, ug[i2], cg[i2]), (NPAIR + j, uv[i2], cv[i2])):
                pb = proj_fm(f"up{ci}", hT_chunks(N), N)
                P.op("act", lambda e, ci=ci, pb=pb, cb=cb: e.activation(
                    out=cb.t[:, :N], in_=pb.t[:, :N], func=AF.Identity,
                    bias=vecs.t[:, V_CB + ci:V_CB + ci + 1], scale=vecs.t[:, V_CW + 88 + ci:V_CW + 88 + ci + 1]),
                    r=[pb, vecs], w=[cb])
                P.op("act", lambda e, pb=pb, ub=ub: e.activation(out=ub.t[:, 2:2 + N], in_=pb.t[:, :N], func=AF.Copy),
                     r=[pb], w=[ub])
                if first_main:
                    P.op("pool", lambda e, ci=ci, ub=ub: e.tensor_scalar(ub.t[:, 0:2], uh.t[:, ci, :], hflag.t[:, 0:1],
                                                                         None, ALU.mult), r=[uh, hflag], w=[ub])
                else:
                    P.op("pool", lambda e, ci=ci, ub=ub: e.tensor_copy(ub.t[:, 0:2], uh.t[:, ci, :]), r=[uh], w=[ub])
                P.op("pool", lambda e, ci=ci, ub=ub: e.tensor_copy(uh.t[:, ci, :], ub.t[:, N:N + 2]), r=[ub], w=[uh])
                P.op("dve", lambda e, ci=ci, ub=ub, cb=cb: e.scalar_tensor_tensor(
                    cb.t[:, :N], ub.t[:, 0:N], vecs.t[:, V_CW + ci:V_CW + ci + 1], cb.t[:, :N], ALU.mult, ALU.add),
                    r=[ub, vecs, cb], w=[cb])
                P.op("dve", lambda e, ci=ci, ub=ub, cb=cb: e.scalar_tensor_tensor(
                    cb.t[:, :N], ub.t[:, 1:N + 1], vecs.t[:, V_CW + 44 + ci:V_CW + 44 + ci + 1], cb.t[:, :N],
                    ALU.mult, ALU.add), r=[ub, vecs, cb], w=[cb])
            if kind == "main":
                P.op("act", lambda e, i2=i2: e.activation(out=cg[i2].t[:, :N], in_=cg[i2].t[:, :N],
                                                          func=AF.Gelu_apprx_tanh), r=[cg[i2]], w=[cg[i2]])
                P.op("dve", lambda e, i2=i2, j=j: e.tensor_tensor(aT[j].t[:, :N], cg[i2].t[:, :N], cv[i2].t[:, :N],
                                                                  ALU.mult), r=[cg[i2], cv[i2]], w=[aT[j]])
        if kind == "main":
            def dn_chunk(c):
                pb = bank()
                for s in range(3):
                    wb = wnext(f"dn{c}_{s}")
                    for kq in range(8):
                        jj = s * 8 + kq
                        if jj >= NPAIR:
                            break
                        mm(pb, pb.t[:, :N], wb, wb.t[:, kq * 128:(kq + 1) * 128], aT[jj], aT[jj].t[:, :N],
                           jj == 0, jj == NPAIR - 1)
                return pb
            postnorm_residual(X, N, V_FFNPOST, [lambda c=c: dn_chunk(c) for c in range(8)])
            ev = P.dma("sp", lambda e: e.dma_start(out=outT[:, :, ooff:ooff + N].rearrange("c p n -> p c n"),
                                                   in_=X.t[:, :, :N]), ("xo", gi % 2), r=X.c)
            P.finals.append(ev)
        else:
            for c in range(8):
                for s in range(3):
                    wnext(f"dn{c}_{s}")

    try:
        ckpt('consts')
        do_memkv()
        ckpt('memkv')
        load_x(0)
        for gi in range(len(groups)):
            if gi + 1 < len(groups):
                load_x(gi + 1)
            if groups[gi][0] == "pre":
                do_prefix(gi)
                ckpt('prefix')
            else:
                do_full(gi)
        assert wstate["next"] == len(wseq), (wstate, len(wseq))
    except StopBuild:
        pass
    P.emit()
    es.close()
    global _LAST
    _LAST = dict(P=P, attnT=attnT, recT=recT, X=Xb, hT=hT, qaT=qaT, kT=kT, Vd=Vd, th=th, qs=qs, gs=gs, qt=qt, kt=kt, S=S, vtok=vtok, stage=stage)
    return nc, P


def run_cores(inputs, PRE=None, trace=False):
    x = np.asarray(inputs["x"], np.float32)
    mem = np.asarray(inputs["mem"], np.float32)
    B, T, _ = x.shape
    TH = T // 2
    if PRE is None:
        PRE = TH
    ws = build_wslots(inputs)
    vecs = build_vecs(inputs)
    sinks = np.asarray(inputs["attn_sinks"][0], np.float32)
    csts = [build_cst(0, sinks), build_cst(1, sinks)]
    nc, P = build_program(TH, PRE)
    in_maps = []
    for c in range(2 * B):
        b, half = c // 2, c % 2
        start = half * TH
        lo = start - PRE - BLK
        NTOT = PRE + BLK + TH
        xs = np.zeros((NTOT, D), np.float32)
        src_lo = max(lo, 0)
        xs[src_lo - lo:] = x[b, src_lo:start + TH]
        xT = np.ascontiguousarray(xs.T).reshape(8, 128, NTOT)
        memT = np.ascontiguousarray(mem[b].T).reshape(8, 128, NMEM)
        in_maps.append({"xT": xT, "memT": memT, "wslots": ws, "vecs": vecs, "cst": csts[half]})
    res = run_bass_kernel_spmd(nc, in_maps, core_ids=list(range(2 * B)), trace=trace)
    out = np.zeros((B, T, D), np.float32)
    for c in range(2 * B):
        b, half = c // 2, c % 2
        o = np.asarray(res.results[c]["outT"]).reshape(D, TH)
        out[b, half * TH:(half + 1) * TH] = o.T
    return out, res


def kernel(**inputs):
    out, _ = run_cores(inputs)
    return out
```
